# Optimizing a Trainium2 kernel written in Bass

```python
import functools
import jax, jax.numpy as jnp
from jax import lax
import numpy as np


D_MODEL = 1024
BATCH = 2
SEQ = 8192
DEPTH = 1
DEC_BATCH = 128
DEC_SEQ = 8
PAST_LEN = 2048
PAGE_SIZE = 128

N_META = 16
MIX_WIDTH = D_MODEL
ATTN_WIDTH = MIX_WIDTH // 2
LRU_WIDTH = MIX_WIDTH - ATTN_WIDTH
HEAD_DIM = 64
N_HEADS = ATTN_WIDTH // HEAD_DIM
LRU_BLOCKS = 8
LRU_BLOCK_W = LRU_WIDTH // LRU_BLOCKS
LRU_C = 8.0
CONV_WIDTH = 4
D_FF = 4 * D_MODEL
Q_BLOCK = 128
RMS_EPS = 1e-6
SB_BIAS_INIT = -6.0

kernel_name = "hymba_stickbreak_rglru_decode_step"


def rmsnorm(x, g):
    xf = x.astype(jnp.float32)
    y = xf * lax.rsqrt(jnp.mean(xf * xf, axis=-1, keepdims=True) + RMS_EPS) * g.astype(jnp.float32)
    return y.astype(x.dtype)


def sb_attend(q, q_pos, k, v, k_pos, bias):
    z = jnp.einsum('bqhd,bkhd->bhqk', q.astype(jnp.float32), k.astype(jnp.float32)) * (HEAD_DIM ** -0.5)
    z = z + bias.astype(jnp.float32)[None, :, None, None]
    mask = k_pos[None, :] < q_pos[:, None]
    log_keep = jnp.where(mask, jax.nn.log_sigmoid(-z), 0.0)
    suffix = lax.cumsum(log_keep, axis=3, reverse=True) - log_keep
    w = jnp.where(mask, jnp.exp(jax.nn.log_sigmoid(z) + suffix), 0.0)
    return jnp.einsum('bhqk,bkhd->bqhd', w, v.astype(jnp.float32)).astype(v.dtype)


def sb_prompt(q, k, v, bias):
    B, T = q.shape[0], q.shape[1]
    pos = jnp.arange(T)
    out_meta = sb_attend(q[:, :N_META], pos[:N_META], k, v, pos, bias)
    n_blk = (T - N_META) // Q_BLOCK
    qr = q[:, N_META:].reshape(B, n_blk, Q_BLOCK, N_HEADS, HEAD_DIM).swapaxes(0, 1)
    pr = pos[N_META:].reshape(n_blk, Q_BLOCK)
    out = lax.map(lambda a: sb_attend(a[0], a[1], k, v, pos, bias), (qr, pr))
    out = out.swapaxes(0, 1).reshape(B, T - N_META, N_HEADS, HEAD_DIM)
    return jnp.concatenate([out_meta, out], axis=1)


def sb_sample(q, k, v, bias, cache_k, cache_v, page_table):
    Bd, Tn = q.shape[0], q.shape[1]
    past = page_table.shape[1] * cache_k.shape[1]
    k_past = cache_k[page_table].reshape(Bd, past, N_HEADS, HEAD_DIM).astype(k.dtype)
    v_past = cache_v[page_table].reshape(Bd, past, N_HEADS, HEAD_DIM).astype(v.dtype)
    k_all = jnp.concatenate([k_past, k], axis=1)
    v_all = jnp.concatenate([v_past, v], axis=1)
    pos = jnp.arange(past + Tn)
    return sb_attend(q, pos[past:], k_all, v_all, pos, bias)


def rglru_branch(xl, conv_prev, h_prev, conv_w, conv_b, w_gate_a, b_gate_a, w_gate_x, b_gate_x, lru_lambda):
    B, T, W = xl.shape
    xpad = jnp.concatenate([conv_prev.astype(xl.dtype), xl], axis=1)
    xc = conv_b
    for j in range(CONV_WIDTH):
        xc = xc + xpad[:, j:j + T] * conv_w[j]
    conv_last = xpad[:, -(CONV_WIDTH - 1):]
    xb = xc.reshape(B, T, LRU_BLOCKS, LRU_BLOCK_W)
    r = jax.nn.sigmoid(jnp.einsum('btnc,ncd->btnd', xb, w_gate_a).reshape(B, T, W) + b_gate_a)
    i = jax.nn.sigmoid(jnp.einsum('btnc,ncd->btnd', xb, w_gate_x).reshape(B, T, W) + b_gate_x)
    log_a = -LRU_C * jax.nn.softplus(-lru_lambda.astype(jnp.float32)) * r.astype(jnp.float32)
    a = jnp.exp(log_a)
    b = jnp.sqrt(-jnp.expm1(2.0 * log_a)) * (i * xc).astype(jnp.float32)

    def step(h, ab):
        h = ab[0] * h + ab[1]
        return h, h

    h_last, hs = lax.scan(step, h_prev.astype(jnp.float32), (a.swapaxes(0, 1), b.swapaxes(0, 1)))
    return hs.swapaxes(0, 1).astype(xl.dtype), h_last, conv_last


def layer(x, attend, conv_prev, h_prev, g_mix_pre, g_mix_post, g_mlp_pre, g_mlp_post, w_in,
          conv_w, conv_b, w_gate_a, b_gate_a, w_gate_x, b_gate_x, lru_lambda, w_out, w_up, w_down):
    B, T, _ = x.shape
    hn = rmsnorm(x, g_mix_pre)
    proj = hn @ w_in
    q, k, v, xl, gate = jnp.split(
        proj, [ATTN_WIDTH, 2 * ATTN_WIDTH, 3 * ATTN_WIDTH, 3 * ATTN_WIDTH + LRU_WIDTH], axis=-1)
    q = q.reshape(B, T, N_HEADS, HEAD_DIM)
    k = k.reshape(B, T, N_HEADS, HEAD_DIM)
    v = v.reshape(B, T, N_HEADS, HEAD_DIM)
    attn = attend(q, k, v)
    lru, h_last, conv_last = rglru_branch(xl, conv_prev, h_prev, conv_w, conv_b,
                                          w_gate_a, b_gate_a, w_gate_x, b_gate_x, lru_lambda)
    mixed = jnp.concatenate([attn.reshape(B, T, ATTN_WIDTH), lru * jax.nn.gelu(gate)], axis=-1) @ w_out
    x = x + rmsnorm(mixed, g_mix_post)
    u = jax.nn.relu(rmsnorm(x, g_mlp_pre) @ w_up)
    x = x + rmsnorm((u * u) @ w_down, g_mlp_post)
    return x, k, v, h_last, conv_last


def setup_inputs(seed: int = 0) -> dict:
    key = jax.random.key(seed)
    ks = jax.random.split(key, 24)
    n_pages = PAST_LEN // PAGE_SIZE
    n_pool = (DEC_BATCH * n_pages * 5) // 4
    f32 = jnp.float32
    nrm = lambda k, s, sc: jax.random.normal(k, s, f32) * sc
    page_table = jax.random.permutation(ks[0], n_pool)[:DEC_BATCH * n_pages].reshape(DEC_BATCH, n_pages).astype(jnp.int32)
    u = jax.random.uniform(ks[1], (DEPTH, LRU_WIDTH), f32, 0.9, 0.999)
    s = u ** (1.0 / LRU_C)
    lru_lambda = jnp.log(s) - jnp.log1p(-s)
    return {
        'x_prompt': nrm(ks[2], (BATCH, SEQ, D_MODEL), 1.0),
        'x_sample': nrm(ks[3], (DEC_BATCH, DEC_SEQ, D_MODEL), 1.0),
        'cache_k': nrm(ks[4], (DEPTH, n_pool, PAGE_SIZE, N_HEADS, HEAD_DIM), 1.0),
        'cache_v': nrm(ks[5], (DEPTH, n_pool, PAGE_SIZE, N_HEADS, HEAD_DIM), 1.0),
        'state_h': nrm(ks[6], (DEPTH, DEC_BATCH, LRU_WIDTH), 0.5),
        'state_conv': nrm(ks[7], (DEPTH, DEC_BATCH, CONV_WIDTH - 1, LRU_WIDTH), 1.0),
        'page_table': page_table,
        'meta_tokens': nrm(ks[8], (N_META, D_MODEL), 1.0),
        'g_mix_pre': 1.0 + nrm(ks[9], (DEPTH, D_MODEL), 0.05),
        'g_mix_post': 1.0 + nrm(ks[10], (DEPTH, D_MODEL), 0.05),
        'g_mlp_pre': 1.0 + nrm(ks[11], (DEPTH, D_MODEL), 0.05),
        'g_mlp_post': 1.0 + nrm(ks[12], (DEPTH, D_MODEL), 0.05),
        'w_in': nrm(ks[13], (DEPTH, D_MODEL, 3 * ATTN_WIDTH + 2 * LRU_WIDTH), D_MODEL ** -0.5),
        'sb_bias': SB_BIAS_INIT + nrm(ks[23], (DEPTH, N_HEADS), 0.1),
        'conv_w': nrm(ks[14], (DEPTH, CONV_WIDTH, LRU_WIDTH), CONV_WIDTH ** -0.5),
        'conv_b': nrm(ks[15], (DEPTH, LRU_WIDTH), 0.01),
        'w_gate_a': nrm(ks[16], (DEPTH, LRU_BLOCKS, LRU_BLOCK_W, LRU_BLOCK_W), LRU_BLOCK_W ** -0.5),
        'b_gate_a': nrm(ks[17], (DEPTH, LRU_WIDTH), 0.1),
        'w_gate_x': nrm(ks[18], (DEPTH, LRU_BLOCKS, LRU_BLOCK_W, LRU_BLOCK_W), LRU_BLOCK_W ** -0.5),
        'b_gate_x': nrm(ks[19], (DEPTH, LRU_WIDTH), 0.1),
        'lru_lambda': lru_lambda,
        'w_out': nrm(ks[20], (DEPTH, MIX_WIDTH, D_MODEL), MIX_WIDTH ** -0.5),
        'w_up': nrm(ks[21], (DEPTH, D_MODEL, D_FF), D_MODEL ** -0.5),
        'w_down': nrm(ks[22], (DEPTH, D_FF, D_MODEL), D_FF ** -0.5),
    }


def reference(x_prompt, x_sample, cache_k, cache_v, state_h, state_conv, page_table, meta_tokens,
              g_mix_pre, g_mix_post, g_mlp_pre, g_mlp_post, w_in, sb_bias, conv_w, conv_b, w_gate_a, b_gate_a,
              w_gate_x, b_gate_x, lru_lambda, w_out, w_up, w_down):
    B = x_prompt.shape[0]
    meta = jnp.broadcast_to(meta_tokens.astype(x_prompt.dtype)[None], (B, N_META, D_MODEL))
    xp = jnp.concatenate([meta, x_prompt], axis=1)
    xs = x_sample
    kp_l, vp_l, hp_l, cp_l, ks_l, vs_l, hs_l, cs_l = [], [], [], [], [], [], [], []
    for l in range(DEPTH):
        lw = (g_mix_pre[l], g_mix_post[l], g_mlp_pre[l], g_mlp_post[l], w_in[l], conv_w[l], conv_b[l],
              w_gate_a[l], b_gate_a[l], w_gate_x[l], b_gate_x[l], lru_lambda[l], w_out[l], w_up[l], w_down[l])
        conv0 = jnp.zeros((B, CONV_WIDTH - 1, LRU_WIDTH), xp.dtype)
        h0 = jnp.zeros((B, LRU_WIDTH), jnp.float32)
        attend_p = functools.partial(sb_prompt, bias=sb_bias[l])
        xp, kp, vp, hp, cp = layer(xp, attend_p, conv0, h0, *lw)
        attend_s = functools.partial(sb_sample, bias=sb_bias[l], cache_k=cache_k[l], cache_v=cache_v[l],
                                     page_table=page_table)
        xs, kn, vn, hn, cn = layer(xs, attend_s, state_conv[l], state_h[l], *lw)
        kp_l.append(kp); vp_l.append(vp); hp_l.append(hp); cp_l.append(cp)
        ks_l.append(kn); vs_l.append(vn); hs_l.append(hn); cs_l.append(cn)
    y_prompt = xp[:, N_META:]
    y_sample = xs
    k_prompt = jnp.stack(kp_l, 0)
    v_prompt = jnp.stack(vp_l, 0)
    h_prompt = jnp.stack(hp_l, 0)
    conv_prompt = jnp.stack(cp_l, 0)
    k_sample = jnp.stack(ks_l, 0)
    v_sample = jnp.stack(vs_l, 0)
    h_sample = jnp.stack(hs_l, 0)
    conv_sample = jnp.stack(cs_l, 0)
    return (y_prompt, y_sample, k_prompt, v_prompt, h_prompt, conv_prompt, k_sample, v_sample, h_sample, conv_sample)
```

```python
import contextlib
import numpy as np
import concourse.bass as bass
import concourse.mybir as mybir
from concourse.bass_utils import run_bass_kernel_spmd

F32 = mybir.dt.float32
BF16 = mybir.dt.bfloat16
I32 = mybir.dt.int32
AF = mybir.ActivationFunctionType
ALU = mybir.AluOpType

D = 1024
DBG = {}
NPAR = 48


class Trk:
    ND = 8

    def __init__(self, nc, es):
        self.nc = nc
        self.eng = {'pe': nc.tensor, 'act': nc.scalar, 'dve': nc.vector, 'pool': nc.gpsimd, 'sp': nc.sync}
        self.sem = {}
        self.cnt = {}
        for k in ('pe', 'act', 'dve', 'pool'):
            self.sem[k] = es.enter_context(nc.semaphore("s_" + k))
            self.cnt[k] = 0
        self.ndq = {'sp': self.ND, 'pool': DBG.get('ndpool', 2)}
        for q in ('sp', 'pool'):
            for i in range(self.ndq[q]):
                k = ('d', q, i)
                self.sem[k] = es.enter_context(nc.semaphore(f"d_{q}{i}"))
                self.cnt[k] = 0
        self.dnext = {'sp': 0, 'pool': 0}
        self.seen = {e: {} for e in self.eng}
        self.lastw = {}
        self.reads = {}

    def _wait(self, eng, tok):
        key, val = tok
        if eng == 'pe' and key == 'pe':
            return
        if self.seen[eng].get(key, 0) >= val:
            return
        self.eng[eng].wait_ge(self.sem[key], val)
        self.seen[eng][key] = val

    def _deps(self, eng, R, W):
        for r in R:
            t = self.lastw.get(r)
            if t is not None:
                self._wait(eng, t)
        for w in W:
            t = self.lastw.get(w)
            if t is not None:
                self._wait(eng, t)
            for t in self.reads.get(w, ()):
                self._wait(eng, t)

    def _record(self, tok, R, W):
        for w in W:
            self.lastw[w] = tok
            self.reads[w] = []
        for r in R:
            if r in W:
                continue
            lst = self.reads.setdefault(r, [])
            lst.append(tok)
            if len(lst) > 8:
                best = {}
                for k, v in lst:
                    if best.get(k, 0) < v:
                        best[k] = v
                self.reads[r] = list(best.items())

    def op(self, eng, fn, R=(), W=()):
        self._deps(eng, R, W)
        ins = fn(self.eng[eng])
        self.cnt[eng] += 1
        ins.then_inc(self.sem[eng], 1)
        tok = (eng, self.cnt[eng])
        self._record(tok, R, W)
        return tok

    def _dma_slot(self, q):
        i = self.dnext[q]
        self.dnext[q] = (i + 1) % self.ndq[q]
        k = ('d', q, i)
        if self.cnt[k] > 0:
            self._wait(q, (k, 16 * self.cnt[k]))
        return k

    def dma(self, q, out, in_, R=(), W=()):
        self._deps(q, R, W)
        k = self._dma_slot(q)
        ins = self.eng[q].dma_start(out=out, in_=in_)
        self.cnt[k] += 1
        ins.then_inc(self.sem[k], 16)
        tok = (k, 16 * self.cnt[k])
        self._record(tok, R, W)
        return tok

    def gather(self, out, in_, idx_ap, R=(), W=()):
        q = 'pool'
        self._deps(q, R, W)
        k = self._dma_slot(q)
        ins = self.nc.gpsimd.indirect_dma_start(
            out=out, out_offset=None, in_=in_,
            in_offset=bass.IndirectOffsetOnAxis(ap=idx_ap, axis=0))
        self.cnt[k] += 1
        ins.then_inc(self.sem[k], 16)
        tok = (k, 16 * self.cnt[k])
        self._record(tok, R, W)
        return tok

    def _all(self, eng):
        for k, c in self.cnt.items():
            if c == 0:
                continue
            v = c if isinstance(k, str) else 16 * c
            self._wait(eng, (k, v))

    def barrier(self):
        for e in ('pe', 'act', 'dve', 'pool', 'sp'):
            self._all(e)
        self.lastw.clear()
        self.reads.clear()

    def finish(self):
        self._all('sp')


def build(NSB, NPG, NPOOL, phases="LSKAM"):
    NB = 4 * NSB
    NT = NB * 128
    NOWN = NSB + 1
    SIDX = NSB
    nc = bass.Bass("TRN2", target_bir_lowering=False)
    es = contextlib.ExitStack()

    def din(name, shape, dt=F32):
        return nc.dram_tensor(name, shape, dt, kind="ExternalInput").ap()

    def dout(name, shape, dt=F32):
        return nc.dram_tensor(name, shape, dt, kind="ExternalOutput").ap()

    xloc = din("xloc", [NT, D])
    xs = din("xs", [128, D])
    ck = din("ck", [NPOOL * 128, 512])
    cv = din("cv", [NPOOL * 128, 512])
    pt = din("pt", [1, 16 * NPG], I32)
    sth = din("sth", [16, 512])
    stc = din("stc", [48, 512])
    vflag_d = din("vflag", [128, 4])
    params_d = din("params", [128, NPAR])
    gpost_d = din("gpost", [2, D])
    sbias_d = din("sbias", [1, 8])
    smask_d = din("smask", [128, 1024])
    w_in = din("w_in", [D, 2560])
    w_out = din("w_out", [D, D])
    w_up = din("w_up", [D, 4096])
    w_down = din("w_down", [4096, D])
    wga = din("wga", [8, 64, 64])
    wgx = din("wgx", [8, 64, 64])

    y_o = dout("y", [NOWN, 128, D])
    k_o = dout("ko", [NOWN, 128, 512])
    v_o = dout("vo", [NOWN, 128, 512])
    hp_o = dout("hp", [128, 4])
    cp_o = dout("cp", [3, 512])
    hs_o = dout("hs", [16, 512])
    cs_o = dout("cs", [16, 3, 512])

    mixs = nc.dram_tensor("mixs", [NOWN, 128, 8, 128], BF16).ap()
    x1s = nc.dram_tensor("x1s", [NOWN, 128, D], F32).ap()

    T = Trk(nc, es)

    def sbt(stack, name, shape, dt):
        return stack.enter_context(nc.sbuf_tensor("t_" + name, shape, dt))

    psf = [es.enter_context(nc.psum_tensor(f"ps{i}", [128, 1024], F32)) for i in range(4)]

    class PS:
        n = 0

        @staticmethod
        def half():
            i = PS.n % 8
            PS.n += 1
            return psf[i // 2][:, (i % 2) * 512:(i % 2) * 512 + 512], f"ps{i}"

        @staticmethod
        def full():
            if PS.n % 2:
                PS.n += 1
            i = PS.n % 8
            PS.n += 2
            return psf[i // 2], [f"ps{i}", f"ps{i + 1}"]

    identf = sbt(es, "identf", [128, 128], F32)
    identb = sbt(es, "identb", [128, 128], BF16)
    tri = sbt(es, "tri", [128, 128], BF16)
    otri = sbt(es, "otri", [128, 128], BF16)
    otrif = sbt(es, "otrif", [128, 128], F32)
    onesf = sbt(es, "onesf", [128, 128], F32)
    params = sbt(es, "params", [128, NPAR], F32)
    vflag = sbt(es, "vflagt", [128, 4], F32)
    lruc = sbt(es, "lruc", [128, 16], F32)
    ones2 = sbt(es, "ones2", [2, 128], BF16)
    brow_p = sbt(es, "brow_p", [2, 1024], BF16)
    brow_s = sbt(es, "brow_s", [2, 1024], BF16)
    zl = sbt(es, "zl", [1, 128], BF16)
    zrow = sbt(es, "zrow", [1, 512], BF16)
    bd = sbt(es, "bd", [128, 8, 128], BF16)

    T.op('pool', lambda e: e.memset(onesf[:], 1.0), W=["onesf"])
    T.op('pool', lambda e: e.affine_select(out=identf[:], in_=onesf[:], pattern=[[1, 128]], compare_op=ALU.is_equal,
                                           fill=0.0, base=0, channel_multiplier=-1), R=["onesf"], W=["identf"])
    T.op('pool', lambda e: e.tensor_copy(out=identb[:], in_=identf[:]), R=["identf"], W=["identb"])
    T.op('pool', lambda e: e.affine_select(out=tri[:], in_=onesf[:], pattern=[[-1, 128]], compare_op=ALU.is_ge,
                                           fill=0.0, base=0, channel_multiplier=1), R=["onesf"], W=["tri"])
    T.op('pool', lambda e: e.affine_select(out=otrif[:], in_=onesf[:], pattern=[[1, 128]], compare_op=ALU.is_gt,
                                           fill=0.0, base=0, channel_multiplier=-1), R=["onesf"], W=["otrif"])
    T.op('pool', lambda e: e.tensor_copy(out=otri[:], in_=otrif[:]), R=["otrif"], W=["otri"])
    T.op('pool', lambda e: e.memset(ones2[:], 1.0), W=["ones2"])
    T.op('pool', lambda e: e.memset(zl[:], 0.0), W=["zl"])
    T.op('pool', lambda e: e.memset(zrow[:], 0.0), W=["zrow"])
    T.dma('sp', params[:], params_d[:, :], W=["params"])
    T.dma('sp', vflag[:], vflag_d[:, :], W=["vflag"])

    with contextlib.ExitStack() as s0:
        bsrc = sbt(s0, "bsrc", [2, 8], F32)
        bst = sbt(s0, "bst", [2, 1024], F32)
        bst2 = sbt(s0, "bst2", [2, 1024], F32)
        lo_t = sbt(s0, "lo_t", [2, 1024], BF16)
        T.dma('sp', bsrc[0:1, :], sbias_d[:, :], W=["bsrc"])
        T.dma('sp', bsrc[1:2, :], sbias_d[:, :], W=["bsrc"])
        for dst, order in ((brow_p, "p"), (brow_s, "s")):
            dn = dst.name
            for hp in range(2):
                srcv = bsrc[:, :].rearrange("p (t hp) -> p hp t", hp=2)[:, hp, :]
                if order == "p":
                    T.op('dve', lambda e, hp=hp, srcv=srcv: e.tensor_copy(
                        out=bst[:, hp * 512:(hp + 1) * 512].rearrange("p (t q) -> p t q", t=4),
                        in_=srcv[:, :, None].to_broadcast([2, 4, 128])), R=["bsrc"], W=["bst"])
                else:
                    T.op('dve', lambda e, hp=hp, srcv=srcv: e.tensor_copy(
                        out=bst[:, hp * 512:(hp + 1) * 512].rearrange("p (s t q) -> p s t q", s=16, t=4),
                        in_=srcv[:, None, :, None].to_broadcast([2, 16, 4, 8])), R=["bsrc"], W=["bst"])
            T.op('dve', lambda e: e.tensor_copy(out=dst[:], in_=bst[:]), R=["bst"], W=[dn])
            T.op('dve', lambda e: e.tensor_copy(out=bst2[:], in_=dst[:]), R=[dn], W=["bst2"])
            T.op('dve', lambda e: e.tensor_tensor(out=bst2[:], in0=bst[:], in1=bst2[:], op=ALU.subtract),
                 R=["bst", "bst2"], W=["bst2"])
            T.op('dve', lambda e: e.tensor_copy(out=lo_t[:], in_=bst2[:]), R=["bst2"], W=["lo_t"])
            T.dma('sp', dst[1:2, :], lo_t[1:2, :], R=["lo_t"], W=[dn])

        T.op('act', lambda e: e.activation(out=lruc[:, 0:4], in_=params[:, 28:32], func=AF.Exp, scale=-1.0),
             R=["params"], W=["lruc"])
        T.op('act', lambda e: e.activation(out=lruc[:, 0:4], in_=lruc[:, 0:4], func=AF.Ln, bias=1.0, scale=1.0),
             R=["lruc"], W=["lruc"])
        T.op('dve', lambda e: e.tensor_scalar(out=lruc[:, 4:8], in0=lruc[:, 0:4], scalar1=-16.0, scalar2=None, op0=ALU.mult),
             R=["lruc"], W=["lruc"])
        T.op('dve', lambda e: e.tensor_scalar(out=lruc[:, 0:4], in0=lruc[:, 0:4], scalar1=-8.0, scalar2=None, op0=ALU.mult),
             R=["lruc"], W=["lruc"])
        T.op('dve', lambda e: e.tensor_scalar(out=lruc[:, 8:16], in0=params[:, 20:28], scalar1=-1.0, scalar2=None,
                                              op0=ALU.mult), R=["params", "lruc"], W=["lruc"])
        bdst = sbt(s0, "bdst", [128, 8, 128], F32)
        T.op('dve', lambda e: e.memset(bdst[:], 0.0), W=["bdst"])
        for gi, wsrc in enumerate((wga, wgx)):
            for t in range(4):
                for hb in range(2):
                    T.dma('sp', bdst[hb * 64:(hb + 1) * 64, gi * 4 + t, hb * 64:(hb + 1) * 64], wsrc[2 * t + hb],
                          W=["bdst"])
        T.op('dve', lambda e: e.tensor_copy(out=bd[:], in_=bdst[:]), R=["bdst"], W=["bd"])
        T.barrier()

    def load_w(stack, name, src, c0, c1, gcol):
        ncol = c1 - c0
        nk = src.shape[0] // 128
        wt = sbt(stack, name, [128, nk, ncol], BF16)
        with contextlib.ExitStack() as st:
            stg = [sbt(st, f"{name}_stg{i}", [128, 1024], F32) for i in range(3)]
            k = 0
            for dc in range(nk):
                for cc in range(0, ncol, 1024):
                    w = min(1024, ncol - cc)
                    sl = k % 3
                    sn = f"{name}_stg{sl}"
                    T.dma('sp', stg[sl][:, 0:w], src[dc * 128:(dc + 1) * 128, c0 + cc:c0 + cc + w], W=[sn])
                    eng = ('pool', 'dve', 'act')[k % 3] if gcol is None else ('act', 'dve')[k % 2]
                    if gcol is None:
                        if eng == 'act':
                            T.op(eng, lambda e, sl=sl, dc=dc, cc=cc, w=w: e.copy(out=wt[:, dc, cc:cc + w], in_=stg[sl][:, 0:w]),
                                 R=[sn], W=[name])
                        else:
                            T.op(eng, lambda e, sl=sl, dc=dc, cc=cc, w=w: e.tensor_copy(out=wt[:, dc, cc:cc + w],
                                                                                          in_=stg[sl][:, 0:w]), R=[sn], W=[name])
                    else:
                        gc = gcol + (dc % 8)
                        if eng == 'act':
                            T.op(eng, lambda e, sl=sl, dc=dc, cc=cc, w=w, gc=gc: e.activation(
                                out=wt[:, dc, cc:cc + w], in_=stg[sl][:, 0:w], func=AF.Identity, scale=params[:, gc:gc + 1]),
                                R=[sn, "params"], W=[name])
                        else:
                            T.op(eng, lambda e, sl=sl, dc=dc, cc=cc, w=w, gc=gc: e.tensor_scalar(
                                out=wt[:, dc, cc:cc + w], in0=stg[sl][:, 0:w], scalar1=params[:, gc:gc + 1],
                                scalar2=None, op0=ALU.mult), R=[sn, "params"], W=[name])
                    k += 1
            T.barrier()
        return wt

    class Prep:
        def __init__(self, stack, tag, nx=2):
            self.tag = tag
            self.nx = nx
            self.xst = [sbt(stack, f"{tag}_xst{i}", [128, D], F32) for i in range(nx)]
            self.hn = [sbt(stack, f"{tag}_hn{i}", [128, D], BF16) for i in range(nx)]
            self.junk = sbt(stack, f"{tag}_junk", [128, D], BF16)
            self.stat = [sbt(stack, f"{tag}_stat{i}", [128, 4], F32) for i in range(4)]
            self.k = 0
            self.ks = 0

        def rstd(self, src_ap, src_names):
            si = self.ks % 4
            self.ks += 1
            stat = self.stat[si]
            sname = f"{self.tag}_stat{si}"
            jn = f"{self.tag}_junk"
            T.op('act', lambda e: e.activation(out=self.junk[:], in_=src_ap, func=AF.Square, accum_out=stat[:, 0:1]),
                 R=src_names, W=[jn, sname])
            T.op('act', lambda e: e.activation(out=stat[:, 1:2], in_=stat[:, 0:1], func=AF.Ln, scale=1.0 / D, bias=1e-6),
                 R=[sname], W=[sname])
            T.op('act', lambda e: e.activation(out=stat[:, 2:3], in_=stat[:, 1:2], func=AF.Exp, scale=-0.5),
                 R=[sname], W=[sname])
            return stat, sname

        def run(self, xrows, dstT, dst_names, x_sb=None, x_names=None):
            sl = self.k % max(self.nx, 1)
            self.k += 1
            tag = self.tag
            if x_sb is None:
                T.dma('sp', self.xst[sl][:], xrows, W=[f"{tag}_xst{sl}"])
                x_sb = self.xst[sl]
                x_names = [f"{tag}_xst{sl}"]
            stat, sname = self.rstd(x_sb[:], x_names)
            hnn = f"{tag}_hn{sl}"
            T.op('act', lambda e: e.activation(out=self.hn[sl][:], in_=x_sb[:], func=AF.Identity, scale=stat[:, 2:3]),
                 R=x_names + [sname], W=[hnn])
            pb, pn = PS.half()
            pbb = pb.bitcast(BF16)

            def tr(e):
                for c in range(8):
                    ins = e.transpose(pbb[:, c * 128:(c + 1) * 128], self.hn[sl][:, c * 128:(c + 1) * 128], identb[:])
                return ins
            T.op('pe', tr, R=[hnn, "identb"], W=[pn])
            T.op('dve', lambda e: e.tensor_copy(out=dstT, in_=pbb.rearrange("p (c t) -> p c t", c=8)), R=[pn], W=dst_names)

    def mm_acc(e, out, pairs):
        ins = None
        n = len(pairs)
        for i, (l, r) in enumerate(pairs):
            ins = e.matmul(out, lhsT=l, rhs=r, start=(i == 0), stop=(i == n - 1))
        return ins

    if "L" in phases:
      with contextlib.ExitStack() as sL:
        wlg = load_w(sL, "wlg", w_in, 1536, 2560, 32)
        prep = Prep(sL, "pl")
        hnT = [sbt(sL, f"l_hnT{i}", [128, 8, 512], BF16) for i in range(2)]
        xle = sbt(sL, "xle", [128, 4, 515], F32)
        xles = sbt(sL, "xles", [128, 4, 16, 11], F32)
        hst = sbt(sL, "hst", [128, 4], F32)
        hs0 = sbt(sL, "hs0", [128, 4, 16], F32)
        hpo = sbt(sL, "hpo", [128, 4], F32)
        hsl = sbt(sL, "hsl", [128, 4, 16], F32)
        NTMP = 2
        tmp = {}
        for nm in ("acc", "er", "ei", "a", "a2", "bb", "hT"):
            tmp[nm] = [sbt(sL, f"l_{nm}{i}", [128, 512], F32) for i in range(NTMP)]
        xcb = [sbt(sL, f"l_xcb{i}", [128, 512], BF16) for i in range(NTMP)]
        gsm = {nm: [sbt(sL, f"l_g{nm}{i}", [128, 128], F32) for i in range(NTMP)] for nm in ("g", "u", "e")}
        lrub = [sbt(sL, f"lrub{i}", [128, 4, 128], BF16) for i in range(2)]
        xltm = sbt(sL, "xltm", [128, 512], F32)
        stsb = sbt(sL, "stsb", [48, 512], F32)
        sthb = sbt(sL, "sthb", [16, 512], F32)
        hso = sbt(sL, "hso", [16, 512], F32)

        T.op('dve', lambda e: e.memset(xle[:], 0.0), W=[f"xle{t}" for t in range(4)])
        T.op('dve', lambda e: e.memset(hst[:], 0.0), W=["hst"])
        tcount = [0]

        def lru_tile(t, ntok, sample, xview, xname, own_lo, own_rhs, own_rname, out_tile, out_name, first_sb):
            sl = tcount[0] % NTMP
            tcount[0] += 1
            tn = lambda nm: f"lt_{nm}{sl}"
            acc, er, ei, a, a2, bb, hT = (tmp[k][sl] for k in ("acc", "er", "ei", "a", "a2", "bb", "hT"))

            def v(tile_):
                if sample:
                    return tile_[:, 0:ntok].rearrange("p (s q) -> p s q", s=16)
                return tile_[:, 0:ntok]
            cw = lambda j: params[:, t * 4 + j:t * 4 + j + 1]
            T.op('dve', lambda e: e.tensor_scalar(out=v(acc), in0=xview(0), scalar1=cw(0), scalar2=params[:, 16 + t:17 + t],
                                                  op0=ALU.mult, op1=ALU.add), R=[xname, "params"], W=[tn("acc")])
            for j in (1, 2, 3):
                T.op('dve', lambda e, j=j: e.scalar_tensor_tensor(out=v(acc), in0=xview(j), scalar=cw(j), in1=v(acc),
                                                                  op0=ALU.mult, op1=ALU.add),
                     R=[xname, "params", tn("acc")], W=[tn("acc")])
            T.op('pool', lambda e: e.tensor_copy(out=xcb[sl][:, 0:ntok], in_=acc[:, 0:ntok]), R=[tn("acc")], W=[tn("xcb")])
            pa, pan = PS.half()
            px, pxn = PS.half()
            T.op('pe', lambda e: e.matmul(pa[:, 0:ntok], lhsT=bd[:, t, :], rhs=xcb[sl][:, 0:ntok], start=True, stop=True),
                 R=["bd", tn("xcb")], W=[pan])
            T.op('pe', lambda e: e.matmul(px[:, 0:ntok], lhsT=bd[:, 4 + t, :], rhs=xcb[sl][:, 0:ntok], start=True, stop=True),
                 R=["bd", tn("xcb")], W=[pxn])
            T.op('act', lambda e: e.activation(out=er[:, 0:ntok], in_=pa[:, 0:ntok], func=AF.Exp, scale=-1.0,
                                               bias=lruc[:, 8 + t:9 + t]), R=[pan, "lruc"], W=[tn("er")])
            T.op('act', lambda e: e.activation(out=ei[:, 0:ntok], in_=px[:, 0:ntok], func=AF.Exp, scale=-1.0,
                                               bias=lruc[:, 12 + t:13 + t]), R=[pxn, "lruc"], W=[tn("ei")])
            for nm, tl in (("er", er), ("ei", ei)):
                T.op('act', lambda e, tl=tl: e.activation(out=tl[:, 0:ntok], in_=tl[:, 0:ntok], func=AF.Ln, bias=1.0, scale=1.0),
                     R=[tn(nm)], W=[tn(nm)])
                T.op('act', lambda e, tl=tl: e.activation(out=tl[:, 0:ntok], in_=tl[:, 0:ntok], func=AF.Exp, scale=-1.0),
                     R=[tn(nm)], W=[tn(nm)])
            T.op('act', lambda e: e.activation(out=a[:, 0:ntok], in_=er[:, 0:ntok], func=AF.Exp, scale=lruc[:, t:t + 1]),
                 R=[tn("er"), "lruc"], W=[tn("a")])
            T.op('act', lambda e: e.activation(out=a2[:, 0:ntok], in_=er[:, 0:ntok], func=AF.Exp, scale=lruc[:, 4 + t:5 + t]),
                 R=[tn("er"), "lruc"], W=[tn("a2")])
            T.op('act', lambda e: e.activation(out=a2[:, 0:ntok], in_=a2[:, 0:ntok], func=AF.Ln, scale=-1.0, bias=1.0),
                 R=[tn("a2")], W=[tn("a2")])
            T.op('act', lambda e: e.activation(out=a2[:, 0:ntok], in_=a2[:, 0:ntok], func=AF.Exp, scale=0.5),
                 R=[tn("a2")], W=[tn("a2")])
            T.op('pool', lambda e: e.tensor_tensor(out=bb[:, 0:ntok], in0=ei[:, 0:ntok], in1=acc[:, 0:ntok], op=ALU.mult),
                 R=[tn("ei"), tn("acc")], W=[tn("bb")])
            T.op('dve', lambda e: e.tensor_tensor(out=bb[:, 0:ntok], in0=bb[:, 0:ntok], in1=a2[:, 0:ntok], op=ALU.mult),
                 R=[tn("bb"), tn("a2")], W=[tn("bb")])
            if first_sb:
                for blk in range(3):
                    T.op('dve', lambda e, blk=blk: e.tensor_scalar(out=bb[:, blk * 128:(blk + 1) * 128],
                                                                    in0=bb[:, blk * 128:(blk + 1) * 128],
                                                                    scalar1=vflag[:, blk:blk + 1], scalar2=None, op0=ALU.mult),
                         R=[tn("bb"), "vflag"], W=[tn("bb")])
            if not sample:
                T.op('dve', lambda e: e.tensor_tensor_scan(out=hT[:, 0:ntok], data0=a[:, 0:ntok], data1=bb[:, 0:ntok],
                                                           initial=hst[:, t:t + 1], op0=ALU.mult, op1=ALU.add),
                     R=[tn("a"), tn("bb"), "hst"], W=[tn("hT")])
                T.op('dve', lambda e: e.tensor_copy(out=hst[:, t:t + 1], in_=hT[:, ntok - 1:ntok]), R=[tn("hT")], W=["hst"])
            else:
                for s in range(16):
                    T.op('dve', lambda e, s=s: e.tensor_tensor_scan(out=hT[:, s * 8:(s + 1) * 8], data0=a[:, s * 8:(s + 1) * 8],
                                                                   data1=bb[:, s * 8:(s + 1) * 8], initial=hs0[:, t, s:s + 1],
                                                                   op0=ALU.mult, op1=ALU.add),
                         R=[tn("a"), tn("bb"), "hs0"], W=[tn("hT")])
                T.op('dve', lambda e: e.tensor_copy(out=hsl[:, t, :],
                                                    in_=hT[:, 0:128].rearrange("p (s q) -> p s q", s=16)[:, :, 7]),
                     R=[tn("hT")], W=["hsl"])
            g, u, ee = (gsm[k][sl] for k in ("g", "u", "e"))
            pg, pgn = PS.half()
            T.op('pe', lambda e: mm_acc(e, pg[:, 0:128], [(wlg[:, dc, 512 + t * 128:512 + (t + 1) * 128], own_rhs(dc))
                                                            for dc in range(8)]), R=["wlg", own_rname], W=[pgn])
            T.op('dve', lambda e: e.tensor_copy(out=g[:], in_=pg[:, 0:128]), R=[pgn], W=[tn("g")])
            T.op('pool', lambda e: e.tensor_tensor(out=u[:], in0=g[:], in1=g[:], op=ALU.mult), R=[tn("g")], W=[tn("u")])
            T.op('dve', lambda e: e.tensor_scalar(out=u[:], in0=u[:], scalar1=0.044715, scalar2=1.0, op0=ALU.mult, op1=ALU.add),
                 R=[tn("u")], W=[tn("u")])
            T.op('pool', lambda e: e.tensor_tensor(out=u[:], in0=u[:], in1=g[:], op=ALU.mult), R=[tn("u"), tn("g")], W=[tn("u")])
            T.op('act', lambda e: e.activation(out=ee[:], in_=u[:], func=AF.Exp, scale=-1.5957691216057308),
                 R=[tn("u")], W=[tn("e")])
            T.op('act', lambda e: e.activation(out=ee[:], in_=ee[:], func=AF.Ln, bias=1.0, scale=1.0), R=[tn("e")], W=[tn("e")])
            T.op('act', lambda e: e.activation(out=ee[:], in_=ee[:], func=AF.Exp, scale=-1.0), R=[tn("e")], W=[tn("e")])
            T.op('dve', lambda e: e.tensor_tensor(out=g[:], in0=g[:], in1=ee[:], op=ALU.mult), R=[tn("g"), tn("e")], W=[tn("g")])
            T.op('dve', lambda e: e.tensor_tensor(out=out_tile[:, t, :], in0=g[:], in1=hT[:, own_lo:own_lo + 128], op=ALU.mult),
                 R=[tn("g"), tn("hT")], W=[out_name])
            return hT, tn("hT")

        for sb in range(NSB):
            hb = hnT[sb % 2]
            hname = f"l_hnT{sb % 2}"
            for blk in range(4):
                L = sb * 4 + blk
                prep.run(xloc[L * 128:(L + 1) * 128, :], hb[:, :, blk * 128:(blk + 1) * 128], [hname])
            lb = lrub[sb % 2]
            lname = f"lrub{sb % 2}"
            for t in range(4):
                pxl, pxn = PS.half()
                T.op('pe', lambda e: mm_acc(e, pxl[:, :], [(wlg[:, dc, t * 128:(t + 1) * 128], hb[:, dc, :]) for dc in range(8)]),
                     R=["wlg", hname], W=[pxn])
                T.op('act', lambda e: e.copy(out=xle[:, t, 3:515], in_=pxl[:, :]), R=[pxn], W=[f"xle{t}"])
                hT, hTn = lru_tile(t, 512, False, lambda j: xle[:, t, j:j + 512], f"xle{t}", 384,
                                   lambda dc: hb[:, dc, 384:512], hname, lb, lname, sb == 0)
                T.op('pool', lambda e: e.tensor_copy(out=xle[:, t, 0:3], in_=xle[:, t, 512:515]), R=[f"xle{t}"], W=[f"xle{t}"])
                if sb == NSB - 1:
                    T.op('dve', lambda e: e.tensor_copy(out=hpo[:, t:t + 1], in_=hT[:, 384 + 15:384 + 16]), R=[hTn], W=["hpo"])
            T.dma('pool', mixs[sb, :, 4:8, :], lb[:], R=[lname], W=[f"mixs{sb}"])
            if sb == NSB - 1:
                T.dma('pool', hp_o[:, :], hpo[:], R=["hpo"])
                pc, pcn = PS.half()
                T.op('pe', lambda e: mm_acc(e, pc[:, :], [(hb[:, dc, 384:512], wlg[:, dc, 0:512]) for dc in range(8)]),
                     R=["wlg", hname], W=[pcn])
                T.op('dve', lambda e: e.tensor_copy(out=xltm[:], in_=pc[:, :]), R=[pcn], W=["xltm"])
                T.dma('pool', cp_o[:, :], xltm[13:16, :], R=["xltm"])

        hbs = hnT[NSB % 2]
        hsname = f"l_hnT{NSB % 2}"
        prep.run(xs[:, :], hbs[:, :, 0:128], [hsname])
        T.dma('sp', stsb[:], stc[:, :], W=["stsb"])
        T.dma('sp', sthb[:], sth[:, :], W=["sthb"])
        pst, pstn = PS.half()

        def trs(e):
            for t in range(4):
                ins = e.transpose(pst[:, t * 48:(t + 1) * 48], stsb[0:48, t * 128:(t + 1) * 128], identf[0:48, 0:48])
            return ins
        T.op('pe', trs, R=["stsb", "identf"], W=[pstn])
        T.op('dve', lambda e: e.tensor_copy(out=xles[:, :, :, 0:3],
                                            in_=pst[:, 0:192].rearrange("p (t s j) -> p t s j", t=4, s=16)), R=[pstn], W=["xles"])
        psh, pshn = PS.half()

        def trh(e):
            for t in range(4):
                ins = e.transpose(psh[:, t * 16:(t + 1) * 16], sthb[0:16, t * 128:(t + 1) * 128], identf[0:16, 0:16])
            return ins
        T.op('pe', trh, R=["sthb", "identf"], W=[pshn])
        T.op('dve', lambda e: e.tensor_copy(out=hs0[:], in_=psh[:, 0:64].rearrange("p (t s) -> p t s", t=4)), R=[pshn], W=["hs0"])
        lb = lrub[NSB % 2]
        lname = f"lrub{NSB % 2}"
        for t in range(4):
            pxl, pxn = PS.half()
            T.op('pe', lambda e: mm_acc(e, pxl[:, 0:128], [(wlg[:, dc, t * 128:(t + 1) * 128], hbs[:, dc, 0:128])
                                                             for dc in range(8)]), R=["wlg", hsname], W=[pxn])
            T.op('act', lambda e: e.copy(out=xles[:, t, :, 3:11], in_=pxl[:, 0:128].rearrange("p (s q) -> p s q", s=16)),
                 R=[pxn], W=["xles"])
            lru_tile(t, 128, True, lambda j: xles[:, t, :, j:j + 8], "xles", 0,
                     lambda dc: hbs[:, dc, 0:128], hsname, lb, lname, False)
        T.dma('pool', mixs[SIDX, :, 4:8, :], lb[:], R=[lname], W=[f"mixs{SIDX}"])
        pho, phon = PS.half()

        def trho(e):
            for t in range(4):
                ins = e.transpose(pho[0:16, t * 128:(t + 1) * 128], hsl[:, t, :], identf[:, :])
            return ins
        T.op('pe', trho, R=["hsl", "identf"], W=[phon])
        T.op('dve', lambda e: e.tensor_copy(out=hso[:], in_=pho[0:16, :]), R=[phon], W=["hso"])
        T.dma('pool', hs_o[:, :], hso[:], R=["hso"])
        pc, pcn = PS.half()
        T.op('pe', lambda e: mm_acc(e, pc[:, :], [(hbs[:, dc, 0:128], wlg[:, dc, 0:512]) for dc in range(8)]),
             R=["wlg", hsname, "xltm"], W=[pcn])
        T.op('dve', lambda e: e.tensor_copy(out=xltm[:], in_=pc[:, :]), R=[pcn], W=["xltm"])
        for s in range(16):
            T.dma('pool', cs_o[s, :, :], xltm[s * 8 + 5:s * 8 + 8, :], R=["xltm"])
        T.barrier()

    ZN = [["ps0", "ps1"], ["ps2", "ps3"]]
    XN = ["ps4", "ps5"]
    ON = "ps6"
    Xps = psf[2]
    Ops = psf[3][:, 0:512]
    sp7 = psf[3][:, 512:1024]

    def attn_pipeline(n, zgen, maskgen, wvgen, Et, St, Gt, Wt, tag, after_b2=None):
        T.op('pe', lambda e: e.matmul(Ops, lhsT=zl[0:1, :], rhs=zrow[0:1, :], start=True, stop=True), R=["zl", "zrow"], W=[ON])

        def A(r):
            zt = psf[r % 2]
            zn = ZN[r % 2]
            zgen(r, zt, zn)
            E = Et[r % 3]
            S = St[r % 3]
            T.op('act', lambda e: e.activation(out=E[:], in_=zt[:, :], func=AF.Exp), R=zn, W=[f"{tag}E{r % 3}"])
            maskgen(r, E, f"{tag}E{r % 3}")
            T.op('act', lambda e: e.activation(out=S[:], in_=E[:], func=AF.Ln, bias=1.0, scale=1.0),
                 R=[f"{tag}E{r % 3}"], W=[f"{tag}S{r % 3}"])

        def TRI(r):
            S = St[r % 3]

            def f(e):
                for hp in range(2):
                    ins = e.matmul(Xps[:, hp * 512:(hp + 1) * 512], lhsT=tri[:, :], rhs=S[:, hp * 512:(hp + 1) * 512],
                                   start=(r == 0), stop=True, skip_group_check=(r > 0))
                return ins
            T.op('pe', f, R=["tri", f"{tag}S{r % 3}"], W=XN)

        def OT(r):
            S = St[r % 3]

            def f(e):
                for hp in range(2):
                    ins = e.matmul(Xps[:, hp * 512:(hp + 1) * 512], lhsT=otri[:, :], rhs=S[:, hp * 512:(hp + 1) * 512],
                                   start=False, stop=True, skip_group_check=True)
                return ins
            T.op('pe', f, R=["otri", f"{tag}S{r % 3}"], W=XN)

        A(0)
        if n > 1:
            A(1)
        TRI(0)
        for r in range(n):
            E = Et[r % 3]
            G = Gt[r % 2]
            W = Wt[r % 2]
            T.op('act', lambda e: e.activation(out=G[:], in_=Xps[:, :], func=AF.Exp, scale=-1.0), R=XN, W=[f"{tag}G{r % 2}"])
            if r + 1 < n:
                OT(r)
                TRI(r + 1)
            if r + 2 < n:
                A(r + 2)
            T.op('dve', lambda e: e.tensor_tensor(out=W[:], in0=E[:], in1=G[:], op=ALU.mult),
                 R=[f"{tag}E{r % 3}", f"{tag}G{r % 2}"], W=[f"{tag}W{r % 2}"])
            wvgen(r, W, f"{tag}W{r % 2}")
            if after_b2 is not None:
                after_b2(r)

    if "S" in phases:
      with contextlib.ExitStack() as sS:
        wqkv = load_w(sS, "wqkv", w_in, 0, 1536, 32)
        prep = Prep(sS, "pq")
        hnTs = sbt(sS, "hnTs", [128, 8, 128], BF16)
        qTs = sbt(sS, "qTs", [128, 4, 128], F32)
        kTn = sbt(sS, "kTn", [128, 4, 128], F32)
        ktm = sbt(sS, "ktm", [128, 512], F32)
        vtm = sbt(sS, "vtm", [128, 512], F32)
        smask = sbt(sS, "smaskt", [128, 1024], F32)
        NI = 16 * NPG
        ptb = sbt(sS, "ptb", [128, NI], I32)
        ptf = sbt(sS, "ptf", [128, NI], F32)
        iot = sbt(sS, "iot", [128, 1], I32)
        iotf = sbt(sS, "iotf", [128, 1], F32)
        idxall = sbt(sS, "idxall", [128, NI], I32)
        kpg = sbt(sS, "kpg", [128, 16, 512], F32)
        vpg = sbt(sS, "vpg", [128, 16, 512], F32)
        ktmp = [sbt(sS, f"ktmp{i}", [128, 4, 128], F32) for i in range(2)]
        Et = [sbt(sS, f"sE{i}", [128, 1024], F32) for i in range(3)]
        St = [sbt(sS, f"sS{i}", [128, 1024], BF16) for i in range(3)]
        Gt = [sbt(sS, f"sG{i}", [128, 1024], F32) for i in range(2)]
        Wt = [sbt(sS, f"sW{i}", [128, 1024], F32) for i in range(2)]
        atto = sbt(sS, "satto", [128, 4, 128], BF16)

        T.dma('sp', smask[:], smask_d[:, :], W=["smask"])
        T.dma('sp', ptb[:], pt.partition_broadcast(128), W=["ptb"])
        T.op('pool', lambda e: e.iota(iot[:], pattern=[[0, 1]], base=0, channel_multiplier=1), W=["iot"])
        T.op('dve', lambda e: e.tensor_copy(out=iotf[:], in_=iot[:]), R=["iot"], W=["iotf"])
        T.op('dve', lambda e: e.tensor_copy(out=ptf[:], in_=ptb[:]), R=["ptb"], W=["ptf"])
        T.op('dve', lambda e: e.tensor_scalar(out=ptf[:], in0=ptf[:], scalar1=128.0, scalar2=iotf[:, 0:1], op0=ALU.mult,
                                              op1=ALU.add), R=["ptf", "iotf"], W=["ptf"])
        T.op('dve', lambda e: e.tensor_copy(out=idxall[:], in_=ptf[:]), R=["ptf"], W=["idxall"])

        prep.run(xs[:, :], hnTs[:, :, :], ["hnTs"])
        pq, pqn = PS.half()
        pk, pkn = PS.half()

        def projT(ps_, c0):
            def f(e):
                for t in range(4):
                    ins = mm_acc(e, ps_[:, t * 128:(t + 1) * 128],
                                 [(wqkv[:, dc, c0 + t * 128:c0 + (t + 1) * 128], hnTs[:, dc, :]) for dc in range(8)])
                return ins
            return f
        T.op('pe', projT(pq, 0), R=["wqkv", "hnTs"], W=[pqn])
        T.op('act', lambda e: e.activation(out=qTs[:].rearrange("p t n -> p (t n)"), in_=pq[:, :], func=AF.Copy, scale=0.125), R=[pqn], W=["qTs"])
        T.op('pe', projT(pk, 512), R=["wqkv", "hnTs"], W=[pkn])
        T.op('dve', lambda e: e.tensor_copy(out=kTn[:].rearrange("p t n -> p (t n)"), in_=pk[:, :]), R=[pkn], W=["kTn"])
        pk2, pk2n = PS.half()
        T.op('pe', lambda e: mm_acc(e, pk2[:, :], [(hnTs[:, dc, :], wqkv[:, dc, 512:1024]) for dc in range(8)]),
             R=["wqkv", "hnTs"], W=[pk2n])
        T.op('dve', lambda e: e.tensor_copy(out=ktm[:], in_=pk2[:, :]), R=[pk2n], W=["ktm"])
        T.dma('sp', k_o[SIDX], ktm[:], R=["ktm"])
        pv2, pv2n = PS.half()
        T.op('pe', lambda e: mm_acc(e, pv2[:, :], [(hnTs[:, dc, :], wqkv[:, dc, 1024:1536]) for dc in range(8)]),
             R=["wqkv", "hnTs"], W=[pv2n])
        T.op('dve', lambda e: e.tensor_copy(out=vtm[:], in_=pv2[:, :]), R=[pv2n], W=["vtm"])
        T.dma('sp', v_o[SIDX], vtm[:], R=["vtm"])
        T.barrier()

        def gat_k(r):
            i = NPG - r
            for s in range(16):
                T.gather(kpg[:, s, :], ck[:, :], idxall[:, s * NPG + i:s * NPG + i + 1], R=["idxall"], W=[f"kpg{s}"])

        def gat_v(r):
            i = NPG - r
            for s in range(16):
                T.gather(vpg[:, s, :], cv[:, :], idxall[:, s * NPG + i:s * NPG + i + 1], R=["idxall"], W=[f"vpg{s}"])

        def zgen(r, zt, zn):
            if r >= 1:
                gat_k(r)

            def fb(e):
                for hp in range(2):
                    ins = e.matmul(zt[:, hp * 512:(hp + 1) * 512], lhsT=ones2[:, :], rhs=brow_p[:, hp * 512:(hp + 1) * 512],
                                   start=True, stop=True)
                return ins
            T.op('pe', fb, R=["ones2", brow_p.name], W=zn)
            if r == 0:
                def f(e):
                    for h in range(8):
                        t, hp = h // 2, h % 2
                        out = zt[:, hp * 512 + t * 128:hp * 512 + (t + 1) * 128]
                        ins = e.matmul(out, lhsT=kTn[hp * 64:(hp + 1) * 64, t, :],
                                       rhs=qTs[hp * 64:(hp + 1) * 64, t, :],
                                       start=False, stop=True, skip_group_check=True)
                    return ins
                T.op('pe', f, R=["kTn", "qTs"], W=zn)
            else:
                for s in range(16):
                    kt = ktmp[s % 2]
                    ktn = f"ktmp{s % 2}"

                    def ftr(e):
                        for t in range(4):
                            ins = e.transpose(sp7[:, t * 128:(t + 1) * 128], kpg[:, s, t * 128:(t + 1) * 128], identf[:, :])
                        return ins
                    T.op('pe', ftr, R=[f"kpg{s}", "identf"], W=["ps7"])
                    T.op('dve', lambda e: e.tensor_copy(out=kt[:].rearrange("p t n -> p (t n)"), in_=sp7[:, :]),
                         R=["ps7"], W=[ktn])

                    def fq(e):
                        for h in range(8):
                            t, hp = h // 2, h % 2
                            c = hp * 512 + t * 128 + s * 8
                            ins = e.matmul(zt[:, c:c + 8], lhsT=kt[hp * 64:(hp + 1) * 64, t, :],
                                           rhs=qTs[hp * 64:(hp + 1) * 64, t, s * 8:(s + 1) * 8],
                                           start=False, stop=True, skip_group_check=True)
                        return ins
                    T.op('pe', fq, R=[ktn, "qTs"], W=zn)

        def maskgen(r, E, en):
            if r == 0:
                T.op('dve', lambda e: e.tensor_tensor(out=E[:], in0=E[:], in1=smask[:], op=ALU.mult), R=[en, "smask"], W=[en])

        def wvgen(r, W, wn):
            if r == 0:
                def f(e):
                    for h in range(8):
                        t, hp = h // 2, h % 2
                        out = Ops[hp * 64:(hp + 1) * 64, t * 128:(t + 1) * 128]
                        rhs = W[:, hp * 512 + t * 128:hp * 512 + (t + 1) * 128]
                        ins = e.matmul(out, lhsT=vtm[:, h * 64:(h + 1) * 64], rhs=rhs, start=False, stop=True,
                                       skip_group_check=True)
                    return ins
                T.op('pe', f, R=["vtm", wn], W=[ON])
            else:
                for s in range(16):
                    def f(e):
                        for h in range(8):
                            t, hp = h // 2, h % 2
                            c = hp * 512 + t * 128 + s * 8
                            ins = e.matmul(Ops[hp * 64:(hp + 1) * 64, t * 128 + s * 8:t * 128 + (s + 1) * 8],
                                           lhsT=vpg[:, s, h * 64:(h + 1) * 64], rhs=W[:, c:c + 8], start=False, stop=True,
                                           skip_group_check=True)
                        return ins
                    T.op('pe', f, R=[f"vpg{s}", wn], W=[ON])

        def after_b2(r):
            if r + 1 <= NPG:
                gat_v(r + 1)

        SS = DBG.get('sstop', 9)
        if SS >= 1:
            attn_pipeline(NPG + 1 if SS >= 2 else 1, zgen, maskgen, wvgen, Et, St, Gt, Wt, "s", after_b2 if SS >= 2 else None)
        T.op('dve', lambda e: e.tensor_copy(out=atto[:].rearrange("p t n -> p (t n)"), in_=Ops), R=[ON], W=["satto"])
        T.dma('pool', mixs[SIDX, :, 0:4, :], atto[:], R=["satto"], W=[f"mixa{SIDX}"])
        T.barrier()

    if "K" in phases:
      with contextlib.ExitStack() as sK:
        kT = sbt(sK, "kT", [128, 4, NT], BF16)
        Vr = sbt(sK, "Vr", [128, NB, 512], BF16)
        qT = sbt(sK, "qT", [128, NSB, 4, 128], BF16)
        with contextlib.ExitStack() as sK2:
            wqkv = load_w(sK2, "wqkv2", w_in, 0, 1536, 32)
            prep = Prep(sK2, "pk", nx=1)
            hnT = [sbt(sK2, f"k_hnT{i}", [128, 8, 512], BF16) for i in range(1)]
            kst = [sbt(sK2, f"kst{i}", [128, 512], F32) for i in range(1)]
            vst = [sbt(sK2, f"vst{i}", [128, 512], F32) for i in range(1)]
            KS = DBG.get('kstop', 9)
            for sb in range(NSB):
                if KS < 1:
                    break
                hb = hnT[0]
                hname = "k_hnT0"
                for blk in range(4):
                    L = sb * 4 + blk
                    prep.run(xloc[L * 128:(L + 1) * 128, :], hb[:, :, blk * 128:(blk + 1) * 128], [hname])
                for t in range(4):
                    if KS < 2:
                        break
                    pk, pkn = PS.half()
                    T.op('pe', lambda e: mm_acc(e, pk[:, :], [(wqkv[:, dc, 512 + t * 128:512 + (t + 1) * 128], hb[:, dc, :])
                                                               for dc in range(8)]), R=["wqkv2", hname], W=[pkn])
                    eng = 'act' if t % 2 else 'dve'
                    if eng == 'act':
                        T.op('act', lambda e: e.copy(out=kT[:, t, sb * 512:(sb + 1) * 512], in_=pk[:, :]), R=[pkn], W=[f"kT{sb}"])
                    else:
                        T.op('dve', lambda e: e.tensor_copy(out=kT[:, t, sb * 512:(sb + 1) * 512], in_=pk[:, :]),
                             R=[pkn], W=[f"kT{sb}"])
                for blk in range(4):
                    if KS < 3:
                        break
                    L = sb * 4 + blk
                    pv, pvn = PS.half()
                    T.op('pe', lambda e: mm_acc(e, pv[:, :], [(hb[:, dc, blk * 128:(blk + 1) * 128], wqkv[:, dc, 1024:1536])
                                                               for dc in range(8)]), R=["wqkv2", hname], W=[pvn])
                    if DBG.get('vevac', 1):
                        if blk == 1 and DBG.get('vact', 1):
                            T.op('act', lambda e: e.copy(out=Vr[:, L, :], in_=pv[:, :]), R=[pvn], W=[f"Vr{L}"])
                        else:
                            T.op('dve', lambda e: e.tensor_copy(out=Vr[:, L, :], in_=pv[:, :]), R=[pvn], W=[f"Vr{L}"])
                    if blk == 3 and DBG.get('vout', 1):
                        vs_ = vst[0]
                        T.op('dve', lambda e: e.tensor_copy(out=vs_[:], in_=pv[:, :]), R=[pvn], W=["vst0"])
                        T.dma('pool', v_o[sb], vs_[:], R=["vst0"])
                if KS < 4:
                    continue
                pk2, pk2n = PS.half()
                T.op('pe', lambda e: mm_acc(e, pk2[:, :], [(hb[:, dc, 384:512], wqkv[:, dc, 512:1024]) for dc in range(8)]),
                     R=["wqkv2", hname], W=[pk2n])
                ks_ = kst[0]
                T.op('act', lambda e: e.copy(out=ks_[:], in_=pk2[:, :]), R=[pk2n], W=["kst0"])
                T.dma('pool', k_o[sb], ks_[:], R=["kst0"])
                if KS < 5:
                    continue
                pq, pqn = PS.half()

                def fq(e):
                    for t in range(4):
                        ins = mm_acc(e, pq[:, t * 128:(t + 1) * 128],
                                     [(wqkv[:, dc, t * 128:(t + 1) * 128], hb[:, dc, 384:512]) for dc in range(8)])
                    return ins
                T.op('pe', fq, R=["wqkv2", hname], W=[pqn])
                T.op('act', lambda e: e.activation(out=qT[:, sb, :, :].rearrange("p t n -> p (t n)"), in_=pq[:, :], func=AF.Copy, scale=0.125), R=[pqn], W=["qT"])
            T.barrier()

        if "A" in phases:
          with contextlib.ExitStack() as sA:
            Et = [sbt(sA, f"aE{i}", [128, 1024], BF16) for i in range(3)]
            St = [sbt(sA, f"aS{i}", [128, 1024], BF16) for i in range(3)]
            Gt = [sbt(sA, f"aG{i}", [128, 1024], BF16) for i in range(2)]
            Wt = [sbt(sA, f"aW{i}", [128, 1024], BF16) for i in range(2)]
            atto = [sbt(sA, f"aatto{i}", [128, 4, 128], BF16) for i in range(2)]
            for m in range(NSB):
                L = 4 * m + 3

                def zgen(r, zt, zn):
                    Lk = L - r

                    def f(e):
                        for hp in range(2):
                            e.matmul(zt[:, hp * 512:(hp + 1) * 512], lhsT=ones2[:, :], rhs=brow_p[:, hp * 512:(hp + 1) * 512],
                                     start=True, stop=True)
                        for h in range(8):
                            t, hp = h // 2, h % 2
                            ins = e.matmul(zt[:, hp * 512 + t * 128:hp * 512 + (t + 1) * 128],
                                           lhsT=kT[hp * 64:(hp + 1) * 64, t, Lk * 128:(Lk + 1) * 128],
                                           rhs=qT[hp * 64:(hp + 1) * 64, m, t, :], start=False, stop=True, skip_group_check=True)
                        return ins
                    T.op('pe', f, R=["ones2", brow_p.name, "qT", f"kT{Lk // 4}"], W=zn)

                def maskgen(r, E, en):
                    Lk = L - r
                    if r == 0:
                        T.op('dve', lambda e: e.tensor_tensor(out=E[:].rearrange("p (h q) -> p h q", h=8),
                                                               in0=E[:].rearrange("p (h q) -> p h q", h=8),
                                                               in1=otri[:, None, :].to_broadcast([128, 8, 128]), op=ALU.mult),
                             R=[en, "otri"], W=[en])
                    if Lk < 3:
                        T.op('dve', lambda e: e.tensor_scalar(out=E[:], in0=E[:], scalar1=vflag[:, Lk:Lk + 1], scalar2=None,
                                                               op0=ALU.mult), R=[en, "vflag"], W=[en])

                def wvgen(r, W, wn):
                    Lk = L - r

                    def f(e):
                        for h in range(8):
                            t, hp = h // 2, h % 2
                            ins = e.matmul(Ops[hp * 64:(hp + 1) * 64, t * 128:(t + 1) * 128],
                                           lhsT=Vr[:, Lk, h * 64:(h + 1) * 64],
                                           rhs=W[:, hp * 512 + t * 128:hp * 512 + (t + 1) * 128], start=False, stop=True,
                                           skip_group_check=True)
                        return ins
                    T.op('pe', f, R=[f"Vr{Lk}", wn], W=[ON])

                attn_pipeline(L + 1, zgen, maskgen, wvgen, Et, St, Gt, Wt, "a")
                ao = atto[m % 2]
                T.op('dve', lambda e: e.tensor_copy(out=ao[:].rearrange("p t n -> p (t n)"), in_=Ops), R=[ON], W=[f"aatto{m % 2}"])
                T.dma('pool', mixs[m, :, 0:4, :], ao[:], R=[f"aatto{m % 2}"], W=[f"mixa{m}"])
            T.barrier()

    if "M" in phases:
      with contextlib.ExitStack() as sM:
        with contextlib.ExitStack() as sM1:
            gpb = sbt(sM1, "gpb", [128, D], F32)
            T.dma('sp', gpb[:, :], gpost_d[0:1, :].partition_broadcast(128), W=["gpb"])
            wo = load_w(sM1, "wo", w_out, 0, D, None)
            prep = Prep(sM1, "pm")
            mixT = [sbt(sM1, f"mixT{i}", [128, 8, 128], BF16) for i in range(2)]
            xr = [sbt(sM1, f"xr{i}", [128, D], F32) for i in range(2)]
            x1b = [sbt(sM1, f"x1b{i}", [128, D], F32) for i in range(2)]
            for o in range(NOWN):
                sl = o % 2
                T.dma('sp', mixT[sl][:], mixs[o], W=[f"mixT{sl}"])
                rows = xs[:, :] if o == SIDX else xloc[(4 * o + 3) * 128:(4 * o + 4) * 128, :]
                T.dma('sp', xr[sl][:], rows, W=[f"xr{sl}"])
                pm, pmn = PS.full()

                def f(e):
                    for hf in range(2):
                        ins = mm_acc(e, pm[:, hf * 512:(hf + 1) * 512],
                                     [(mixT[sl][:, fc, :], wo[:, fc, hf * 512:(hf + 1) * 512]) for fc in range(8)])
                    return ins
                T.op('pe', f, R=["wo", f"mixT{sl}"], W=pmn)
                stat, sname = prep.rstd(pm[:, :], pmn)
                T.op('dve', lambda e: e.scalar_tensor_tensor(out=x1b[sl][:], in0=pm[:, :], scalar=stat[:, 2:3], in1=gpb[:, :],
                                                             op0=ALU.mult, op1=ALU.mult), R=pmn + [sname, "gpb"], W=[f"x1b{sl}"])
                T.op('pool', lambda e: e.tensor_tensor(out=x1b[sl][:], in0=x1b[sl][:], in1=xr[sl][:], op=ALU.add),
                     R=[f"x1b{sl}", f"xr{sl}"], W=[f"x1b{sl}"])
                T.dma('pool', x1s[o], x1b[sl][:], R=[f"x1b{sl}"], W=[f"x1s{o}"])
            T.barrier()

        with contextlib.ExitStack() as sM2:
            wu = load_w(sM2, "wu", w_up, 0, 4096, 40)
            wd = load_w(sM2, "wd", w_down, 0, D, None)
            gpb = sbt(sM2, "gpb2", [128, D], F32)
            T.dma('sp', gpb[:, :], gpost_d[1:2, :].partition_broadcast(128), W=["gpb"])
            prep = Prep(sM2, "pn", nx=1)
            GB = 4
            x1 = [sbt(sM2, f"x1_{i}", [128, D], F32) for i in range(GB)]
            hn2T = sbt(sM2, "hn2T", [128, 8, GB * 128], BF16)
            u2T = sbt(sM2, "u2T", [128, 32, GB * 128], BF16)
            sqv = prep.junk[:, :].bitcast(F32)
            for g0 in range(0, NOWN, GB):
                nb = min(GB, NOWN - g0)
                ntok = nb * 128
                for i in range(nb):
                    o = g0 + i
                    T.dma('sp', x1[i][:], x1s[o], R=[f"x1s{o}"], W=[f"x1_{i}"])
                    prep.run(None, hn2T[:, :, i * 128:(i + 1) * 128], ["hn2T"], x_sb=x1[i], x_names=[f"x1_{i}"])
                for fc in range(32):
                    pu, pun = PS.half()
                    T.op('pe', lambda e: mm_acc(e, pu[:, 0:ntok], [(wu[:, dc, fc * 128:(fc + 1) * 128], hn2T[:, dc, 0:ntok])
                                                                    for dc in range(8)]), R=["wu", "hn2T"], W=[pun])
                    T.op('act', lambda e: e.activation(out=sqv[:, 0:ntok], in_=pu[:, 0:ntok], func=AF.Square),
                         R=[pun], W=["pn_junk"])
                    T.op('dve', lambda e: e.scalar_tensor_tensor(out=u2T[:, fc, 0:ntok], in0=pu[:, 0:ntok], scalar=0.0,
                                                                 in1=sqv[:, 0:ntok], op0=ALU.is_gt, op1=ALU.mult),
                         R=[pun, "pn_junk"], W=["u2T"])
                for i in range(nb):
                    o = g0 + i
                    pd, pdn = PS.full()

                    def f(e):
                        for hf in range(2):
                            ins = mm_acc(e, pd[:, hf * 512:(hf + 1) * 512],
                                         [(u2T[:, fc, i * 128:(i + 1) * 128], wd[:, fc, hf * 512:(hf + 1) * 512])
                                          for fc in range(32)])
                        return ins
                    T.op('pe', f, R=["wd", "u2T"], W=pdn)
                    stat, sname = prep.rstd(pd[:, :], pdn)
                    T.op('dve', lambda e: e.scalar_tensor_tensor(out=prep.xst[0][:], in0=pd[:, :], scalar=stat[:, 2:3],
                                                                 in1=gpb[:, :], op0=ALU.mult, op1=ALU.mult),
                         R=pdn + [sname, "gpb"], W=["pn_xst0"])
                    T.op('pool', lambda e: e.tensor_tensor(out=x1[i][:], in0=prep.xst[0][:], in1=x1[i][:],
                                                           op=ALU.add), R=["pn_xst0", f"x1_{i}"], W=[f"x1_{i}"])
                    T.dma('pool', y_o[o], x1[i][:], R=[f"x1_{i}"])
            T.barrier()

    T.finish()
    es.close()
    return nc


def _host_inputs(inp, NSB, NPG, NPOOL):
    f32 = np.float32
    xp = np.asarray(inp["x_prompt"], f32)
    xsm = np.asarray(inp["x_sample"], f32)
    meta = np.asarray(inp["meta_tokens"], f32)
    B, SEQ, _ = xp.shape
    TR = SEQ + meta.shape[0]
    NB = 4 * NSB
    assert TR <= (NB - 3) * 128
    ck = np.ascontiguousarray(np.asarray(inp["cache_k"], f32)[0].reshape(NPOOL * 128, 512))
    cv = np.ascontiguousarray(np.asarray(inp["cache_v"], f32)[0].reshape(NPOOL * 128, 512))
    ptab = np.asarray(inp["page_table"], np.int32)
    params = np.zeros((128, NPAR), f32)

    def chan(v):
        return np.asarray(v, f32).reshape(4, 128).T
    cw = np.asarray(inp["conv_w"], f32)[0]
    for j in range(4):
        params[:, j:16:4] = chan(cw[j])
    params[:, 16:20] = chan(inp["conv_b"][0])
    params[:, 20:24] = chan(inp["b_gate_a"][0])
    params[:, 24:28] = chan(inp["b_gate_x"][0])
    params[:, 28:32] = chan(inp["lru_lambda"][0])
    params[:, 32:40] = np.asarray(inp["g_mix_pre"], f32)[0].reshape(8, 128).T
    params[:, 40:48] = np.asarray(inp["g_mlp_pre"], f32)[0].reshape(8, 128).T
    gpost = np.stack([np.asarray(inp["g_mix_post"], f32)[0], np.asarray(inp["g_mlp_post"], f32)[0]], 0)
    k_s = np.arange(128) // 8
    k_t = np.arange(128) % 8
    col = np.arange(1024)
    c_s = (col % 128) // 8
    c_q = col % 8
    smask = ((k_s[:, None] == c_s[None, :]) & (k_t[:, None] < c_q[None, :])).astype(f32)
    shared = dict(ck=ck, cv=cv, params=params, gpost=np.ascontiguousarray(gpost),
                  sbias=np.asarray(inp["sb_bias"], f32).reshape(1, 8), smask=smask,
                  w_in=np.ascontiguousarray(np.asarray(inp["w_in"], f32)[0]),
                  w_out=np.ascontiguousarray(np.asarray(inp["w_out"], f32)[0]),
                  w_up=np.ascontiguousarray(np.asarray(inp["w_up"], f32)[0]),
                  w_down=np.ascontiguousarray(np.asarray(inp["w_down"], f32)[0]),
                  wga=np.ascontiguousarray(np.asarray(inp["w_gate_a"], f32)[0]),
                  wgx=np.ascontiguousarray(np.asarray(inp["w_gate_x"], f32)[0]))
    maps = []
    for c in range(8):
        b, j = c // 4, c % 4
        xloc = np.zeros((NB * 128, D), f32)
        o = (3 - j) * 128
        xloc[o:o + meta.shape[0]] = meta
        xloc[o + meta.shape[0]:o + TR] = xp[b]
        vflag = np.ones((128, 4), f32)
        for L in range(3):
            if L < 3 - j:
                vflag[:, L] = 0.0
        m = dict(shared)
        m.update(xloc=xloc, xs=np.ascontiguousarray(xsm[16 * c:16 * c + 16].reshape(128, D)),
                 pt=np.ascontiguousarray(ptab[16 * c:16 * c + 16].reshape(1, 16 * NPG)),
                 sth=np.ascontiguousarray(np.asarray(inp["state_h"], f32)[0, 16 * c:16 * c + 16]),
                 stc=np.ascontiguousarray(np.asarray(inp["state_conv"], f32)[0, 16 * c:16 * c + 16].reshape(48, 512)),
                 vflag=vflag)
        maps.append(m)
    return maps


def _host_outputs(res, inp, NSB):
    f32 = np.float32
    B, SEQ, _ = inp["x_prompt"].shape
    NM = inp["meta_tokens"].shape[0]
    TR = SEQ + NM
    yp = np.zeros((B, TR, D), f32)
    kp = np.zeros((B, TR, 512), f32)
    vp = np.zeros((B, TR, 512), f32)
    hp = np.zeros((1, B, 512), f32)
    cp = np.zeros((1, B, 3, 512), f32)
    ys = np.zeros((128, 8, D), f32)
    ks = np.zeros((1, 128, 8, 8, 64), f32)
    vs = np.zeros((1, 128, 8, 8, 64), f32)
    hs = np.zeros((1, 128, 512), f32)
    cs = np.zeros((1, 128, 3, 512), f32)
    glast = (TR - 1) // 128
    for c in range(8):
        b, j = c // 4, c % 4
        r = res[c]
        for m in range(NSB):
            g = 4 * m + j
            lo = g * 128
            if lo >= TR:
                continue
            n = min(128, TR - lo)
            yp[b, lo:lo + n] = r["y"][m, :n]
            kp[b, lo:lo + n] = r["ko"][m, :n]
            vp[b, lo:lo + n] = r["vo"][m, :n]
        if j == glast % 4:
            hp[0, b] = r["hp"].T.reshape(512)
            cp[0, b] = r["cp"]
        ys[16 * c:16 * c + 16] = r["y"][NSB].reshape(16, 8, D)
        ks[0, 16 * c:16 * c + 16] = r["ko"][NSB].reshape(16, 8, 8, 64)
        vs[0, 16 * c:16 * c + 16] = r["vo"][NSB].reshape(16, 8, 8, 64)
        hs[0, 16 * c:16 * c + 16] = r["hs"]
        cs[0, 16 * c:16 * c + 16] = r["cs"]
    return (yp[:, NM:], ys, kp.reshape(1, B, TR, 8, 64), vp.reshape(1, B, TR, 8, 64), hp, cp, ks, vs, hs, cs)


_CACHE = {}


def kernel(**inputs):
    SEQ = inputs["x_prompt"].shape[1]
    NM = inputs["meta_tokens"].shape[0]
    nblk = -(-(SEQ + NM) // 128)
    NSB = -(-(nblk + 3) // 4)
    NPG = inputs["page_table"].shape[1]
    NPOOL = inputs["cache_k"].shape[1]
    key = (NSB, NPG, NPOOL)
    if key not in _CACHE:
        _CACHE[key] = build(NSB, NPG, NPOOL)
    nc = _CACHE[key]
    maps = _host_inputs(inputs, NSB, NPG, NPOOL)
    res = run_bass_kernel_spmd(nc, maps, core_ids=list(range(8)))
    return _host_outputs(res.results, inputs, NSB)
```

```python
import contextlib
import numpy as np
import concourse.bass as bass
import concourse.mybir as mybir
from concourse.bass_utils import run_bass_kernel_spmd

F32 = mybir.dt.float32
BF16 = mybir.dt.bfloat16
I32 = mybir.dt.int32
AF = mybir.ActivationFunctionType
ALU = mybir.AluOpType

D = 1024
DBG = {}
NPAR = 48


class Trk:
    ND = 8

    def __init__(self, nc, es):
        self.nc = nc
        self.eng = {'pe': nc.tensor, 'act': nc.scalar, 'dve': nc.vector, 'pool': nc.gpsimd, 'sp': nc.sync}
        self.sem = {}
        self.cnt = {}
        for k in ('pe', 'act', 'dve', 'pool'):
            self.sem[k] = es.enter_context(nc.semaphore("s_" + k))
            self.cnt[k] = 0
        self.ndq = {'sp': self.ND, 'pool': DBG.get('ndpool', 2)}
        for q in ('sp', 'pool'):
            for i in range(self.ndq[q]):
                k = ('d', q, i)
                self.sem[k] = es.enter_context(nc.semaphore(f"d_{q}{i}"))
                self.cnt[k] = 0
        self.dnext = {'sp': 0, 'pool': 0}
        self.seen = {e: {} for e in self.eng}
        self.lastw = {}
        self.reads = {}

    def _wait(self, eng, tok):
        key, val = tok
        if eng == 'pe' and key == 'pe':
            return
        if self.seen[eng].get(key, 0) >= val:
            return
        self.eng[eng].wait_ge(self.sem[key], val)
        self.seen[eng][key] = val

    def _deps(self, eng, R, W):
        for r in R:
            t = self.lastw.get(r)
            if t is not None:
                self._wait(eng, t)
        for w in W:
            t = self.lastw.get(w)
            if t is not None:
                self._wait(eng, t)
            for t in self.reads.get(w, ()):
                self._wait(eng, t)

    def _record(self, tok, R, W):
        for w in W:
            self.lastw[w] = tok
            self.reads[w] = []
        for r in R:
            if r in W:
                continue
            lst = self.reads.setdefault(r, [])
            lst.append(tok)
            if len(lst) > 8:
                best = {}
                for k, v in lst:
                    if best.get(k, 0) < v:
                        best[k] = v
                self.reads[r] = list(best.items())

    def op(self, eng, fn, R=(), W=()):
        self._deps(eng, R, W)
        ins = fn(self.eng[eng])
        self.cnt[eng] += 1
        ins.then_inc(self.sem[eng], 1)
        tok = (eng, self.cnt[eng])
        self._record(tok, R, W)
        return tok

    def _dma_slot(self, q):
        i = self.dnext[q]
        self.dnext[q] = (i + 1) % self.ndq[q]
        k = ('d', q, i)
        if self.cnt[k] > 0:
            self._wait(q, (k, 16 * self.cnt[k]))
        return k

    def dma(self, q, out, in_, R=(), W=()):
        self._deps(q, R, W)
        k = self._dma_slot(q)
        ins = self.eng[q].dma_start(out=out, in_=in_)
        self.cnt[k] += 1
        ins.then_inc(self.sem[k], 16)
        tok = (k, 16 * self.cnt[k])
        self._record(tok, R, W)
        return tok

    def gather(self, out, in_, idx_ap, R=(), W=()):
        q = 'pool'
        self._deps(q, R, W)
        k = self._dma_slot(q)
        ins = self.nc.gpsimd.indirect_dma_start(
            out=out, out_offset=None, in_=in_,
            in_offset=bass.IndirectOffsetOnAxis(ap=idx_ap, axis=0))
        self.cnt[k] += 1
        ins.then_inc(self.sem[k], 16)
        tok = (k, 16 * self.cnt[k])
        self._record(tok, R, W)
        return tok

    def _all(self, eng):
        for k, c in self.cnt.items():
            if c == 0:
                continue
            v = c if isinstance(k, str) else 16 * c
            self._wait(eng, (k, v))

    def barrier(self):
        for e in ('pe', 'act', 'dve', 'pool', 'sp'):
            self._all(e)
        self.lastw.clear()
        self.reads.clear()

    def finish(self):
        self._all('sp')


def build(NSB, NPG, NPOOL, phases="LSKAM"):
    NB = 4 * NSB
    NT = NB * 128
    NOWN = NSB + 1
    SIDX = NSB
    nc = bass.Bass("TRN2", target_bir_lowering=False)
    es = contextlib.ExitStack()

    def din(name, shape, dt=F32):
        return nc.dram_tensor(name, shape, dt, kind="ExternalInput").ap()

    def dout(name, shape, dt=F32):
        return nc.dram_tensor(name, shape, dt, kind="ExternalOutput").ap()

    xloc = din("xloc", [NT, D])
    xs = din("xs", [128, D])
    ck = din("ck", [NPOOL * 128, 512])
    cv = din("cv", [NPOOL * 128, 512])
    pt = din("pt", [1, 16 * NPG], I32)
    sth = din("sth", [16, 512])
    stc = din("stc", [48, 512])
    vflag_d = din("vflag", [128, 4])
    params_d = din("params", [128, NPAR])
    gpost_d = din("gpost", [2, D])
    sbias_d = din("sbias", [1, 8])
    smask_d = din("smask", [128, 1024])
    w_in = din("w_in", [D, 2560])
    w_out = din("w_out", [D, D])
    w_up = din("w_up", [D, 4096])
    w_down = din("w_down", [4096, D])
    wga = din("wga", [8, 64, 64])
    wgx = din("wgx", [8, 64, 64])

    y_o = dout("y", [NOWN, 128, D])
    k_o = dout("ko", [NOWN, 128, 512])
    v_o = dout("vo", [NOWN, 128, 512])
    hp_o = dout("hp", [128, 4])
    cp_o = dout("cp", [3, 512])
    hs_o = dout("hs", [16, 512])
    cs_o = dout("cs", [16, 3, 512])

    mixs = nc.dram_tensor("mixs", [NOWN, 128, 8, 128], BF16).ap()
    x1s = nc.dram_tensor("x1s", [NOWN, 128, D], F32).ap()

    T = Trk(nc, es)

    def sbt(stack, name, shape, dt):
        return stack.enter_context(nc.sbuf_tensor("t_" + name, shape, dt))

    psf = [es.enter_context(nc.psum_tensor(f"ps{i}", [128, 1024], F32)) for i in range(4)]

    class PS:
        n = 0

        @staticmethod
        def half():
            i = PS.n % 8
            PS.n += 1
            return psf[i // 2][:, (i % 2) * 512:(i % 2) * 512 + 512], f"ps{i}"

        @staticmethod
        def full():
            if PS.n % 2:
                PS.n += 1
            i = PS.n % 8
            PS.n += 2
            return psf[i // 2], [f"ps{i}", f"ps{i + 1}"]

    identf = sbt(es, "identf", [128, 128], F32)
    identb = sbt(es, "identb", [128, 128], BF16)
    tri = sbt(es, "tri", [128, 128], BF16)
    otri = sbt(es, "otri", [128, 128], BF16)
    otrif = sbt(es, "otrif", [128, 128], F32)
    onesf = sbt(es, "onesf", [128, 128], F32)
    params = sbt(es, "params", [128, NPAR], F32)
    vflag = sbt(es, "vflagt", [128, 4], F32)
    lruc = sbt(es, "lruc", [128, 16], F32)
    ones2 = sbt(es, "ones2", [2, 128], BF16)
    brow_p = sbt(es, "brow_p", [2, 1024], BF16)
    brow_s = sbt(es, "brow_s", [2, 1024], BF16)
    zl = sbt(es, "zl", [1, 128], BF16)
    zrow = sbt(es, "zrow", [1, 512], BF16)
    bd = sbt(es, "bd", [128, 8, 128], BF16)

    T.op('pool', lambda e: e.memset(onesf[:], 1.0), W=["onesf"])
    T.op('pool', lambda e: e.affine_select(out=identf[:], in_=onesf[:], pattern=[[1, 128]], compare_op=ALU.is_equal,
                                           fill=0.0, base=0, channel_multiplier=-1), R=["onesf"], W=["identf"])
    T.op('pool', lambda e: e.tensor_copy(out=identb[:], in_=identf[:]), R=["identf"], W=["identb"])
    T.op('pool', lambda e: e.affine_select(out=tri[:], in_=onesf[:], pattern=[[-1, 128]], compare_op=ALU.is_ge,
                                           fill=0.0, base=0, channel_multiplier=1), R=["onesf"], W=["tri"])
    T.op('pool', lambda e: e.affine_select(out=otrif[:], in_=onesf[:], pattern=[[1, 128]], compare_op=ALU.is_gt,
                                           fill=0.0, base=0, channel_multiplier=-1), R=["onesf"], W=["otrif"])
    T.op('pool', lambda e: e.tensor_copy(out=otri[:], in_=otrif[:]), R=["otrif"], W=["otri"])
    T.op('pool', lambda e: e.memset(ones2[:], 1.0), W=["ones2"])
    T.op('pool', lambda e: e.memset(zl[:], 0.0), W=["zl"])
    T.op('pool', lambda e: e.memset(zrow[:], 0.0), W=["zrow"])
    T.dma('sp', params[:], params_d[:, :], W=["params"])
    T.dma('sp', vflag[:], vflag_d[:, :], W=["vflag"])

    with contextlib.ExitStack() as s0:
        bsrc = sbt(s0, "bsrc", [2, 8], F32)
        bst = sbt(s0, "bst", [2, 1024], F32)
        bst2 = sbt(s0, "bst2", [2, 1024], F32)
        lo_t = sbt(s0, "lo_t", [2, 1024], BF16)
        T.dma('sp', bsrc[0:1, :], sbias_d[:, :], W=["bsrc"])
        T.dma('sp', bsrc[1:2, :], sbias_d[:, :], W=["bsrc"])
        for dst, order in ((brow_p, "p"), (brow_s, "s")):
            dn = dst.name
            for hp in range(2):
                srcv = bsrc[:, :].rearrange("p (t hp) -> p hp t", hp=2)[:, hp, :]
                if order == "p":
                    T.op('dve', lambda e, hp=hp, srcv=srcv: e.tensor_copy(
                        out=bst[:, hp * 512:(hp + 1) * 512].rearrange("p (t q) -> p t q", t=4),
                        in_=srcv[:, :, None].to_broadcast([2, 4, 128])), R=["bsrc"], W=["bst"])
                else:
                    T.op('dve', lambda e, hp=hp, srcv=srcv: e.tensor_copy(
                        out=bst[:, hp * 512:(hp + 1) * 512].rearrange("p (s t q) -> p s t q", s=16, t=4),
                        in_=srcv[:, None, :, None].to_broadcast([2, 16, 4, 8])), R=["bsrc"], W=["bst"])
            T.op('dve', lambda e: e.tensor_copy(out=dst[:], in_=bst[:]), R=["bst"], W=[dn])
            T.op('dve', lambda e: e.tensor_copy(out=bst2[:], in_=dst[:]), R=[dn], W=["bst2"])
            T.op('dve', lambda e: e.tensor_tensor(out=bst2[:], in0=bst[:], in1=bst2[:], op=ALU.subtract),
                 R=["bst", "bst2"], W=["bst2"])
            T.op('dve', lambda e: e.tensor_copy(out=lo_t[:], in_=bst2[:]), R=["bst2"], W=["lo_t"])
            T.dma('sp', dst[1:2, :], lo_t[1:2, :], R=["lo_t"], W=[dn])

        T.op('act', lambda e: e.activation(out=lruc[:, 0:4], in_=params[:, 28:32], func=AF.Exp, scale=-1.0),
             R=["params"], W=["lruc"])
        T.op('act', lambda e: e.activation(out=lruc[:, 0:4], in_=lruc[:, 0:4], func=AF.Ln, bias=1.0, scale=1.0),
             R=["lruc"], W=["lruc"])
        T.op('dve', lambda e: e.tensor_scalar(out=lruc[:, 4:8], in0=lruc[:, 0:4], scalar1=-16.0, scalar2=None, op0=ALU.mult),
             R=["lruc"], W=["lruc"])
        T.op('dve', lambda e: e.tensor_scalar(out=lruc[:, 0:4], in0=lruc[:, 0:4], scalar1=-8.0, scalar2=None, op0=ALU.mult),
             R=["lruc"], W=["lruc"])
        T.op('dve', lambda e: e.tensor_scalar(out=lruc[:, 8:16], in0=params[:, 20:28], scalar1=-1.0, scalar2=None,
                                              op0=ALU.mult), R=["params", "lruc"], W=["lruc"])
        bdst = sbt(s0, "bdst", [128, 8, 128], F32)
        T.op('dve', lambda e: e.memset(bdst[:], 0.0), W=["bdst"])
        for gi, wsrc in enumerate((wga, wgx)):
            for t in range(4):
                for hb in range(2):
                    T.dma('sp', bdst[hb * 64:(hb + 1) * 64, gi * 4 + t, hb * 64:(hb + 1) * 64], wsrc[2 * t + hb],
                          W=["bdst"])
        T.op('dve', lambda e: e.tensor_copy(out=bd[:], in_=bdst[:]), R=["bdst"], W=["bd"])
        T.barrier()

    def load_w(stack, name, src, c0, c1, gcol):
        ncol = c1 - c0
        nk = src.shape[0] // 128
        wt = sbt(stack, name, [128, nk, ncol], BF16)
        with contextlib.ExitStack() as st:
            stg = [sbt(st, f"{name}_stg{i}", [128, 1024], F32) for i in range(3)]
            k = 0
            for dc in range(nk):
                for cc in range(0, ncol, 1024):
                    w = min(1024, ncol - cc)
                    sl = k % 3
                    sn = f"{name}_stg{sl}"
                    T.dma('sp', stg[sl][:, 0:w], src[dc * 128:(dc + 1) * 128, c0 + cc:c0 + cc + w], W=[sn])
                    eng = ('pool', 'dve', 'act')[k % 3] if gcol is None else ('act', 'dve')[k % 2]
                    if gcol is None:
                        if eng == 'act':
                            T.op(eng, lambda e, sl=sl, dc=dc, cc=cc, w=w: e.copy(out=wt[:, dc, cc:cc + w], in_=stg[sl][:, 0:w]),
                                 R=[sn], W=[name])
                        else:
                            T.op(eng, lambda e, sl=sl, dc=dc, cc=cc, w=w: e.tensor_copy(out=wt[:, dc, cc:cc + w],
                                                                                          in_=stg[sl][:, 0:w]), R=[sn], W=[name])
                    else:
                        gc = gcol + (dc % 8)
                        if eng == 'act':
                            T.op(eng, lambda e, sl=sl, dc=dc, cc=cc, w=w, gc=gc: e.activation(
                                out=wt[:, dc, cc:cc + w], in_=stg[sl][:, 0:w], func=AF.Identity, scale=params[:, gc:gc + 1]),
                                R=[sn, "params"], W=[name])
                        else:
                            T.op(eng, lambda e, sl=sl, dc=dc, cc=cc, w=w, gc=gc: e.tensor_scalar(
                                out=wt[:, dc, cc:cc + w], in0=stg[sl][:, 0:w], scalar1=params[:, gc:gc + 1],
                                scalar2=None, op0=ALU.mult), R=[sn, "params"], W=[name])
                    k += 1
            T.barrier()
        return wt

    class Prep:
        def __init__(self, stack, tag, nx=2):
            self.tag = tag
            self.nx = nx
            self.xst = [sbt(stack, f"{tag}_xst{i}", [128, D], F32) for i in range(nx)]
            self.hn = [sbt(stack, f"{tag}_hn{i}", [128, D], BF16) for i in range(nx)]
            self.junk = sbt(stack, f"{tag}_junk", [128, D], BF16)
            self.stat = [sbt(stack, f"{tag}_stat{i}", [128, 4], F32) for i in range(4)]
            self.k = 0
            self.ks = 0

        def rstd(self, src_ap, src_names):
            si = self.ks % 4
            self.ks += 1
            stat = self.stat[si]
            sname = f"{self.tag}_stat{si}"
            jn = f"{self.tag}_junk"
            T.op('act', lambda e: e.activation(out=self.junk[:], in_=src_ap, func=AF.Square, accum_out=stat[:, 0:1]),
                 R=src_names, W=[jn, sname])
            T.op('act', lambda e: e.activation(out=stat[:, 1:2], in_=stat[:, 0:1], func=AF.Ln, scale=1.0 / D, bias=1e-6),
                 R=[sname], W=[sname])
            T.op('act', lambda e: e.activation(out=stat[:, 2:3], in_=stat[:, 1:2], func=AF.Exp, scale=-0.5),
                 R=[sname], W=[sname])
            return stat, sname

        def run(self, xrows, dstT, dst_names, x_sb=None, x_names=None):
            sl = self.k % max(self.nx, 1)
            self.k += 1
            tag = self.tag
            if x_sb is None:
                T.dma('sp', self.xst[sl][:], xrows, W=[f"{tag}_xst{sl}"])
                x_sb = self.xst[sl]
                x_names = [f"{tag}_xst{sl}"]
            stat, sname = self.rstd(x_sb[:], x_names)
            hnn = f"{tag}_hn{sl}"
            T.op('act', lambda e: e.activation(out=self.hn[sl][:], in_=x_sb[:], func=AF.Identity, scale=stat[:, 2:3]),
                 R=x_names + [sname], W=[hnn])
            pb, pn = PS.half()
            pbb = pb.bitcast(BF16)

            def tr(e):
                for c in range(8):
                    ins = e.transpose(pbb[:, c * 128:(c + 1) * 128], self.hn[sl][:, c * 128:(c + 1) * 128], identb[:])
                return ins
            T.op('pe', tr, R=[hnn, "identb"], W=[pn])
            T.op('dve', lambda e: e.tensor_copy(out=dstT, in_=pbb.rearrange("p (c t) -> p c t", c=8)), R=[pn], W=dst_names)

    def mm_acc(e, out, pairs):
        ins = None
        n = len(pairs)
        for i, (l, r) in enumerate(pairs):
            ins = e.matmul(out, lhsT=l, rhs=r, start=(i == 0), stop=(i == n - 1))
        return ins

    if "L" in phases:
      with contextlib.ExitStack() as sL:
        wlg = load_w(sL, "wlg", w_in, 1536, 2560, 32)
        prep = Prep(sL, "pl")
        hnT = [sbt(sL, f"l_hnT{i}", [128, 8, 512], BF16) for i in range(2)]
        xle = sbt(sL, "xle", [128, 4, 515], F32)
        xles = sbt(sL, "xles", [128, 4, 16, 11], F32)
        hst = sbt(sL, "hst", [128, 4], F32)
        hs0 = sbt(sL, "hs0", [128, 4, 16], F32)
        hpo = sbt(sL, "hpo", [128, 4], F32)
        hsl = sbt(sL, "hsl", [128, 4, 16], F32)
        NTMP = 4
        tmp = {}
        for nm in ("acc", "er", "ei", "a", "a2", "bb", "hT"):
            tmp[nm] = [sbt(sL, f"l_{nm}{i}", [128, 512], F32) for i in range(NTMP)]
        xcb = [sbt(sL, f"l_xcb{i}", [128, 512], BF16) for i in range(NTMP)]
        gsm = {nm: [sbt(sL, f"l_g{nm}{i}", [128, 128], F32) for i in range(NTMP)] for nm in ("g", "u", "e")}
        lrub = [sbt(sL, f"lrub{i}", [128, 4, 128], BF16) for i in range(2)]
        xltm = sbt(sL, "xltm", [128, 512], F32)
        stsb = sbt(sL, "stsb", [48, 512], F32)
        sthb = sbt(sL, "sthb", [16, 512], F32)
        hso = sbt(sL, "hso", [16, 512], F32)

        T.op('dve', lambda e: e.memset(xle[:], 0.0), W=[f"xle{t}" for t in range(4)])
        T.op('dve', lambda e: e.memset(hst[:], 0.0), W=["hst"])
        tcount = [0]

        lt_res = {}

        def lru_tile(t, ntok, sample, xview, xname, own_lo, own_rhs, own_rname, out_tile, out_name, first_sb):
            sl = tcount[0] % NTMP
            tcount[0] += 1
            tn = lambda nm: f"lt_{nm}{sl}"
            acc, er, ei, a, a2, bb, hT = (tmp[k][sl] for k in ("acc", "er", "ei", "a", "a2", "bb", "hT"))

            def v(tile_):
                if sample:
                    return tile_[:, 0:ntok].rearrange("p (s q) -> p s q", s=16)
                return tile_[:, 0:ntok]
            cw = lambda j: params[:, t * 4 + j:t * 4 + j + 1]
            T.op('dve', lambda e: e.tensor_scalar(out=v(acc), in0=xview(0), scalar1=cw(0), scalar2=params[:, 16 + t:17 + t],
                                                  op0=ALU.mult, op1=ALU.add), R=[xname, "params"], W=[tn("acc")])
            for j in (1, 2, 3):
                T.op('dve', lambda e, j=j: e.scalar_tensor_tensor(out=v(acc), in0=xview(j), scalar=cw(j), in1=v(acc),
                                                                  op0=ALU.mult, op1=ALU.add),
                     R=[xname, "params", tn("acc")], W=[tn("acc")])
            T.op('pool', lambda e: e.tensor_copy(out=xcb[sl][:, 0:ntok], in_=acc[:, 0:ntok]), R=[tn("acc")], W=[tn("xcb")])
            yield
            pa, pan = PS.half()
            px, pxn = PS.half()
            T.op('pe', lambda e: e.matmul(pa[:, 0:ntok], lhsT=bd[:, t, :], rhs=xcb[sl][:, 0:ntok], start=True, stop=True),
                 R=["bd", tn("xcb")], W=[pan])
            T.op('pe', lambda e: e.matmul(px[:, 0:ntok], lhsT=bd[:, 4 + t, :], rhs=xcb[sl][:, 0:ntok], start=True, stop=True),
                 R=["bd", tn("xcb")], W=[pxn])
            yield
            T.op('act', lambda e: e.activation(out=er[:, 0:ntok], in_=pa[:, 0:ntok], func=AF.Exp, scale=-1.0,
                                               bias=lruc[:, 8 + t:9 + t]), R=[pan, "lruc"], W=[tn("er")])
            T.op('act', lambda e: e.activation(out=ei[:, 0:ntok], in_=px[:, 0:ntok], func=AF.Exp, scale=-1.0,
                                               bias=lruc[:, 12 + t:13 + t]), R=[pxn, "lruc"], W=[tn("ei")])
            yield
            for nm, tl in (("er", er), ("ei", ei)):
                T.op('act', lambda e, tl=tl: e.activation(out=tl[:, 0:ntok], in_=tl[:, 0:ntok], func=AF.Ln, bias=1.0, scale=1.0),
                     R=[tn(nm)], W=[tn(nm)])
                T.op('act', lambda e, tl=tl: e.activation(out=tl[:, 0:ntok], in_=tl[:, 0:ntok], func=AF.Exp, scale=-1.0),
                     R=[tn(nm)], W=[tn(nm)])
            T.op('act', lambda e: e.activation(out=a[:, 0:ntok], in_=er[:, 0:ntok], func=AF.Exp, scale=lruc[:, t:t + 1]),
                 R=[tn("er"), "lruc"], W=[tn("a")])
            T.op('act', lambda e: e.activation(out=a2[:, 0:ntok], in_=er[:, 0:ntok], func=AF.Exp, scale=lruc[:, 4 + t:5 + t]),
                 R=[tn("er"), "lruc"], W=[tn("a2")])
            yield
            T.op('act', lambda e: e.activation(out=a2[:, 0:ntok], in_=a2[:, 0:ntok], func=AF.Ln, scale=-1.0, bias=1.0),
                 R=[tn("a2")], W=[tn("a2")])
            T.op('act', lambda e: e.activation(out=a2[:, 0:ntok], in_=a2[:, 0:ntok], func=AF.Exp, scale=0.5),
                 R=[tn("a2")], W=[tn("a2")])
            yield
            T.op('pool', lambda e: e.tensor_tensor(out=bb[:, 0:ntok], in0=ei[:, 0:ntok], in1=acc[:, 0:ntok], op=ALU.mult),
                 R=[tn("ei"), tn("acc")], W=[tn("bb")])
            T.op('dve', lambda e: e.tensor_tensor(out=bb[:, 0:ntok], in0=bb[:, 0:ntok], in1=a2[:, 0:ntok], op=ALU.mult),
                 R=[tn("bb"), tn("a2")], W=[tn("bb")])
            yield
            if first_sb:
                for blk in range(3):
                    T.op('dve', lambda e, blk=blk: e.tensor_scalar(out=bb[:, blk * 128:(blk + 1) * 128],
                                                                    in0=bb[:, blk * 128:(blk + 1) * 128],
                                                                    scalar1=vflag[:, blk:blk + 1], scalar2=None, op0=ALU.mult),
                         R=[tn("bb"), "vflag"], W=[tn("bb")])
            if not sample:
                T.op('dve', lambda e: e.tensor_tensor_scan(out=hT[:, 0:ntok], data0=a[:, 0:ntok], data1=bb[:, 0:ntok],
                                                           initial=hst[:, t:t + 1], op0=ALU.mult, op1=ALU.add),
                     R=[tn("a"), tn("bb"), "hst"], W=[tn("hT")])
                T.op('dve', lambda e: e.tensor_copy(out=hst[:, t:t + 1], in_=hT[:, ntok - 1:ntok]), R=[tn("hT")], W=["hst"])
            else:
                for s in range(16):
                    T.op('dve', lambda e, s=s: e.tensor_tensor_scan(out=hT[:, s * 8:(s + 1) * 8], data0=a[:, s * 8:(s + 1) * 8],
                                                                   data1=bb[:, s * 8:(s + 1) * 8], initial=hs0[:, t, s:s + 1],
                                                                   op0=ALU.mult, op1=ALU.add),
                         R=[tn("a"), tn("bb"), "hs0"], W=[tn("hT")])
                T.op('dve', lambda e: e.tensor_copy(out=hsl[:, t, :],
                                                    in_=hT[:, 0:128].rearrange("p (s q) -> p s q", s=16)[:, :, 7]),
                     R=[tn("hT")], W=["hsl"])
            g, u, ee = (gsm[k][sl] for k in ("g", "u", "e"))
            pg, pgn = PS.half()
            T.op('pe', lambda e: mm_acc(e, pg[:, 0:128], [(wlg[:, dc, 512 + t * 128:512 + (t + 1) * 128], own_rhs(dc))
                                                            for dc in range(8)]), R=["wlg", own_rname], W=[pgn])
            T.op('dve', lambda e: e.tensor_copy(out=g[:], in_=pg[:, 0:128]), R=[pgn], W=[tn("g")])
            yield
            T.op('pool', lambda e: e.tensor_tensor(out=u[:], in0=g[:], in1=g[:], op=ALU.mult), R=[tn("g")], W=[tn("u")])
            T.op('dve', lambda e: e.tensor_scalar(out=u[:], in0=u[:], scalar1=0.044715, scalar2=1.0, op0=ALU.mult, op1=ALU.add),
                 R=[tn("u")], W=[tn("u")])
            T.op('pool', lambda e: e.tensor_tensor(out=u[:], in0=u[:], in1=g[:], op=ALU.mult), R=[tn("u"), tn("g")], W=[tn("u")])
            T.op('act', lambda e: e.activation(out=ee[:], in_=u[:], func=AF.Exp, scale=-1.5957691216057308),
                 R=[tn("u")], W=[tn("e")])
            yield
            T.op('act', lambda e: e.activation(out=ee[:], in_=ee[:], func=AF.Ln, bias=1.0, scale=1.0), R=[tn("e")], W=[tn("e")])
            T.op('act', lambda e: e.activation(out=ee[:], in_=ee[:], func=AF.Exp, scale=-1.0), R=[tn("e")], W=[tn("e")])
            T.op('dve', lambda e: e.tensor_tensor(out=g[:], in0=g[:], in1=ee[:], op=ALU.mult), R=[tn("g"), tn("e")], W=[tn("g")])
            T.op('dve', lambda e: e.tensor_tensor(out=out_tile[:, t, :], in0=g[:], in1=hT[:, own_lo:own_lo + 128], op=ALU.mult),
                 R=[tn("g"), tn("hT")], W=[out_name])
            lt_res[t] = (hT, tn("hT"))
            yield

        for sb in range(NSB):
            hb = hnT[sb % 2]
            hname = f"l_hnT{sb % 2}"
            for blk in range(4):
                L = sb * 4 + blk
                prep.run(xloc[L * 128:(L + 1) * 128, :], hb[:, :, blk * 128:(blk + 1) * 128], [hname])
            lb = lrub[sb % 2]
            lname = f"lrub{sb % 2}"
            gens = []
            for t in range(4):
                pxl, pxn = PS.half()
                T.op('pe', lambda e: mm_acc(e, pxl[:, :], [(wlg[:, dc, t * 128:(t + 1) * 128], hb[:, dc, :]) for dc in range(8)]),
                     R=["wlg", hname], W=[pxn])
                T.op('act', lambda e: e.copy(out=xle[:, t, 3:515], in_=pxl[:, :]), R=[pxn], W=[f"xle{t}"])
                gens.append(lru_tile(t, 512, False, (lambda t: (lambda j: xle[:, t, j:j + 512]))(t), f"xle{t}", 384,
                                     lambda dc: hb[:, dc, 384:512], hname, lb, lname, sb == 0))
            live = list(gens)
            while live:
                for g_ in list(live):
                    try:
                        next(g_)
                    except StopIteration:
                        live.remove(g_)
            for t in range(4):
                hT, hTn = lt_res[t]
                T.op('pool', lambda e: e.tensor_copy(out=xle[:, t, 0:3], in_=xle[:, t, 512:515]), R=[f"xle{t}"], W=[f"xle{t}"])
                if sb == NSB - 1:
                    T.op('dve', lambda e: e.tensor_copy(out=hpo[:, t:t + 1], in_=hT[:, 384 + 15:384 + 16]), R=[hTn], W=["hpo"])
            T.dma('pool', mixs[sb, :, 4:8, :], lb[:], R=[lname], W=[f"mixs{sb}"])
            if sb == NSB - 1:
                T.dma('pool', hp_o[:, :], hpo[:], R=["hpo"])
                pc, pcn = PS.half()
                T.op('pe', lambda e: mm_acc(e, pc[:, :], [(hb[:, dc, 384:512], wlg[:, dc, 0:512]) for dc in range(8)]),
                     R=["wlg", hname], W=[pcn])
                T.op('dve', lambda e: e.tensor_copy(out=xltm[:], in_=pc[:, :]), R=[pcn], W=["xltm"])
                T.dma('pool', cp_o[:, :], xltm[13:16, :], R=["xltm"])

        hbs = hnT[NSB % 2]
        hsname = f"l_hnT{NSB % 2}"
        prep.run(xs[:, :], hbs[:, :, 0:128], [hsname])
        T.dma('sp', stsb[:], stc[:, :], W=["stsb"])
        T.dma('sp', sthb[:], sth[:, :], W=["sthb"])
        pst, pstn = PS.half()

        def trs(e):
            for t in range(4):
                ins = e.transpose(pst[:, t * 48:(t + 1) * 48], stsb[0:48, t * 128:(t + 1) * 128], identf[0:48, 0:48])
            return ins
        T.op('pe', trs, R=["stsb", "identf"], W=[pstn])
        T.op('dve', lambda e: e.tensor_copy(out=xles[:, :, :, 0:3],
                                            in_=pst[:, 0:192].rearrange("p (t s j) -> p t s j", t=4, s=16)), R=[pstn], W=["xles"])
        psh, pshn = PS.half()

        def trh(e):
            for t in range(4):
                ins = e.transpose(psh[:, t * 16:(t + 1) * 16], sthb[0:16, t * 128:(t + 1) * 128], identf[0:16, 0:16])
            return ins
        T.op('pe', trh, R=["sthb", "identf"], W=[pshn])
        T.op('dve', lambda e: e.tensor_copy(out=hs0[:], in_=psh[:, 0:64].rearrange("p (t s) -> p t s", t=4)), R=[pshn], W=["hs0"])
        lb = lrub[NSB % 2]
        lname = f"lrub{NSB % 2}"
        for t in range(4):
            pxl, pxn = PS.half()
            T.op('pe', lambda e: mm_acc(e, pxl[:, 0:128], [(wlg[:, dc, t * 128:(t + 1) * 128], hbs[:, dc, 0:128])
                                                             for dc in range(8)]), R=["wlg", hsname], W=[pxn])
            T.op('act', lambda e: e.copy(out=xles[:, t, :, 3:11], in_=pxl[:, 0:128].rearrange("p (s q) -> p s q", s=16)),
                 R=[pxn], W=["xles"])
            for _ in lru_tile(t, 128, True, lambda j: xles[:, t, :, j:j + 8], "xles", 0,
                              lambda dc: hbs[:, dc, 0:128], hsname, lb, lname, False):
                pass
        T.dma('pool', mixs[SIDX, :, 4:8, :], lb[:], R=[lname], W=[f"mixs{SIDX}"])
        pho, phon = PS.half()

        def trho(e):
            for t in range(4):
                ins = e.transpose(pho[0:16, t * 128:(t + 1) * 128], hsl[:, t, :], identf[:, :])
            return ins
        T.op('pe', trho, R=["hsl", "identf"], W=[phon])
        T.op('dve', lambda e: e.tensor_copy(out=hso[:], in_=pho[0:16, :]), R=[phon], W=["hso"])
        T.dma('pool', hs_o[:, :], hso[:], R=["hso"])
        pc, pcn = PS.half()
        T.op('pe', lambda e: mm_acc(e, pc[:, :], [(hbs[:, dc, 0:128], wlg[:, dc, 0:512]) for dc in range(8)]),
             R=["wlg", hsname, "xltm"], W=[pcn])
        T.op('dve', lambda e: e.tensor_copy(out=xltm[:], in_=pc[:, :]), R=[pcn], W=["xltm"])
        for s in range(16):
            T.dma('pool', cs_o[s, :, :], xltm[s * 8 + 5:s * 8 + 8, :], R=["xltm"])
        T.barrier()

    ZN = [["ps0", "ps1"], ["ps2", "ps3"]]
    XN = ["ps4", "ps5"]
    ON = "ps6"
    Xps = psf[2]
    Ops = psf[3][:, 0:512]
    sp7 = psf[3][:, 512:1024]

    def attn_pipeline(n, zgen, maskgen, wvgen, Et, St, Gt, Wt, tag, after_b2=None):
        T.op('pe', lambda e: e.matmul(Ops, lhsT=zl[0:1, :], rhs=zrow[0:1, :], start=True, stop=True), R=["zl", "zrow"], W=[ON])

        def Z(r):
            zgen(r, psf[r % 2], ZN[r % 2])

        def ES(r):
            zt = psf[r % 2]
            zn = ZN[r % 2]
            E = Et[r % 3]
            S = St[r % 3]
            T.op('act', lambda e: e.activation(out=E[:], in_=zt[:, :], func=AF.Exp), R=zn, W=[f"{tag}E{r % 3}"])
            maskgen(r, E, f"{tag}E{r % 3}")
            T.op('act', lambda e: e.activation(out=S[:], in_=E[:], func=AF.Ln, bias=1.0, scale=1.0),
                 R=[f"{tag}E{r % 3}"], W=[f"{tag}S{r % 3}"])

        def TRI(r):
            S = St[r % 3]

            def f(e):
                for hp in range(2):
                    ins = e.matmul(Xps[:, hp * 512:(hp + 1) * 512], lhsT=tri[:, :], rhs=S[:, hp * 512:(hp + 1) * 512],
                                   start=(r == 0), stop=True, skip_group_check=(r > 0))
                return ins
            T.op('pe', f, R=["tri", f"{tag}S{r % 3}"], W=XN)

        def OT(r):
            S = St[r % 3]

            def f(e):
                for hp in range(2):
                    ins = e.matmul(Xps[:, hp * 512:(hp + 1) * 512], lhsT=otri[:, :], rhs=S[:, hp * 512:(hp + 1) * 512],
                                   start=False, stop=True, skip_group_check=True)
                return ins
            T.op('pe', f, R=["otri", f"{tag}S{r % 3}"], W=XN)

        Z(0)
        ES(0)
        if n > 1:
            Z(1)
            ES(1)
        if n > 2:
            Z(2)
        TRI(0)
        for r in range(n):
            E = Et[r % 3]
            G = Gt[r % 2]
            W = Wt[r % 2]
            T.op('act', lambda e: e.activation(out=G[:], in_=Xps[:, :], func=AF.Exp, scale=-1.0), R=XN, W=[f"{tag}G{r % 2}"])
            if r + 1 < n:
                OT(r)
                TRI(r + 1)
            if r + 2 < n:
                ES(r + 2)
            if r + 3 < n:
                Z(r + 3)
            T.op('dve', lambda e: e.tensor_tensor(out=W[:], in0=E[:], in1=G[:], op=ALU.mult),
                 R=[f"{tag}E{r % 3}", f"{tag}G{r % 2}"], W=[f"{tag}W{r % 2}"])
            wvgen(r, W, f"{tag}W{r % 2}")
            if after_b2 is not None:
                after_b2(r)

    if "S" in phases:
      with contextlib.ExitStack() as sS:
        wqkv = load_w(sS, "wqkv", w_in, 0, 1536, 32)
        prep = Prep(sS, "pq")
        hnTs = sbt(sS, "hnTs", [128, 8, 128], BF16)
        qTs = sbt(sS, "qTs", [128, 4, 128], F32)
        kTn = sbt(sS, "kTn", [128, 4, 128], F32)
        ktm = sbt(sS, "ktm", [128, 512], F32)
        vtm = sbt(sS, "vtm", [128, 512], F32)
        smask = sbt(sS, "smaskt", [128, 1024], F32)
        NI = 16 * NPG
        ptb = sbt(sS, "ptb", [128, NI], I32)
        ptf = sbt(sS, "ptf", [128, NI], F32)
        iot = sbt(sS, "iot", [128, 1], I32)
        iotf = sbt(sS, "iotf", [128, 1], F32)
        idxall = sbt(sS, "idxall", [128, NI], I32)
        kpg = sbt(sS, "kpg", [128, 16, 512], F32)
        vpg = sbt(sS, "vpg", [128, 16, 512], F32)
        ktmp = [sbt(sS, f"ktmp{i}", [128, 4, 128], F32) for i in range(2)]
        Et = [sbt(sS, f"sE{i}", [128, 1024], F32) for i in range(3)]
        St = [sbt(sS, f"sS{i}", [128, 1024], BF16) for i in range(3)]
        Gt = [sbt(sS, f"sG{i}", [128, 1024], F32) for i in range(2)]
        Wt = [sbt(sS, f"sW{i}", [128, 1024], F32) for i in range(2)]
        atto = sbt(sS, "satto", [128, 4, 128], BF16)

        T.dma('sp', smask[:], smask_d[:, :], W=["smask"])
        T.dma('sp', ptb[:], pt.partition_broadcast(128), W=["ptb"])
        T.op('pool', lambda e: e.iota(iot[:], pattern=[[0, 1]], base=0, channel_multiplier=1), W=["iot"])
        T.op('dve', lambda e: e.tensor_copy(out=iotf[:], in_=iot[:]), R=["iot"], W=["iotf"])
        T.op('dve', lambda e: e.tensor_copy(out=ptf[:], in_=ptb[:]), R=["ptb"], W=["ptf"])
        T.op('dve', lambda e: e.tensor_scalar(out=ptf[:], in0=ptf[:], scalar1=128.0, scalar2=iotf[:, 0:1], op0=ALU.mult,
                                              op1=ALU.add), R=["ptf", "iotf"], W=["ptf"])
        T.op('dve', lambda e: e.tensor_copy(out=idxall[:], in_=ptf[:]), R=["ptf"], W=["idxall"])

        prep.run(xs[:, :], hnTs[:, :, :], ["hnTs"])
        pq, pqn = PS.half()
        pk, pkn = PS.half()

        def projT(ps_, c0):
            def f(e):
                for t in range(4):
                    ins = mm_acc(e, ps_[:, t * 128:(t + 1) * 128],
                                 [(wqkv[:, dc, c0 + t * 128:c0 + (t + 1) * 128], hnTs[:, dc, :]) for dc in range(8)])
                return ins
            return f
        T.op('pe', projT(pq, 0), R=["wqkv", "hnTs"], W=[pqn])
        T.op('act', lambda e: e.activation(out=qTs[:].rearrange("p t n -> p (t n)"), in_=pq[:, :], func=AF.Copy, scale=0.125), R=[pqn], W=["qTs"])
        T.op('pe', projT(pk, 512), R=["wqkv", "hnTs"], W=[pkn])
        T.op('dve', lambda e: e.tensor_copy(out=kTn[:].rearrange("p t n -> p (t n)"), in_=pk[:, :]), R=[pkn], W=["kTn"])
        pk2, pk2n = PS.half()
        T.op('pe', lambda e: mm_acc(e, pk2[:, :], [(hnTs[:, dc, :], wqkv[:, dc, 512:1024]) for dc in range(8)]),
             R=["wqkv", "hnTs"], W=[pk2n])
        T.op('dve', lambda e: e.tensor_copy(out=ktm[:], in_=pk2[:, :]), R=[pk2n], W=["ktm"])
        T.dma('sp', k_o[SIDX], ktm[:], R=["ktm"])
        pv2, pv2n = PS.half()
        T.op('pe', lambda e: mm_acc(e, pv2[:, :], [(hnTs[:, dc, :], wqkv[:, dc, 1024:1536]) for dc in range(8)]),
             R=["wqkv", "hnTs"], W=[pv2n])
        T.op('dve', lambda e: e.tensor_copy(out=vtm[:], in_=pv2[:, :]), R=[pv2n], W=["vtm"])
        T.dma('sp', v_o[SIDX], vtm[:], R=["vtm"])
        T.barrier()

        def gat_k(r):
            i = NPG - r
            for s in range(16):
                T.gather(kpg[:, s, :], ck[:, :], idxall[:, s * NPG + i:s * NPG + i + 1], R=["idxall"], W=[f"kpg{s}"])

        def gat_v(r):
            i = NPG - r
            for s in range(16):
                T.gather(vpg[:, s, :], cv[:, :], idxall[:, s * NPG + i:s * NPG + i + 1], R=["idxall"], W=[f"vpg{s}"])

        def zgen(r, zt, zn):
            if r >= 1:
                gat_k(r)

            def fb(e):
                for hp in range(2):
                    ins = e.matmul(zt[:, hp * 512:(hp + 1) * 512], lhsT=ones2[:, :], rhs=brow_p[:, hp * 512:(hp + 1) * 512],
                                   start=True, stop=True)
                return ins
            T.op('pe', fb, R=["ones2", brow_p.name], W=zn)
            if r == 0:
                def f(e):
                    for h in range(8):
                        t, hp = h // 2, h % 2
                        out = zt[:, hp * 512 + t * 128:hp * 512 + (t + 1) * 128]
                        ins = e.matmul(out, lhsT=kTn[hp * 64:(hp + 1) * 64, t, :],
                                       rhs=qTs[hp * 64:(hp + 1) * 64, t, :],
                                       start=False, stop=True, skip_group_check=True)
                    return ins
                T.op('pe', f, R=["kTn", "qTs"], W=zn)
            else:
                for s in range(16):
                    kt = ktmp[s % 2]
                    ktn = f"ktmp{s % 2}"

                    def ftr(e):
                        for t in range(4):
                            ins = e.transpose(sp7[:, t * 128:(t + 1) * 128], kpg[:, s, t * 128:(t + 1) * 128], identf[:, :])
                        return ins
                    T.op('pe', ftr, R=[f"kpg{s}", "identf"], W=["ps7"])
                    T.op('dve', lambda e: e.tensor_copy(out=kt[:].rearrange("p t n -> p (t n)"), in_=sp7[:, :]),
                         R=["ps7"], W=[ktn])

                    def fq(e):
                        for h in range(8):
                            t, hp = h // 2, h % 2
                            c = hp * 512 + t * 128 + s * 8
                            ins = e.matmul(zt[:, c:c + 8], lhsT=kt[hp * 64:(hp + 1) * 64, t, :],
                                           rhs=qTs[hp * 64:(hp + 1) * 64, t, s * 8:(s + 1) * 8],
                                           start=False, stop=True, skip_group_check=True)
                        return ins
                    T.op('pe', fq, R=[ktn, "qTs"], W=zn)

        def maskgen(r, E, en):
            if r == 0:
                T.op('dve', lambda e: e.tensor_tensor(out=E[:], in0=E[:], in1=smask[:], op=ALU.mult), R=[en, "smask"], W=[en])

        def wvgen(r, W, wn):
            if r == 0:
                def f(e):
                    for h in range(8):
                        t, hp = h // 2, h % 2
                        out = Ops[hp * 64:(hp + 1) * 64, t * 128:(t + 1) * 128]
                        rhs = W[:, hp * 512 + t * 128:hp * 512 + (t + 1) * 128]
                        ins = e.matmul(out, lhsT=vtm[:, h * 64:(h + 1) * 64], rhs=rhs, start=False, stop=True,
                                       skip_group_check=True)
                    return ins
                T.op('pe', f, R=["vtm", wn], W=[ON])
            else:
                for s in range(16):
                    def f(e):
                        for h in range(8):
                            t, hp = h // 2, h % 2
                            c = hp * 512 + t * 128 + s * 8
                            ins = e.matmul(Ops[hp * 64:(hp + 1) * 64, t * 128 + s * 8:t * 128 + (s + 1) * 8],
                                           lhsT=vpg[:, s, h * 64:(h + 1) * 64], rhs=W[:, c:c + 8], start=False, stop=True,
                                           skip_group_check=True)
                        return ins
                    T.op('pe', f, R=[f"vpg{s}", wn], W=[ON])

        def after_b2(r):
            if r + 1 <= NPG:
                gat_v(r + 1)

        SS = DBG.get('sstop', 9)
        if SS >= 1:
            attn_pipeline(NPG + 1 if SS >= 2 else 1, zgen, maskgen, wvgen, Et, St, Gt, Wt, "s", after_b2 if SS >= 2 else None)
        T.op('dve', lambda e: e.tensor_copy(out=atto[:].rearrange("p t n -> p (t n)"), in_=Ops), R=[ON], W=["satto"])
        T.dma('pool', mixs[SIDX, :, 0:4, :], atto[:], R=["satto"], W=[f"mixa{SIDX}"])
        T.barrier()

    if "K" in phases:
      with contextlib.ExitStack() as sK:
        kT = sbt(sK, "kT", [128, 4, NT], BF16)
        Vr = sbt(sK, "Vr", [128, NB, 512], BF16)
        qT = sbt(sK, "qT", [128, NSB, 4, 128], BF16)
        with contextlib.ExitStack() as sK2:
            wqkv = load_w(sK2, "wqkv2", w_in, 0, 1536, 32)
            prep = Prep(sK2, "pk", nx=1)
            hnT = [sbt(sK2, f"k_hnT{i}", [128, 8, 512], BF16) for i in range(1)]
            kst = [sbt(sK2, f"kst{i}", [128, 512], F32) for i in range(1)]
            vst = [sbt(sK2, f"vst{i}", [128, 512], F32) for i in range(1)]
            KS = DBG.get('kstop', 9)
            for sb in range(NSB):
                if KS < 1:
                    break
                hb = hnT[0]
                hname = "k_hnT0"
                for blk in range(4):
                    L = sb * 4 + blk
                    prep.run(xloc[L * 128:(L + 1) * 128, :], hb[:, :, blk * 128:(blk + 1) * 128], [hname])
                for t in range(4):
                    if KS < 2:
                        break
                    pk, pkn = PS.half()
                    T.op('pe', lambda e: mm_acc(e, pk[:, :], [(wqkv[:, dc, 512 + t * 128:512 + (t + 1) * 128], hb[:, dc, :])
                                                               for dc in range(8)]), R=["wqkv2", hname], W=[pkn])
                    eng = 'act' if t % 2 else 'dve'
                    if eng == 'act':
                        T.op('act', lambda e: e.copy(out=kT[:, t, sb * 512:(sb + 1) * 512], in_=pk[:, :]), R=[pkn], W=[f"kT{sb}"])
                    else:
                        T.op('dve', lambda e: e.tensor_copy(out=kT[:, t, sb * 512:(sb + 1) * 512], in_=pk[:, :]),
                             R=[pkn], W=[f"kT{sb}"])
                for blk in range(4):
                    if KS < 3:
                        break
                    L = sb * 4 + blk
                    pv, pvn = PS.half()
                    T.op('pe', lambda e: mm_acc(e, pv[:, :], [(hb[:, dc, blk * 128:(blk + 1) * 128], wqkv[:, dc, 1024:1536])
                                                               for dc in range(8)]), R=["wqkv2", hname], W=[pvn])
                    if DBG.get('vevac', 1):
                        if blk == 1 and DBG.get('vact', 1):
                            T.op('act', lambda e: e.copy(out=Vr[:, L, :], in_=pv[:, :]), R=[pvn], W=[f"Vr{L}"])
                        else:
                            T.op('dve', lambda e: e.tensor_copy(out=Vr[:, L, :], in_=pv[:, :]), R=[pvn], W=[f"Vr{L}"])
                    if blk == 3 and DBG.get('vout', 1):
                        vs_ = vst[0]
                        T.op('dve', lambda e: e.tensor_copy(out=vs_[:], in_=pv[:, :]), R=[pvn], W=["vst0"])
                        T.dma('pool', v_o[sb], vs_[:], R=["vst0"])
                if KS < 4:
                    continue
                pk2, pk2n = PS.half()
                T.op('pe', lambda e: mm_acc(e, pk2[:, :], [(hb[:, dc, 384:512], wqkv[:, dc, 512:1024]) for dc in range(8)]),
                     R=["wqkv2", hname], W=[pk2n])
                ks_ = kst[0]
                T.op('act', lambda e: e.copy(out=ks_[:], in_=pk2[:, :]), R=[pk2n], W=["kst0"])
                T.dma('pool', k_o[sb], ks_[:], R=["kst0"])
                if KS < 5:
                    continue
                pq, pqn = PS.half()

                def fq(e):
                    for t in range(4):
                        ins = mm_acc(e, pq[:, t * 128:(t + 1) * 128],
                                     [(wqkv[:, dc, t * 128:(t + 1) * 128], hb[:, dc, 384:512]) for dc in range(8)])
                    return ins
                T.op('pe', fq, R=["wqkv2", hname], W=[pqn])
                T.op('act', lambda e: e.activation(out=qT[:, sb, :, :].rearrange("p t n -> p (t n)"), in_=pq[:, :], func=AF.Copy, scale=0.125), R=[pqn], W=["qT"])
            T.barrier()

        if "A" in phases:
          with contextlib.ExitStack() as sA:
            Et = [sbt(sA, f"aE{i}", [128, 1024], BF16) for i in range(3)]
            St = [sbt(sA, f"aS{i}", [128, 1024], BF16) for i in range(3)]
            Gt = [sbt(sA, f"aG{i}", [128, 1024], BF16) for i in range(2)]
            Wt = [sbt(sA, f"aW{i}", [128, 1024], BF16) for i in range(2)]
            atto = [sbt(sA, f"aatto{i}", [128, 4, 128], BF16) for i in range(2)]
            for m in range(NSB):
                L = 4 * m + 3

                def zgen(r, zt, zn):
                    Lk = L - r

                    def f(e):
                        for hp in range(2):
                            e.matmul(zt[:, hp * 512:(hp + 1) * 512], lhsT=ones2[:, :], rhs=brow_p[:, hp * 512:(hp + 1) * 512],
                                     start=True, stop=True)
                        for h in range(8):
                            t, hp = h // 2, h % 2
                            ins = e.matmul(zt[:, hp * 512 + t * 128:hp * 512 + (t + 1) * 128],
                                           lhsT=kT[hp * 64:(hp + 1) * 64, t, Lk * 128:(Lk + 1) * 128],
                                           rhs=qT[hp * 64:(hp + 1) * 64, m, t, :], start=False, stop=True, skip_group_check=True)
                        return ins
                    T.op('pe', f, R=["ones2", brow_p.name, "qT", f"kT{Lk // 4}"], W=zn)

                def maskgen(r, E, en):
                    Lk = L - r
                    if r == 0:
                        T.op('dve', lambda e: e.tensor_tensor(out=E[:].rearrange("p (h q) -> p h q", h=8),
                                                               in0=E[:].rearrange("p (h q) -> p h q", h=8),
                                                               in1=otri[:, None, :].to_broadcast([128, 8, 128]), op=ALU.mult),
                             R=[en, "otri"], W=[en])
                    if Lk < 3:
                        T.op('dve', lambda e: e.tensor_scalar(out=E[:], in0=E[:], scalar1=vflag[:, Lk:Lk + 1], scalar2=None,
                                                               op0=ALU.mult), R=[en, "vflag"], W=[en])

                def wvgen(r, W, wn):
                    Lk = L - r

                    def f(e):
                        for h in range(8):
                            t, hp = h // 2, h % 2
                            ins = e.matmul(Ops[hp * 64:(hp + 1) * 64, t * 128:(t + 1) * 128],
                                           lhsT=Vr[:, Lk, h * 64:(h + 1) * 64],
                                           rhs=W[:, hp * 512 + t * 128:hp * 512 + (t + 1) * 128], start=False, stop=True,
                                           skip_group_check=True)
                        return ins
                    T.op('pe', f, R=[f"Vr{Lk}", wn], W=[ON])

                attn_pipeline(L + 1, zgen, maskgen, wvgen, Et, St, Gt, Wt, "a")
                ao = atto[m % 2]
                T.op('dve', lambda e: e.tensor_copy(out=ao[:].rearrange("p t n -> p (t n)"), in_=Ops), R=[ON], W=[f"aatto{m % 2}"])
                T.dma('pool', mixs[m, :, 0:4, :], ao[:], R=[f"aatto{m % 2}"], W=[f"mixa{m}"])
            T.barrier()

    if "M" in phases:
      with contextlib.ExitStack() as sM:
        with contextlib.ExitStack() as sM1:
            gpb = sbt(sM1, "gpb", [128, D], F32)
            T.dma('sp', gpb[:, :], gpost_d[0:1, :].partition_broadcast(128), W=["gpb"])
            wo = load_w(sM1, "wo", w_out, 0, D, None)
            prep = Prep(sM1, "pm")
            mixT = [sbt(sM1, f"mixT{i}", [128, 8, 128], BF16) for i in range(2)]
            xr = [sbt(sM1, f"xr{i}", [128, D], F32) for i in range(2)]
            x1b = [sbt(sM1, f"x1b{i}", [128, D], F32) for i in range(2)]
            for o in range(NOWN):
                sl = o % 2
                T.dma('sp', mixT[sl][:], mixs[o], W=[f"mixT{sl}"])
                rows = xs[:, :] if o == SIDX else xloc[(4 * o + 3) * 128:(4 * o + 4) * 128, :]
                T.dma('sp', xr[sl][:], rows, W=[f"xr{sl}"])
                pm, pmn = PS.full()

                def f(e):
                    for hf in range(2):
                        ins = mm_acc(e, pm[:, hf * 512:(hf + 1) * 512],
                                     [(mixT[sl][:, fc, :], wo[:, fc, hf * 512:(hf + 1) * 512]) for fc in range(8)])
                    return ins
                T.op('pe', f, R=["wo", f"mixT{sl}"], W=pmn)
                stat, sname = prep.rstd(pm[:, :], pmn)
                T.op('dve', lambda e: e.scalar_tensor_tensor(out=x1b[sl][:], in0=pm[:, :], scalar=stat[:, 2:3], in1=gpb[:, :],
                                                             op0=ALU.mult, op1=ALU.mult), R=pmn + [sname, "gpb"], W=[f"x1b{sl}"])
                T.op('pool', lambda e: e.tensor_tensor(out=x1b[sl][:], in0=x1b[sl][:], in1=xr[sl][:], op=ALU.add),
                     R=[f"x1b{sl}", f"xr{sl}"], W=[f"x1b{sl}"])
                T.dma('pool', x1s[o], x1b[sl][:], R=[f"x1b{sl}"], W=[f"x1s{o}"])
            T.barrier()

        with contextlib.ExitStack() as sM2:
            wu = load_w(sM2, "wu", w_up, 0, 4096, 40)
            wd = load_w(sM2, "wd", w_down, 0, D, None)
            gpb = sbt(sM2, "gpb2", [128, D], F32)
            T.dma('sp', gpb[:, :], gpost_d[1:2, :].partition_broadcast(128), W=["gpb"])
            prep = Prep(sM2, "pn", nx=1)
            GB = 4
            x1 = [sbt(sM2, f"x1_{i}", [128, D], F32) for i in range(GB)]
            hn2T = sbt(sM2, "hn2T", [128, 8, GB * 128], BF16)
            u2T = sbt(sM2, "u2T", [128, 32, GB * 128], BF16)
            sqv = prep.junk[:, :].bitcast(F32)
            for g0 in range(0, NOWN, GB):
                nb = min(GB, NOWN - g0)
                ntok = nb * 128
                for i in range(nb):
                    o = g0 + i
                    T.dma('sp', x1[i][:], x1s[o], R=[f"x1s{o}"], W=[f"x1_{i}"])
                    prep.run(None, hn2T[:, :, i * 128:(i + 1) * 128], ["hn2T"], x_sb=x1[i], x_names=[f"x1_{i}"])
                for fc in range(32):
                    pu, pun = PS.half()
                    T.op('pe', lambda e: mm_acc(e, pu[:, 0:ntok], [(wu[:, dc, fc * 128:(fc + 1) * 128], hn2T[:, dc, 0:ntok])
                                                                    for dc in range(8)]), R=["wu", "hn2T"], W=[pun])
                    T.op('act', lambda e: e.activation(out=sqv[:, 0:ntok], in_=pu[:, 0:ntok], func=AF.Square),
                         R=[pun], W=["pn_junk"])
                    T.op('dve', lambda e: e.scalar_tensor_tensor(out=u2T[:, fc, 0:ntok], in0=pu[:, 0:ntok], scalar=0.0,
                                                                 in1=sqv[:, 0:ntok], op0=ALU.is_gt, op1=ALU.mult),
                         R=[pun, "pn_junk"], W=["u2T"])
                for i in range(nb):
                    o = g0 + i
                    pd, pdn = PS.full()

                    def f(e):
                        for hf in range(2):
                            ins = mm_acc(e, pd[:, hf * 512:(hf + 1) * 512],
                                         [(u2T[:, fc, i * 128:(i + 1) * 128], wd[:, fc, hf * 512:(hf + 1) * 512])
                                          for fc in range(32)])
                        return ins
                    T.op('pe', f, R=["wd", "u2T"], W=pdn)
                    stat, sname = prep.rstd(pd[:, :], pdn)
                    T.op('dve', lambda e: e.scalar_tensor_tensor(out=prep.xst[0][:], in0=pd[:, :], scalar=stat[:, 2:3],
                                                                 in1=gpb[:, :], op0=ALU.mult, op1=ALU.mult),
                         R=pdn + [sname, "gpb"], W=["pn_xst0"])
                    T.op('pool', lambda e: e.tensor_tensor(out=x1[i][:], in0=prep.xst[0][:], in1=x1[i][:],
                                                           op=ALU.add), R=["pn_xst0", f"x1_{i}"], W=[f"x1_{i}"])
                    T.dma('pool', y_o[o], x1[i][:], R=[f"x1_{i}"])
            T.barrier()

    T.finish()
    es.close()
    return nc


def _host_inputs(inp, NSB, NPG, NPOOL):
    f32 = np.float32
    xp = np.asarray(inp["x_prompt"], f32)
    xsm = np.asarray(inp["x_sample"], f32)
    meta = np.asarray(inp["meta_tokens"], f32)
    B, SEQ, _ = xp.shape
    TR = SEQ + meta.shape[0]
    NB = 4 * NSB
    assert TR <= (NB - 3) * 128
    ck = np.ascontiguousarray(np.asarray(inp["cache_k"], f32)[0].reshape(NPOOL * 128, 512))
    cv = np.ascontiguousarray(np.asarray(inp["cache_v"], f32)[0].reshape(NPOOL * 128, 512))
    ptab = np.asarray(inp["page_table"], np.int32)
    params = np.zeros((128, NPAR), f32)

    def chan(v):
        return np.asarray(v, f32).reshape(4, 128).T
    cw = np.asarray(inp["conv_w"], f32)[0]
    for j in range(4):
        params[:, j:16:4] = chan(cw[j])
    params[:, 16:20] = chan(inp["conv_b"][0])
    params[:, 20:24] = chan(inp["b_gate_a"][0])
    params[:, 24:28] = chan(inp["b_gate_x"][0])
    params[:, 28:32] = chan(inp["lru_lambda"][0])
    params[:, 32:40] = np.asarray(inp["g_mix_pre"], f32)[0].reshape(8, 128).T
    params[:, 40:48] = np.asarray(inp["g_mlp_pre"], f32)[0].reshape(8, 128).T
    gpost = np.stack([np.asarray(inp["g_mix_post"], f32)[0], np.asarray(inp["g_mlp_post"], f32)[0]], 0)
    k_s = np.arange(128) // 8
    k_t = np.arange(128) % 8
    col = np.arange(1024)
    c_s = (col % 128) // 8
    c_q = col % 8
    smask = ((k_s[:, None] == c_s[None, :]) & (k_t[:, None] < c_q[None, :])).astype(f32)
    shared = dict(ck=ck, cv=cv, params=params, gpost=np.ascontiguousarray(gpost),
                  sbias=np.asarray(inp["sb_bias"], f32).reshape(1, 8), smask=smask,
                  w_in=np.ascontiguousarray(np.asarray(inp["w_in"], f32)[0]),
                  w_out=np.ascontiguousarray(np.asarray(inp["w_out"], f32)[0]),
                  w_up=np.ascontiguousarray(np.asarray(inp["w_up"], f32)[0]),
                  w_down=np.ascontiguousarray(np.asarray(inp["w_down"], f32)[0]),
                  wga=np.ascontiguousarray(np.asarray(inp["w_gate_a"], f32)[0]),
                  wgx=np.ascontiguousarray(np.asarray(inp["w_gate_x"], f32)[0]))
    maps = []
    for c in range(8):
        b, j = c // 4, c % 4
        xloc = np.zeros((NB * 128, D), f32)
        o = (3 - j) * 128
        xloc[o:o + meta.shape[0]] = meta
        xloc[o + meta.shape[0]:o + TR] = xp[b]
        vflag = np.ones((128, 4), f32)
        for L in range(3):
            if L < 3 - j:
                vflag[:, L] = 0.0
        m = dict(shared)
        m.update(xloc=xloc, xs=np.ascontiguousarray(xsm[16 * c:16 * c + 16].reshape(128, D)),
                 pt=np.ascontiguousarray(ptab[16 * c:16 * c + 16].reshape(1, 16 * NPG)),
                 sth=np.ascontiguousarray(np.asarray(inp["state_h"], f32)[0, 16 * c:16 * c + 16]),
                 stc=np.ascontiguousarray(np.asarray(inp["state_conv"], f32)[0, 16 * c:16 * c + 16].reshape(48, 512)),
                 vflag=vflag)
        maps.append(m)
    return maps


def _host_outputs(res, inp, NSB):
    f32 = np.float32
    B, SEQ, _ = inp["x_prompt"].shape
    NM = inp["meta_tokens"].shape[0]
    TR = SEQ + NM
    yp = np.zeros((B, TR, D), f32)
    kp = np.zeros((B, TR, 512), f32)
    vp = np.zeros((B, TR, 512), f32)
    hp = np.zeros((1, B, 512), f32)
    cp = np.zeros((1, B, 3, 512), f32)
    ys = np.zeros((128, 8, D), f32)
    ks = np.zeros((1, 128, 8, 8, 64), f32)
    vs = np.zeros((1, 128, 8, 8, 64), f32)
    hs = np.zeros((1, 128, 512), f32)
    cs = np.zeros((1, 128, 3, 512), f32)
    glast = (TR - 1) // 128
    for c in range(8):
        b, j = c // 4, c % 4
        r = res[c]
        for m in range(NSB):
            g = 4 * m + j
            lo = g * 128
            if lo >= TR:
                continue
            n = min(128, TR - lo)
            yp[b, lo:lo + n] = r["y"][m, :n]
            kp[b, lo:lo + n] = r["ko"][m, :n]
            vp[b, lo:lo + n] = r["vo"][m, :n]
        if j == glast % 4:
            hp[0, b] = r["hp"].T.reshape(512)
            cp[0, b] = r["cp"]
        ys[16 * c:16 * c + 16] = r["y"][NSB].reshape(16, 8, D)
        ks[0, 16 * c:16 * c + 16] = r["ko"][NSB].reshape(16, 8, 8, 64)
        vs[0, 16 * c:16 * c + 16] = r["vo"][NSB].reshape(16, 8, 8, 64)
        hs[0, 16 * c:16 * c + 16] = r["hs"]
        cs[0, 16 * c:16 * c + 16] = r["cs"]
    return (yp[:, NM:], ys, kp.reshape(1, B, TR, 8, 64), vp.reshape(1, B, TR, 8, 64), hp, cp, ks, vs, hs, cs)


_CACHE = {}


def kernel(**inputs):
    SEQ = inputs["x_prompt"].shape[1]
    NM = inputs["meta_tokens"].shape[0]
    nblk = -(-(SEQ + NM) // 128)
    NSB = -(-(nblk + 3) // 4)
    NPG = inputs["page_table"].shape[1]
    NPOOL = inputs["cache_k"].shape[1]
    key = (NSB, NPG, NPOOL)
    if key not in _CACHE:
        _CACHE[key] = build(NSB, NPG, NPOOL)
    nc = _CACHE[key]
    maps = _host_inputs(inputs, NSB, NPG, NPOOL)
    res = run_bass_kernel_spmd(nc, maps, core_ids=list(range(8)))
    return _host_outputs(res.results, inputs, NSB)
```

```python
import contextlib
import numpy as np
import concourse.bass as bass
import concourse.mybir as mybir
from concourse.bass_utils import run_bass_kernel_spmd

F32 = mybir.dt.float32
BF16 = mybir.dt.bfloat16
I32 = mybir.dt.int32
AF = mybir.ActivationFunctionType
ALU = mybir.AluOpType

D = 1024
DBG = {}
NPAR = 48


class Trk:
    ND = 8

    def __init__(self, nc, es):
        self.nc = nc
        self.eng = {'pe': nc.tensor, 'act': nc.scalar, 'dve': nc.vector, 'pool': nc.gpsimd, 'sp': nc.sync}
        self.sem = {}
        self.cnt = {}
        for k in ('pe', 'act', 'dve', 'pool'):
            self.sem[k] = es.enter_context(nc.semaphore("s_" + k))
            self.cnt[k] = 0
        self.ndq = {'sp': self.ND, 'pool': DBG.get('ndpool', 2)}
        for q in ('sp', 'pool'):
            for i in range(self.ndq[q]):
                k = ('d', q, i)
                self.sem[k] = es.enter_context(nc.semaphore(f"d_{q}{i}"))
                self.cnt[k] = 0
        self.dnext = {'sp': 0, 'pool': 0}
        self.seen = {e: {} for e in self.eng}
        self.lastw = {}
        self.reads = {}

    def _wait(self, eng, tok):
        key, val = tok
        if eng == 'pe' and key == 'pe':
            return
        if self.seen[eng].get(key, 0) >= val:
            return
        self.eng[eng].wait_ge(self.sem[key], val)
        self.seen[eng][key] = val

    def _deps(self, eng, R, W):
        for r in R:
            t = self.lastw.get(r)
            if t is not None:
                self._wait(eng, t)
        for w in W:
            t = self.lastw.get(w)
            if t is not None:
                self._wait(eng, t)
            for t in self.reads.get(w, ()):
                self._wait(eng, t)

    def _record(self, tok, R, W):
        for w in W:
            self.lastw[w] = tok
            self.reads[w] = []
        for r in R:
            if r in W:
                continue
            lst = self.reads.setdefault(r, [])
            lst.append(tok)
            if len(lst) > 8:
                best = {}
                for k, v in lst:
                    if best.get(k, 0) < v:
                        best[k] = v
                self.reads[r] = list(best.items())

    def op(self, eng, fn, R=(), W=()):
        self._deps(eng, R, W)
        ins = fn(self.eng[eng])
        self.cnt[eng] += 1
        ins.then_inc(self.sem[eng], 1)
        tok = (eng, self.cnt[eng])
        self._record(tok, R, W)
        return tok

    def _dma_slot(self, q):
        i = self.dnext[q]
        self.dnext[q] = (i + 1) % self.ndq[q]
        k = ('d', q, i)
        if self.cnt[k] > 0:
            self._wait(q, (k, 16 * self.cnt[k]))
        return k

    def dma(self, q, out, in_, R=(), W=()):
        self._deps(q, R, W)
        k = self._dma_slot(q)
        ins = self.eng[q].dma_start(out=out, in_=in_)
        self.cnt[k] += 1
        ins.then_inc(self.sem[k], 16)
        tok = (k, 16 * self.cnt[k])
        self._record(tok, R, W)
        return tok

    def gather(self, out, in_, idx_ap, R=(), W=()):
        q = 'pool'
        self._deps(q, R, W)
        k = self._dma_slot(q)
        ins = self.nc.gpsimd.indirect_dma_start(
            out=out, out_offset=None, in_=in_,
            in_offset=bass.IndirectOffsetOnAxis(ap=idx_ap, axis=0))
        self.cnt[k] += 1
        ins.then_inc(self.sem[k], 16)
        tok = (k, 16 * self.cnt[k])
        self._record(tok, R, W)
        return tok

    def _all(self, eng):
        for k, c in self.cnt.items():
            if c == 0:
                continue
            v = c if isinstance(k, str) else 16 * c
            self._wait(eng, (k, v))

    def barrier(self):
        for e in ('pe', 'act', 'dve', 'pool', 'sp'):
            self._all(e)
        self.lastw.clear()
        self.reads.clear()

    def finish(self):
        self._all('sp')


def build(NSB, NPG, NPOOL, phases="LSKAM"):
    NB = 4 * NSB
    NT = NB * 128
    NOWN = NSB + 1
    SIDX = NSB
    nc = bass.Bass("TRN2", target_bir_lowering=False)
    es = contextlib.ExitStack()

    def din(name, shape, dt=F32):
        return nc.dram_tensor(name, shape, dt, kind="ExternalInput").ap()

    def dout(name, shape, dt=F32):
        return nc.dram_tensor(name, shape, dt, kind="ExternalOutput").ap()

    xloc = din("xloc", [NT, D])
    xs = din("xs", [128, D])
    ck = din("ck", [NPOOL * 128, 512])
    cv = din("cv", [NPOOL * 128, 512])
    pt = din("pt", [1, 16 * NPG], I32)
    sth = din("sth", [16, 512])
    stc = din("stc", [48, 512])
    vflag_d = din("vflag", [128, 4])
    params_d = din("params", [128, NPAR])
    gpost_d = din("gpost", [2, D])
    sbias_d = din("sbias", [1, 8])
    smask_d = din("smask", [128, 1024])
    w_in = din("w_in", [D, 2560])
    w_out = din("w_out", [D, D])
    w_up = din("w_up", [D, 4096])
    w_down = din("w_down", [4096, D])
    wga = din("wga", [8, 64, 64])
    wgx = din("wgx", [8, 64, 64])

    y_o = dout("y", [NOWN, 128, D])
    k_o = dout("ko", [NOWN, 128, 512])
    v_o = dout("vo", [NOWN, 128, 512])
    hp_o = dout("hp", [128, 4])
    cp_o = dout("cp", [3, 512])
    hs_o = dout("hs", [16, 512])
    cs_o = dout("cs", [16, 3, 512])

    mixs = nc.dram_tensor("mixs", [NOWN, 128, 8, 128], BF16).ap()
    x1s = nc.dram_tensor("x1s", [NOWN, 128, D], F32).ap()

    T = Trk(nc, es)

    def sbt(stack, name, shape, dt):
        return stack.enter_context(nc.sbuf_tensor("t_" + name, shape, dt))

    psf = [es.enter_context(nc.psum_tensor(f"ps{i}", [128, 1024], F32)) for i in range(4)]

    class PS:
        n = 0

        @staticmethod
        def half():
            i = PS.n % 8
            PS.n += 1
            return psf[i // 2][:, (i % 2) * 512:(i % 2) * 512 + 512], f"ps{i}"

        @staticmethod
        def full():
            if PS.n % 2:
                PS.n += 1
            i = PS.n % 8
            PS.n += 2
            return psf[i // 2], [f"ps{i}", f"ps{i + 1}"]

    identf = sbt(es, "identf", [128, 128], F32)
    identb = sbt(es, "identb", [128, 128], BF16)
    tri = sbt(es, "tri", [128, 128], BF16)
    otri = sbt(es, "otri", [128, 128], BF16)
    otrif = sbt(es, "otrif", [128, 128], F32)
    onesf = sbt(es, "onesf", [128, 128], F32)
    params = sbt(es, "params", [128, NPAR], F32)
    vflag = sbt(es, "vflagt", [128, 4], F32)
    lruc = sbt(es, "lruc", [128, 16], F32)
    ones2 = sbt(es, "ones2", [2, 128], BF16)
    brow_p = sbt(es, "brow_p", [2, 1024], BF16)
    brow_s = sbt(es, "brow_s", [2, 1024], BF16)
    zl = sbt(es, "zl", [1, 128], BF16)
    zrow = sbt(es, "zrow", [1, 512], BF16)
    bd = sbt(es, "bd", [128, 8, 128], BF16)

    T.op('pool', lambda e: e.memset(onesf[:], 1.0), W=["onesf"])
    T.op('pool', lambda e: e.affine_select(out=identf[:], in_=onesf[:], pattern=[[1, 128]], compare_op=ALU.is_equal,
                                           fill=0.0, base=0, channel_multiplier=-1), R=["onesf"], W=["identf"])
    T.op('pool', lambda e: e.tensor_copy(out=identb[:], in_=identf[:]), R=["identf"], W=["identb"])
    T.op('pool', lambda e: e.affine_select(out=tri[:], in_=onesf[:], pattern=[[-1, 128]], compare_op=ALU.is_ge,
                                           fill=0.0, base=0, channel_multiplier=1), R=["onesf"], W=["tri"])
    T.op('pool', lambda e: e.affine_select(out=otrif[:], in_=onesf[:], pattern=[[1, 128]], compare_op=ALU.is_gt,
                                           fill=0.0, base=0, channel_multiplier=-1), R=["onesf"], W=["otrif"])
    T.op('pool', lambda e: e.tensor_copy(out=otri[:], in_=otrif[:]), R=["otrif"], W=["otri"])
    T.op('pool', lambda e: e.memset(ones2[:], 1.0), W=["ones2"])
    T.op('pool', lambda e: e.memset(zl[:], 0.0), W=["zl"])
    T.op('pool', lambda e: e.memset(zrow[:], 0.0), W=["zrow"])
    T.dma('sp', params[:], params_d[:, :], W=["params"])
    T.dma('sp', vflag[:], vflag_d[:, :], W=["vflag"])

    with contextlib.ExitStack() as s0:
        bsrc = sbt(s0, "bsrc", [2, 8], F32)
        bst = sbt(s0, "bst", [2, 1024], F32)
        bst2 = sbt(s0, "bst2", [2, 1024], F32)
        lo_t = sbt(s0, "lo_t", [2, 1024], BF16)
        T.dma('sp', bsrc[0:1, :], sbias_d[:, :], W=["bsrc"])
        T.dma('sp', bsrc[1:2, :], sbias_d[:, :], W=["bsrc"])
        for dst, order in ((brow_p, "p"), (brow_s, "s")):
            dn = dst.name
            for hp in range(2):
                srcv = bsrc[:, :].rearrange("p (t hp) -> p hp t", hp=2)[:, hp, :]
                if order == "p":
                    T.op('dve', lambda e, hp=hp, srcv=srcv: e.tensor_copy(
                        out=bst[:, hp * 512:(hp + 1) * 512].rearrange("p (t q) -> p t q", t=4),
                        in_=srcv[:, :, None].to_broadcast([2, 4, 128])), R=["bsrc"], W=["bst"])
                else:
                    T.op('dve', lambda e, hp=hp, srcv=srcv: e.tensor_copy(
                        out=bst[:, hp * 512:(hp + 1) * 512].rearrange("p (s t q) -> p s t q", s=16, t=4),
                        in_=srcv[:, None, :, None].to_broadcast([2, 16, 4, 8])), R=["bsrc"], W=["bst"])
            T.op('dve', lambda e: e.tensor_copy(out=dst[:], in_=bst[:]), R=["bst"], W=[dn])
            T.op('dve', lambda e: e.tensor_copy(out=bst2[:], in_=dst[:]), R=[dn], W=["bst2"])
            T.op('dve', lambda e: e.tensor_tensor(out=bst2[:], in0=bst[:], in1=bst2[:], op=ALU.subtract),
                 R=["bst", "bst2"], W=["bst2"])
            T.op('dve', lambda e: e.tensor_copy(out=lo_t[:], in_=bst2[:]), R=["bst2"], W=["lo_t"])
            T.dma('sp', dst[1:2, :], lo_t[1:2, :], R=["lo_t"], W=[dn])

        T.op('act', lambda e: e.activation(out=lruc[:, 0:4], in_=params[:, 28:32], func=AF.Exp, scale=-1.0),
             R=["params"], W=["lruc"])
        T.op('act', lambda e: e.activation(out=lruc[:, 0:4], in_=lruc[:, 0:4], func=AF.Ln, bias=1.0, scale=1.0),
             R=["lruc"], W=["lruc"])
        T.op('dve', lambda e: e.tensor_scalar(out=lruc[:, 4:8], in0=lruc[:, 0:4], scalar1=-16.0, scalar2=None, op0=ALU.mult),
             R=["lruc"], W=["lruc"])
        T.op('dve', lambda e: e.tensor_scalar(out=lruc[:, 0:4], in0=lruc[:, 0:4], scalar1=-8.0, scalar2=None, op0=ALU.mult),
             R=["lruc"], W=["lruc"])
        T.op('dve', lambda e: e.tensor_scalar(out=lruc[:, 8:16], in0=params[:, 20:28], scalar1=-1.0, scalar2=None,
                                              op0=ALU.mult), R=["params", "lruc"], W=["lruc"])
        bdst = sbt(s0, "bdst", [128, 8, 128], F32)
        T.op('dve', lambda e: e.memset(bdst[:], 0.0), W=["bdst"])
        for gi, wsrc in enumerate((wga, wgx)):
            for t in range(4):
                for hb in range(2):
                    T.dma('sp', bdst[hb * 64:(hb + 1) * 64, gi * 4 + t, hb * 64:(hb + 1) * 64], wsrc[2 * t + hb],
                          W=["bdst"])
        T.op('dve', lambda e: e.tensor_copy(out=bd[:], in_=bdst[:]), R=["bdst"], W=["bd"])
        T.barrier()

    def load_w(stack, name, src, c0, c1, gcol):
        ncol = c1 - c0
        nk = src.shape[0] // 128
        wt = sbt(stack, name, [128, nk, ncol], BF16)
        with contextlib.ExitStack() as st:
            stg = [sbt(st, f"{name}_stg{i}", [128, 1024], F32) for i in range(3)]
            k = 0
            for dc in range(nk):
                for cc in range(0, ncol, 1024):
                    w = min(1024, ncol - cc)
                    sl = k % 3
                    sn = f"{name}_stg{sl}"
                    T.dma('sp', stg[sl][:, 0:w], src[dc * 128:(dc + 1) * 128, c0 + cc:c0 + cc + w], W=[sn])
                    eng = ('pool', 'dve', 'act')[k % 3] if gcol is None else ('act', 'dve')[k % 2]
                    if gcol is None:
                        if eng == 'act':
                            T.op(eng, lambda e, sl=sl, dc=dc, cc=cc, w=w: e.copy(out=wt[:, dc, cc:cc + w], in_=stg[sl][:, 0:w]),
                                 R=[sn], W=[name])
                        else:
                            T.op(eng, lambda e, sl=sl, dc=dc, cc=cc, w=w: e.tensor_copy(out=wt[:, dc, cc:cc + w],
                                                                                          in_=stg[sl][:, 0:w]), R=[sn], W=[name])
                    else:
                        gc = gcol + (dc % 8)
                        if eng == 'act':
                            T.op(eng, lambda e, sl=sl, dc=dc, cc=cc, w=w, gc=gc: e.activation(
                                out=wt[:, dc, cc:cc + w], in_=stg[sl][:, 0:w], func=AF.Identity, scale=params[:, gc:gc + 1]),
                                R=[sn, "params"], W=[name])
                        else:
                            T.op(eng, lambda e, sl=sl, dc=dc, cc=cc, w=w, gc=gc: e.tensor_scalar(
                                out=wt[:, dc, cc:cc + w], in0=stg[sl][:, 0:w], scalar1=params[:, gc:gc + 1],
                                scalar2=None, op0=ALU.mult), R=[sn, "params"], W=[name])
                    k += 1
            T.barrier()
        return wt

    class Prep:
        def __init__(self, stack, tag, nx=2):
            self.tag = tag
            self.nx = nx
            self.xst = [sbt(stack, f"{tag}_xst{i}", [128, D], F32) for i in range(nx)]
            self.hn = [sbt(stack, f"{tag}_hn{i}", [128, D], BF16) for i in range(nx)]
            self.junk = sbt(stack, f"{tag}_junk", [128, D], BF16)
            self.stat = [sbt(stack, f"{tag}_stat{i}", [128, 4], F32) for i in range(4)]
            self.k = 0
            self.ks = 0

        def rstd(self, src_ap, src_names):
            si = self.ks % 4
            self.ks += 1
            stat = self.stat[si]
            sname = f"{self.tag}_stat{si}"
            jn = f"{self.tag}_junk"
            T.op('act', lambda e: e.activation(out=self.junk[:], in_=src_ap, func=AF.Square, accum_out=stat[:, 0:1]),
                 R=src_names, W=[jn, sname])
            T.op('act', lambda e: e.activation(out=stat[:, 1:2], in_=stat[:, 0:1], func=AF.Ln, scale=1.0 / D, bias=1e-6),
                 R=[sname], W=[sname])
            T.op('act', lambda e: e.activation(out=stat[:, 2:3], in_=stat[:, 1:2], func=AF.Exp, scale=-0.5),
                 R=[sname], W=[sname])
            return stat, sname

        def run(self, xrows, dstT, dst_names, x_sb=None, x_names=None):
            sl = self.k % max(self.nx, 1)
            self.k += 1
            tag = self.tag
            if x_sb is None:
                T.dma('sp', self.xst[sl][:], xrows, W=[f"{tag}_xst{sl}"])
                x_sb = self.xst[sl]
                x_names = [f"{tag}_xst{sl}"]
            stat, sname = self.rstd(x_sb[:], x_names)
            hnn = f"{tag}_hn{sl}"
            T.op('act', lambda e: e.activation(out=self.hn[sl][:], in_=x_sb[:], func=AF.Identity, scale=stat[:, 2:3]),
                 R=x_names + [sname], W=[hnn])
            pb, pn = PS.half()
            pbb = pb.bitcast(BF16)

            def tr(e):
                for c in range(8):
                    ins = e.transpose(pbb[:, c * 128:(c + 1) * 128], self.hn[sl][:, c * 128:(c + 1) * 128], identb[:])
                return ins
            T.op('pe', tr, R=[hnn, "identb"], W=[pn])
            T.op('dve', lambda e: e.tensor_copy(out=dstT, in_=pbb.rearrange("p (c t) -> p c t", c=8)), R=[pn], W=dst_names)

    def mm_acc(e, out, pairs):
        ins = None
        n = len(pairs)
        for i, (l, r) in enumerate(pairs):
            ins = e.matmul(out, lhsT=l, rhs=r, start=(i == 0), stop=(i == n - 1))
        return ins

    if "L" in phases:
      with contextlib.ExitStack() as sL:
        wlg = load_w(sL, "wlg", w_in, 1536, 2560, 32)
        prep = Prep(sL, "pl")
        hnT = [sbt(sL, f"l_hnT{i}", [128, 8, 512], BF16) for i in range(2)]
        xle = sbt(sL, "xle", [128, 4, 515], F32)
        xles = sbt(sL, "xles", [128, 4, 16, 11], F32)
        hst = sbt(sL, "hst", [128, 4], F32)
        hs0 = sbt(sL, "hs0", [128, 4, 16], F32)
        hpo = sbt(sL, "hpo", [128, 4], F32)
        hsl = sbt(sL, "hsl", [128, 4, 16], F32)
        NTMP = 4
        tmp = {}
        for nm in ("acc", "er", "ei", "a", "a2", "bb", "hT"):
            tmp[nm] = [sbt(sL, f"l_{nm}{i}", [128, 512], F32) for i in range(NTMP)]
        xcb = [sbt(sL, f"l_xcb{i}", [128, 512], BF16) for i in range(NTMP)]
        gsm = {nm: [sbt(sL, f"l_g{nm}{i}", [128, 128], F32) for i in range(NTMP)] for nm in ("g", "u", "e")}
        lrub = [sbt(sL, f"lrub{i}", [128, 4, 128], BF16) for i in range(2)]
        xltm = sbt(sL, "xltm", [128, 512], F32)
        stsb = sbt(sL, "stsb", [48, 512], F32)
        sthb = sbt(sL, "sthb", [16, 512], F32)
        hso = sbt(sL, "hso", [16, 512], F32)

        T.op('dve', lambda e: e.memset(xle[:], 0.0), W=[f"xle{t}" for t in range(4)])
        T.op('dve', lambda e: e.memset(hst[:], 0.0), W=["hst"])
        tcount = [0]

        lt_res = {}

        def lru_tile(t, ntok, sample, xview, xname, own_lo, own_rhs, own_rname, out_tile, out_name, first_sb):
            sl = tcount[0] % NTMP
            tcount[0] += 1
            tn = lambda nm: f"lt_{nm}{sl}"
            acc, er, ei, a, a2, bb, hT = (tmp[k][sl] for k in ("acc", "er", "ei", "a", "a2", "bb", "hT"))

            def v(tile_):
                if sample:
                    return tile_[:, 0:ntok].rearrange("p (s q) -> p s q", s=16)
                return tile_[:, 0:ntok]
            cw = lambda j: params[:, t * 4 + j:t * 4 + j + 1]
            T.op('dve', lambda e: e.tensor_scalar(out=v(acc), in0=xview(0), scalar1=cw(0), scalar2=params[:, 16 + t:17 + t],
                                                  op0=ALU.mult, op1=ALU.add), R=[xname, "params"], W=[tn("acc")])
            for j in (1, 2, 3):
                T.op('dve', lambda e, j=j: e.scalar_tensor_tensor(out=v(acc), in0=xview(j), scalar=cw(j), in1=v(acc),
                                                                  op0=ALU.mult, op1=ALU.add),
                     R=[xname, "params", tn("acc")], W=[tn("acc")])
            T.op('pool', lambda e: e.tensor_copy(out=xcb[sl][:, 0:ntok], in_=acc[:, 0:ntok]), R=[tn("acc")], W=[tn("xcb")])
            yield
            pa, pan = PS.half()
            px, pxn = PS.half()
            T.op('pe', lambda e: e.matmul(pa[:, 0:ntok], lhsT=bd[:, t, :], rhs=xcb[sl][:, 0:ntok], start=True, stop=True),
                 R=["bd", tn("xcb")], W=[pan])
            T.op('pe', lambda e: e.matmul(px[:, 0:ntok], lhsT=bd[:, 4 + t, :], rhs=xcb[sl][:, 0:ntok], start=True, stop=True),
                 R=["bd", tn("xcb")], W=[pxn])
            yield
            T.op('act', lambda e: e.activation(out=er[:, 0:ntok], in_=pa[:, 0:ntok], func=AF.Exp, scale=-1.0,
                                               bias=lruc[:, 8 + t:9 + t]), R=[pan, "lruc"], W=[tn("er")])
            T.op('act', lambda e: e.activation(out=ei[:, 0:ntok], in_=px[:, 0:ntok], func=AF.Exp, scale=-1.0,
                                               bias=lruc[:, 12 + t:13 + t]), R=[pxn, "lruc"], W=[tn("ei")])
            yield
            for nm, tl in (("er", er), ("ei", ei)):
                T.op('act', lambda e, tl=tl: e.activation(out=tl[:, 0:ntok], in_=tl[:, 0:ntok], func=AF.Ln, bias=1.0, scale=1.0),
                     R=[tn(nm)], W=[tn(nm)])
                T.op('act', lambda e, tl=tl: e.activation(out=tl[:, 0:ntok], in_=tl[:, 0:ntok], func=AF.Exp, scale=-1.0),
                     R=[tn(nm)], W=[tn(nm)])
            T.op('act', lambda e: e.activation(out=a[:, 0:ntok], in_=er[:, 0:ntok], func=AF.Exp, scale=lruc[:, t:t + 1]),
                 R=[tn("er"), "lruc"], W=[tn("a")])
            T.op('act', lambda e: e.activation(out=a2[:, 0:ntok], in_=er[:, 0:ntok], func=AF.Exp, scale=lruc[:, 4 + t:5 + t]),
                 R=[tn("er"), "lruc"], W=[tn("a2")])
            yield
            T.op('act', lambda e: e.activation(out=a2[:, 0:ntok], in_=a2[:, 0:ntok], func=AF.Ln, scale=-1.0, bias=1.0),
                 R=[tn("a2")], W=[tn("a2")])
            T.op('act', lambda e: e.activation(out=a2[:, 0:ntok], in_=a2[:, 0:ntok], func=AF.Exp, scale=0.5),
                 R=[tn("a2")], W=[tn("a2")])
            yield
            T.op('pool', lambda e: e.tensor_tensor(out=bb[:, 0:ntok], in0=ei[:, 0:ntok], in1=acc[:, 0:ntok], op=ALU.mult),
                 R=[tn("ei"), tn("acc")], W=[tn("bb")])
            T.op('dve', lambda e: e.tensor_tensor(out=bb[:, 0:ntok], in0=bb[:, 0:ntok], in1=a2[:, 0:ntok], op=ALU.mult),
                 R=[tn("bb"), tn("a2")], W=[tn("bb")])
            yield
            if first_sb:
                for blk in range(3):
                    T.op('dve', lambda e, blk=blk: e.tensor_scalar(out=bb[:, blk * 128:(blk + 1) * 128],
                                                                    in0=bb[:, blk * 128:(blk + 1) * 128],
                                                                    scalar1=vflag[:, blk:blk + 1], scalar2=None, op0=ALU.mult),
                         R=[tn("bb"), "vflag"], W=[tn("bb")])
            if not sample:
                T.op('dve', lambda e: e.tensor_tensor_scan(out=hT[:, 0:ntok], data0=a[:, 0:ntok], data1=bb[:, 0:ntok],
                                                           initial=hst[:, t:t + 1], op0=ALU.mult, op1=ALU.add),
                     R=[tn("a"), tn("bb"), "hst"], W=[tn("hT")])
                T.op('dve', lambda e: e.tensor_copy(out=hst[:, t:t + 1], in_=hT[:, ntok - 1:ntok]), R=[tn("hT")], W=["hst"])
            else:
                for s in range(16):
                    T.op('dve', lambda e, s=s: e.tensor_tensor_scan(out=hT[:, s * 8:(s + 1) * 8], data0=a[:, s * 8:(s + 1) * 8],
                                                                   data1=bb[:, s * 8:(s + 1) * 8], initial=hs0[:, t, s:s + 1],
                                                                   op0=ALU.mult, op1=ALU.add),
                         R=[tn("a"), tn("bb"), "hs0"], W=[tn("hT")])
                T.op('dve', lambda e: e.tensor_copy(out=hsl[:, t, :],
                                                    in_=hT[:, 0:128].rearrange("p (s q) -> p s q", s=16)[:, :, 7]),
                     R=[tn("hT")], W=["hsl"])
            g, u, ee = (gsm[k][sl] for k in ("g", "u", "e"))
            pg, pgn = PS.half()
            T.op('pe', lambda e: mm_acc(e, pg[:, 0:128], [(wlg[:, dc, 512 + t * 128:512 + (t + 1) * 128], own_rhs(dc))
                                                            for dc in range(8)]), R=["wlg", own_rname], W=[pgn])
            T.op('dve', lambda e: e.tensor_copy(out=g[:], in_=pg[:, 0:128]), R=[pgn], W=[tn("g")])
            yield
            T.op('pool', lambda e: e.tensor_tensor(out=u[:], in0=g[:], in1=g[:], op=ALU.mult), R=[tn("g")], W=[tn("u")])
            T.op('dve', lambda e: e.tensor_scalar(out=u[:], in0=u[:], scalar1=0.044715, scalar2=1.0, op0=ALU.mult, op1=ALU.add),
                 R=[tn("u")], W=[tn("u")])
            T.op('pool', lambda e: e.tensor_tensor(out=u[:], in0=u[:], in1=g[:], op=ALU.mult), R=[tn("u"), tn("g")], W=[tn("u")])
            T.op('act', lambda e: e.activation(out=ee[:], in_=u[:], func=AF.Exp, scale=-1.5957691216057308),
                 R=[tn("u")], W=[tn("e")])
            yield
            T.op('act', lambda e: e.activation(out=ee[:], in_=ee[:], func=AF.Ln, bias=1.0, scale=1.0), R=[tn("e")], W=[tn("e")])
            T.op('act', lambda e: e.activation(out=ee[:], in_=ee[:], func=AF.Exp, scale=-1.0), R=[tn("e")], W=[tn("e")])
            T.op('dve', lambda e: e.tensor_tensor(out=g[:], in0=g[:], in1=ee[:], op=ALU.mult), R=[tn("g"), tn("e")], W=[tn("g")])
            T.op('dve', lambda e: e.tensor_tensor(out=out_tile[:, t, :], in0=g[:], in1=hT[:, own_lo:own_lo + 128], op=ALU.mult),
                 R=[tn("g"), tn("hT")], W=[out_name])
            lt_res[t] = (hT, tn("hT"))
            yield

        for sb in range(NSB):
            hb = hnT[sb % 2]
            hname = f"l_hnT{sb % 2}"
            for blk in range(4):
                L = sb * 4 + blk
                prep.run(xloc[L * 128:(L + 1) * 128, :], hb[:, :, blk * 128:(blk + 1) * 128], [hname])
            lb = lrub[sb % 2]
            lname = f"lrub{sb % 2}"
            gens = []
            for t in range(4):
                pxl, pxn = PS.half()
                T.op('pe', lambda e: mm_acc(e, pxl[:, :], [(wlg[:, dc, t * 128:(t + 1) * 128], hb[:, dc, :]) for dc in range(8)]),
                     R=["wlg", hname], W=[pxn])
                T.op('act', lambda e: e.copy(out=xle[:, t, 3:515], in_=pxl[:, :]), R=[pxn], W=[f"xle{t}"])
                gens.append(lru_tile(t, 512, False, (lambda t: (lambda j: xle[:, t, j:j + 512]))(t), f"xle{t}", 384,
                                     lambda dc: hb[:, dc, 384:512], hname, lb, lname, sb == 0))
            live = list(gens)
            while live:
                for g_ in list(live):
                    try:
                        next(g_)
                    except StopIteration:
                        live.remove(g_)
            for t in range(4):
                hT, hTn = lt_res[t]
                T.op('pool', lambda e: e.tensor_copy(out=xle[:, t, 0:3], in_=xle[:, t, 512:515]), R=[f"xle{t}"], W=[f"xle{t}"])
                if sb == NSB - 1:
                    T.op('dve', lambda e: e.tensor_copy(out=hpo[:, t:t + 1], in_=hT[:, 384 + 15:384 + 16]), R=[hTn], W=["hpo"])
            T.dma('pool', mixs[sb, :, 4:8, :], lb[:], R=[lname], W=[f"mixs{sb}"])
            if sb == NSB - 1:
                T.dma('pool', hp_o[:, :], hpo[:], R=["hpo"])
                pc, pcn = PS.half()
                T.op('pe', lambda e: mm_acc(e, pc[:, :], [(hb[:, dc, 384:512], wlg[:, dc, 0:512]) for dc in range(8)]),
                     R=["wlg", hname], W=[pcn])
                T.op('dve', lambda e: e.tensor_copy(out=xltm[:], in_=pc[:, :]), R=[pcn], W=["xltm"])
                T.dma('pool', cp_o[:, :], xltm[13:16, :], R=["xltm"])

        hbs = hnT[NSB % 2]
        hsname = f"l_hnT{NSB % 2}"
        prep.run(xs[:, :], hbs[:, :, 0:128], [hsname])
        T.dma('sp', stsb[:], stc[:, :], W=["stsb"])
        T.dma('sp', sthb[:], sth[:, :], W=["sthb"])
        pst, pstn = PS.half()

        def trs(e):
            for t in range(4):
                ins = e.transpose(pst[:, t * 48:(t + 1) * 48], stsb[0:48, t * 128:(t + 1) * 128], identf[0:48, 0:48])
            return ins
        T.op('pe', trs, R=["stsb", "identf"], W=[pstn])
        T.op('dve', lambda e: e.tensor_copy(out=xles[:, :, :, 0:3],
                                            in_=pst[:, 0:192].rearrange("p (t s j) -> p t s j", t=4, s=16)), R=[pstn], W=["xles"])
        psh, pshn = PS.half()

        def trh(e):
            for t in range(4):
                ins = e.transpose(psh[:, t * 16:(t + 1) * 16], sthb[0:16, t * 128:(t + 1) * 128], identf[0:16, 0:16])
            return ins
        T.op('pe', trh, R=["sthb", "identf"], W=[pshn])
        T.op('dve', lambda e: e.tensor_copy(out=hs0[:], in_=psh[:, 0:64].rearrange("p (t s) -> p t s", t=4)), R=[pshn], W=["hs0"])
        lb = lrub[NSB % 2]
        lname = f"lrub{NSB % 2}"
        for t in range(4):
            pxl, pxn = PS.half()
            T.op('pe', lambda e: mm_acc(e, pxl[:, 0:128], [(wlg[:, dc, t * 128:(t + 1) * 128], hbs[:, dc, 0:128])
                                                             for dc in range(8)]), R=["wlg", hsname], W=[pxn])
            T.op('act', lambda e: e.copy(out=xles[:, t, :, 3:11], in_=pxl[:, 0:128].rearrange("p (s q) -> p s q", s=16)),
                 R=[pxn], W=["xles"])
            for _ in lru_tile(t, 128, True, lambda j: xles[:, t, :, j:j + 8], "xles", 0,
                              lambda dc: hbs[:, dc, 0:128], hsname, lb, lname, False):
                pass
        T.dma('pool', mixs[SIDX, :, 4:8, :], lb[:], R=[lname], W=[f"mixs{SIDX}"])
        pho, phon = PS.half()

        def trho(e):
            for t in range(4):
                ins = e.transpose(pho[0:16, t * 128:(t + 1) * 128], hsl[:, t, :], identf[:, :])
            return ins
        T.op('pe', trho, R=["hsl", "identf"], W=[phon])
        T.op('dve', lambda e: e.tensor_copy(out=hso[:], in_=pho[0:16, :]), R=[phon], W=["hso"])
        T.dma('pool', hs_o[:, :], hso[:], R=["hso"])
        pc, pcn = PS.half()
        T.op('pe', lambda e: mm_acc(e, pc[:, :], [(hbs[:, dc, 0:128], wlg[:, dc, 0:512]) for dc in range(8)]),
             R=["wlg", hsname, "xltm"], W=[pcn])
        T.op('dve', lambda e: e.tensor_copy(out=xltm[:], in_=pc[:, :]), R=[pcn], W=["xltm"])
        for s in range(16):
            T.dma('pool', cs_o[s, :, :], xltm[s * 8 + 5:s * 8 + 8, :], R=["xltm"])
        T.barrier()

    ZN = [["ps0", "ps1"], ["ps2", "ps3"]]
    XN = ["ps4", "ps5"]
    ON = "ps6"
    Xps = psf[2]
    Ops = psf[3][:, 0:512]
    sp7 = psf[3][:, 512:1024]

    def attn_pipeline(n, zgen, maskgen, wvgen, Et, St, Gt, Wt, tag, after_b2=None):
        T.op('pe', lambda e: e.matmul(Ops, lhsT=zl[0:1, :], rhs=zrow[0:1, :], start=True, stop=True), R=["zl", "zrow"], W=[ON])

        def Z(r):
            zgen(r, psf[r % 2], ZN[r % 2])

        def ES(r):
            zt = psf[r % 2]
            zn = ZN[r % 2]
            E = Et[r % 3]
            S = St[r % 3]
            T.op('act', lambda e: e.activation(out=E[:], in_=zt[:, :], func=AF.Exp), R=zn, W=[f"{tag}E{r % 3}"])
            maskgen(r, E, f"{tag}E{r % 3}")
            T.op('act', lambda e: e.activation(out=S[:], in_=E[:], func=AF.Ln, bias=1.0, scale=1.0),
                 R=[f"{tag}E{r % 3}"], W=[f"{tag}S{r % 3}"])

        def TRI(r):
            S = St[r % 3]

            def f(e):
                for hp in range(2):
                    ins = e.matmul(Xps[:, hp * 512:(hp + 1) * 512], lhsT=tri[:, :], rhs=S[:, hp * 512:(hp + 1) * 512],
                                   start=(r == 0), stop=True, skip_group_check=(r > 0))
                return ins
            T.op('pe', f, R=["tri", f"{tag}S{r % 3}"], W=XN)

        def OT(r):
            S = St[r % 3]

            def f(e):
                for hp in range(2):
                    ins = e.matmul(Xps[:, hp * 512:(hp + 1) * 512], lhsT=otri[:, :], rhs=S[:, hp * 512:(hp + 1) * 512],
                                   start=False, stop=True, skip_group_check=True)
                return ins
            T.op('pe', f, R=["otri", f"{tag}S{r % 3}"], W=XN)

        Z(0)
        ES(0)
        if n > 1:
            Z(1)
            ES(1)
        if n > 2:
            Z(2)
        TRI(0)
        for r in range(n):
            E = Et[r % 3]
            G = Gt[r % 2]
            W = Wt[r % 2]
            T.op('act', lambda e: e.activation(out=G[:], in_=Xps[:, :], func=AF.Exp, scale=-1.0), R=XN, W=[f"{tag}G{r % 2}"])
            if r + 1 < n:
                OT(r)
                TRI(r + 1)
            if r + 2 < n:
                ES(r + 2)
            if r + 3 < n:
                Z(r + 3)
            T.op('dve', lambda e: e.tensor_tensor(out=W[:], in0=E[:], in1=G[:], op=ALU.mult),
                 R=[f"{tag}E{r % 3}", f"{tag}G{r % 2}"], W=[f"{tag}W{r % 2}"])
            wvgen(r, W, f"{tag}W{r % 2}")
            if after_b2 is not None:
                after_b2(r)

    if "S" in phases:
      with contextlib.ExitStack() as sS:
        wqkv = load_w(sS, "wqkv", w_in, 0, 1536, 32)
        prep = Prep(sS, "pq")
        hnTs = sbt(sS, "hnTs", [128, 8, 128], BF16)
        qTs = sbt(sS, "qTs", [128, 4, 128], F32)
        kTn = sbt(sS, "kTn", [128, 4, 128], F32)
        ktm = sbt(sS, "ktm", [128, 512], F32)
        vtm = sbt(sS, "vtm", [128, 512], F32)
        smask = sbt(sS, "smaskt", [128, 1024], F32)
        NI = 16 * NPG
        ptb = sbt(sS, "ptb", [128, NI], I32)
        ptf = sbt(sS, "ptf", [128, NI], F32)
        iot = sbt(sS, "iot", [128, 1], I32)
        iotf = sbt(sS, "iotf", [128, 1], F32)
        idxall = sbt(sS, "idxall", [128, NI], I32)
        kpg = sbt(sS, "kpg", [128, 16, 512], F32)
        vpg = sbt(sS, "vpg", [128, 16, 512], F32)
        ktmp = [sbt(sS, f"ktmp{i}", [128, 4, 128], BF16) for i in range(2)]
        qTb = sbt(sS, "qTb", [128, 4, 128], BF16)
        vtmb = sbt(sS, "vtmb", [128, 512], BF16)
        vpgb = sbt(sS, "vpgb", [128, 16, 512], BF16)
        Et = [sbt(sS, f"sE{i}", [128, 1024], F32) for i in range(3)]
        St = [sbt(sS, f"sS{i}", [128, 1024], BF16) for i in range(3)]
        Gt = [sbt(sS, f"sG{i}", [128, 1024], F32) for i in range(2)]
        Wt = [sbt(sS, f"sW{i}", [128, 1024], BF16) for i in range(2)]
        atto = sbt(sS, "satto", [128, 4, 128], BF16)

        T.dma('sp', smask[:], smask_d[:, :], W=["smask"])
        T.dma('sp', ptb[:], pt.partition_broadcast(128), W=["ptb"])
        T.op('pool', lambda e: e.iota(iot[:], pattern=[[0, 1]], base=0, channel_multiplier=1), W=["iot"])
        T.op('dve', lambda e: e.tensor_copy(out=iotf[:], in_=iot[:]), R=["iot"], W=["iotf"])
        T.op('dve', lambda e: e.tensor_copy(out=ptf[:], in_=ptb[:]), R=["ptb"], W=["ptf"])
        T.op('dve', lambda e: e.tensor_scalar(out=ptf[:], in0=ptf[:], scalar1=128.0, scalar2=iotf[:, 0:1], op0=ALU.mult,
                                              op1=ALU.add), R=["ptf", "iotf"], W=["ptf"])
        T.op('dve', lambda e: e.tensor_copy(out=idxall[:], in_=ptf[:]), R=["ptf"], W=["idxall"])

        prep.run(xs[:, :], hnTs[:, :, :], ["hnTs"])
        pq, pqn = PS.half()
        pk, pkn = PS.half()

        def projT(ps_, c0):
            def f(e):
                for t in range(4):
                    ins = mm_acc(e, ps_[:, t * 128:(t + 1) * 128],
                                 [(wqkv[:, dc, c0 + t * 128:c0 + (t + 1) * 128], hnTs[:, dc, :]) for dc in range(8)])
                return ins
            return f
        T.op('pe', projT(pq, 0), R=["wqkv", "hnTs"], W=[pqn])
        T.op('act', lambda e: e.activation(out=qTs[:].rearrange("p t n -> p (t n)"), in_=pq[:, :], func=AF.Copy, scale=0.125), R=[pqn], W=["qTs"])
        T.op('pe', projT(pk, 512), R=["wqkv", "hnTs"], W=[pkn])
        T.op('dve', lambda e: e.tensor_copy(out=kTn[:].rearrange("p t n -> p (t n)"), in_=pk[:, :]), R=[pkn], W=["kTn"])
        pk2, pk2n = PS.half()
        T.op('pe', lambda e: mm_acc(e, pk2[:, :], [(hnTs[:, dc, :], wqkv[:, dc, 512:1024]) for dc in range(8)]),
             R=["wqkv", "hnTs"], W=[pk2n])
        T.op('dve', lambda e: e.tensor_copy(out=ktm[:], in_=pk2[:, :]), R=[pk2n], W=["ktm"])
        T.dma('sp', k_o[SIDX], ktm[:], R=["ktm"])
        pv2, pv2n = PS.half()
        T.op('pe', lambda e: mm_acc(e, pv2[:, :], [(hnTs[:, dc, :], wqkv[:, dc, 1024:1536]) for dc in range(8)]),
             R=["wqkv", "hnTs"], W=[pv2n])
        T.op('dve', lambda e: e.tensor_copy(out=vtm[:], in_=pv2[:, :]), R=[pv2n], W=["vtm"])
        T.dma('sp', v_o[SIDX], vtm[:], R=["vtm"])
        T.op('dve', lambda e: e.tensor_copy(out=vtmb[:], in_=vtm[:]), R=["vtm"], W=["vtmb"])
        T.op('dve', lambda e: e.tensor_copy(out=qTb[:], in_=qTs[:]), R=["qTs"], W=["qTb"])
        T.barrier()

        def gat_k(r):
            i = NPG - r
            for s in range(16):
                T.gather(kpg[:, s, :], ck[:, :], idxall[:, s * NPG + i:s * NPG + i + 1], R=["idxall"], W=[f"kpg{s}"])

        def gat_v(r):
            i = NPG - r
            for s in range(16):
                T.gather(vpg[:, s, :], cv[:, :], idxall[:, s * NPG + i:s * NPG + i + 1], R=["idxall"], W=[f"vpg{s}"])
            for s in range(16):
                T.op('dve', lambda e, s=s: e.tensor_copy(out=vpgb[:, s, :], in_=vpg[:, s, :]), R=[f"vpg{s}"], W=[f"vpgb{s}"])

        def zgen(r, zt, zn):
            if r >= 1:
                gat_k(r)

            def fb(e):
                for hp in range(2):
                    ins = e.matmul(zt[:, hp * 512:(hp + 1) * 512], lhsT=ones2[:, :], rhs=brow_p[:, hp * 512:(hp + 1) * 512],
                                   start=True, stop=True)
                return ins
            T.op('pe', fb, R=["ones2", brow_p.name], W=zn)
            if r == 0:
                def f(e):
                    for h in range(8):
                        t, hp = h // 2, h % 2
                        out = zt[:, hp * 512 + t * 128:hp * 512 + (t + 1) * 128]
                        ins = e.matmul(out, lhsT=kTn[hp * 64:(hp + 1) * 64, t, :],
                                       rhs=qTs[hp * 64:(hp + 1) * 64, t, :],
                                       start=False, stop=True, skip_group_check=True)
                    return ins
                T.op('pe', f, R=["kTn", "qTs"], W=zn)
            else:
                for s in range(16):
                    kt = ktmp[s % 2]
                    ktn = f"ktmp{s % 2}"

                    def ftr(e):
                        for t in range(4):
                            ins = e.transpose(sp7[:, t * 128:(t + 1) * 128], kpg[:, s, t * 128:(t + 1) * 128], identf[:, :])
                        return ins
                    T.op('pe', ftr, R=[f"kpg{s}", "identf"], W=["ps7"])
                    T.op('dve', lambda e: e.tensor_copy(out=kt[:].rearrange("p t n -> p (t n)"), in_=sp7[:, :]),
                         R=["ps7"], W=[ktn])

                    def fq(e):
                        for h in range(8):
                            t, hp = h // 2, h % 2
                            c = hp * 512 + t * 128 + s * 8
                            ins = e.matmul(zt[:, c:c + 8], lhsT=kt[hp * 64:(hp + 1) * 64, t, :],
                                           rhs=qTb[hp * 64:(hp + 1) * 64, t, s * 8:(s + 1) * 8],
                                           start=False, stop=True, skip_group_check=True)
                        return ins
                    T.op('pe', fq, R=[ktn, "qTb"], W=zn)

        def maskgen(r, E, en):
            if r == 0:
                T.op('dve', lambda e: e.tensor_tensor(out=E[:], in0=E[:], in1=smask[:], op=ALU.mult), R=[en, "smask"], W=[en])

        def wvgen(r, W, wn):
            if r == 0:
                def f(e):
                    for h in range(8):
                        t, hp = h // 2, h % 2
                        out = Ops[hp * 64:(hp + 1) * 64, t * 128:(t + 1) * 128]
                        rhs = W[:, hp * 512 + t * 128:hp * 512 + (t + 1) * 128]
                        ins = e.matmul(out, lhsT=vtmb[:, h * 64:(h + 1) * 64], rhs=rhs, start=False, stop=True,
                                       skip_group_check=True)
                    return ins
                T.op('pe', f, R=["vtmb", wn], W=[ON])
            else:
                for s in range(16):
                    def f(e):
                        for h in range(8):
                            t, hp = h // 2, h % 2
                            c = hp * 512 + t * 128 + s * 8
                            ins = e.matmul(Ops[hp * 64:(hp + 1) * 64, t * 128 + s * 8:t * 128 + (s + 1) * 8],
                                           lhsT=vpgb[:, s, h * 64:(h + 1) * 64], rhs=W[:, c:c + 8], start=False, stop=True,
                                           skip_group_check=True)
                        return ins
                    T.op('pe', f, R=[f"vpgb{s}", wn], W=[ON])

        def after_b2(r):
            if r + 1 <= NPG:
                gat_v(r + 1)

        SS = DBG.get('sstop', 9)
        if SS >= 1:
            attn_pipeline(NPG + 1 if SS >= 2 else 1, zgen, maskgen, wvgen, Et, St, Gt, Wt, "s", after_b2 if SS >= 2 else None)
        T.op('dve', lambda e: e.tensor_copy(out=atto[:].rearrange("p t n -> p (t n)"), in_=Ops), R=[ON], W=["satto"])
        T.dma('pool', mixs[SIDX, :, 0:4, :], atto[:], R=["satto"], W=[f"mixa{SIDX}"])
        T.barrier()

    if "K" in phases:
      with contextlib.ExitStack() as sK:
        kT = sbt(sK, "kT", [128, 4, NT], BF16)
        Vr = sbt(sK, "Vr", [128, NB, 512], BF16)
        qT = sbt(sK, "qT", [128, NSB, 4, 128], BF16)
        with contextlib.ExitStack() as sK2:
            wqkv = load_w(sK2, "wqkv2", w_in, 0, 1536, 32)
            prep = Prep(sK2, "pk", nx=1)
            hnT = [sbt(sK2, f"k_hnT{i}", [128, 8, 512], BF16) for i in range(1)]
            kst = [sbt(sK2, f"kst{i}", [128, 512], F32) for i in range(1)]
            vst = [sbt(sK2, f"vst{i}", [128, 512], F32) for i in range(1)]
            KS = DBG.get('kstop', 9)
            for sb in range(NSB):
                if KS < 1:
                    break
                hb = hnT[0]
                hname = "k_hnT0"
                for blk in range(4):
                    L = sb * 4 + blk
                    prep.run(xloc[L * 128:(L + 1) * 128, :], hb[:, :, blk * 128:(blk + 1) * 128], [hname])
                for t in range(4):
                    if KS < 2:
                        break
                    pk, pkn = PS.half()
                    T.op('pe', lambda e: mm_acc(e, pk[:, :], [(wqkv[:, dc, 512 + t * 128:512 + (t + 1) * 128], hb[:, dc, :])
                                                               for dc in range(8)]), R=["wqkv2", hname], W=[pkn])
                    eng = 'act' if t % 2 else 'dve'
                    if eng == 'act':
                        T.op('act', lambda e: e.copy(out=kT[:, t, sb * 512:(sb + 1) * 512], in_=pk[:, :]), R=[pkn], W=[f"kT{sb}"])
                    else:
                        T.op('dve', lambda e: e.tensor_copy(out=kT[:, t, sb * 512:(sb + 1) * 512], in_=pk[:, :]),
                             R=[pkn], W=[f"kT{sb}"])
                for blk in range(4):
                    if KS < 3:
                        break
                    L = sb * 4 + blk
                    pv, pvn = PS.half()
                    T.op('pe', lambda e: mm_acc(e, pv[:, :], [(hb[:, dc, blk * 128:(blk + 1) * 128], wqkv[:, dc, 1024:1536])
                                                               for dc in range(8)]), R=["wqkv2", hname], W=[pvn])
                    if DBG.get('vevac', 1):
                        if blk == 1 and DBG.get('vact', 1):
                            T.op('act', lambda e: e.copy(out=Vr[:, L, :], in_=pv[:, :]), R=[pvn], W=[f"Vr{L}"])
                        else:
                            T.op('dve', lambda e: e.tensor_copy(out=Vr[:, L, :], in_=pv[:, :]), R=[pvn], W=[f"Vr{L}"])
                    if blk == 3 and DBG.get('vout', 1):
                        vs_ = vst[0]
                        T.op('dve', lambda e: e.tensor_copy(out=vs_[:], in_=pv[:, :]), R=[pvn], W=["vst0"])
                        T.dma('pool', v_o[sb], vs_[:], R=["vst0"])
                if KS < 4:
                    continue
                pk2, pk2n = PS.half()
                T.op('pe', lambda e: mm_acc(e, pk2[:, :], [(hb[:, dc, 384:512], wqkv[:, dc, 512:1024]) for dc in range(8)]),
                     R=["wqkv2", hname], W=[pk2n])
                ks_ = kst[0]
                T.op('act', lambda e: e.copy(out=ks_[:], in_=pk2[:, :]), R=[pk2n], W=["kst0"])
                T.dma('pool', k_o[sb], ks_[:], R=["kst0"])
                if KS < 5:
                    continue
                pq, pqn = PS.half()

                def fq(e):
                    for t in range(4):
                        ins = mm_acc(e, pq[:, t * 128:(t + 1) * 128],
                                     [(wqkv[:, dc, t * 128:(t + 1) * 128], hb[:, dc, 384:512]) for dc in range(8)])
                    return ins
                T.op('pe', fq, R=["wqkv2", hname], W=[pqn])
                T.op('act', lambda e: e.activation(out=qT[:, sb, :, :].rearrange("p t n -> p (t n)"), in_=pq[:, :], func=AF.Copy, scale=0.125), R=[pqn], W=["qT"])
            T.barrier()

        if "A" in phases:
          with contextlib.ExitStack() as sA:
            Et = [sbt(sA, f"aE{i}", [128, 1024], BF16) for i in range(3)]
            St = [sbt(sA, f"aS{i}", [128, 1024], BF16) for i in range(3)]
            Gt = [sbt(sA, f"aG{i}", [128, 1024], BF16) for i in range(2)]
            Wt = [sbt(sA, f"aW{i}", [128, 1024], BF16) for i in range(2)]
            atto = [sbt(sA, f"aatto{i}", [128, 4, 128], BF16) for i in range(2)]
            for m in range(NSB):
                L = 4 * m + 3

                def zgen(r, zt, zn):
                    Lk = L - r

                    def f(e):
                        for hp in range(2):
                            e.matmul(zt[:, hp * 512:(hp + 1) * 512], lhsT=ones2[:, :], rhs=brow_p[:, hp * 512:(hp + 1) * 512],
                                     start=True, stop=True)
                        for h in range(8):
                            t, hp = h // 2, h % 2
                            ins = e.matmul(zt[:, hp * 512 + t * 128:hp * 512 + (t + 1) * 128],
                                           lhsT=kT[hp * 64:(hp + 1) * 64, t, Lk * 128:(Lk + 1) * 128],
                                           rhs=qT[hp * 64:(hp + 1) * 64, m, t, :], start=False, stop=True, skip_group_check=True)
                        return ins
                    T.op('pe', f, R=["ones2", brow_p.name, "qT", f"kT{Lk // 4}"], W=zn)

                def maskgen(r, E, en):
                    Lk = L - r
                    if r == 0:
                        T.op('dve', lambda e: e.tensor_tensor(out=E[:].rearrange("p (h q) -> p h q", h=8),
                                                               in0=E[:].rearrange("p (h q) -> p h q", h=8),
                                                               in1=otri[:, None, :].to_broadcast([128, 8, 128]), op=ALU.mult),
                             R=[en, "otri"], W=[en])
                    if Lk < 3:
                        T.op('dve', lambda e: e.tensor_scalar(out=E[:], in0=E[:], scalar1=vflag[:, Lk:Lk + 1], scalar2=None,
                                                               op0=ALU.mult), R=[en, "vflag"], W=[en])

                def wvgen(r, W, wn):
                    Lk = L - r

                    def f(e):
                        for h in range(8):
                            t, hp = h // 2, h % 2
                            ins = e.matmul(Ops[hp * 64:(hp + 1) * 64, t * 128:(t + 1) * 128],
                                           lhsT=Vr[:, Lk, h * 64:(h + 1) * 64],
                                           rhs=W[:, hp * 512 + t * 128:hp * 512 + (t + 1) * 128], start=False, stop=True,
                                           skip_group_check=True)
                        return ins
                    T.op('pe', f, R=[f"Vr{Lk}", wn], W=[ON])

                attn_pipeline(L + 1, zgen, maskgen, wvgen, Et, St, Gt, Wt, "a")
                ao = atto[m % 2]
                T.op('dve', lambda e: e.tensor_copy(out=ao[:].rearrange("p t n -> p (t n)"), in_=Ops), R=[ON], W=[f"aatto{m % 2}"])
                T.dma('pool', mixs[m, :, 0:4, :], ao[:], R=[f"aatto{m % 2}"], W=[f"mixa{m}"])
            T.barrier()

    if "M" in phases:
      with contextlib.ExitStack() as sM:
        with contextlib.ExitStack() as sM1:
            gpb = sbt(sM1, "gpb", [128, D], F32)
            T.dma('sp', gpb[:, :], gpost_d[0:1, :].partition_broadcast(128), W=["gpb"])
            wo = load_w(sM1, "wo", w_out, 0, D, None)
            prep = Prep(sM1, "pm")
            mixT = [sbt(sM1, f"mixT{i}", [128, 8, 128], BF16) for i in range(2)]
            xr = [sbt(sM1, f"xr{i}", [128, D], F32) for i in range(2)]
            x1b = [sbt(sM1, f"x1b{i}", [128, D], F32) for i in range(2)]
            for o in range(NOWN):
                sl = o % 2
                T.dma('sp', mixT[sl][:], mixs[o], W=[f"mixT{sl}"])
                rows = xs[:, :] if o == SIDX else xloc[(4 * o + 3) * 128:(4 * o + 4) * 128, :]
                T.dma('sp', xr[sl][:], rows, W=[f"xr{sl}"])
                pm, pmn = PS.full()

                def f(e):
                    for hf in range(2):
                        ins = mm_acc(e, pm[:, hf * 512:(hf + 1) * 512],
                                     [(mixT[sl][:, fc, :], wo[:, fc, hf * 512:(hf + 1) * 512]) for fc in range(8)])
                    return ins
                T.op('pe', f, R=["wo", f"mixT{sl}"], W=pmn)
                stat, sname = prep.rstd(pm[:, :], pmn)
                T.op('dve', lambda e: e.scalar_tensor_tensor(out=x1b[sl][:], in0=pm[:, :], scalar=stat[:, 2:3], in1=gpb[:, :],
                                                             op0=ALU.mult, op1=ALU.mult), R=pmn + [sname, "gpb"], W=[f"x1b{sl}"])
                T.op('pool', lambda e: e.tensor_tensor(out=x1b[sl][:], in0=x1b[sl][:], in1=xr[sl][:], op=ALU.add),
                     R=[f"x1b{sl}", f"xr{sl}"], W=[f"x1b{sl}"])
                T.dma('pool', x1s[o], x1b[sl][:], R=[f"x1b{sl}"], W=[f"x1s{o}"])
            T.barrier()

        with contextlib.ExitStack() as sM2:
            wu = load_w(sM2, "wu", w_up, 0, 4096, 40)
            wd = load_w(sM2, "wd", w_down, 0, D, None)
            gpb = sbt(sM2, "gpb2", [128, D], F32)
            T.dma('sp', gpb[:, :], gpost_d[1:2, :].partition_broadcast(128), W=["gpb"])
            prep = Prep(sM2, "pn", nx=1)
            GB = 4
            x1 = [sbt(sM2, f"x1_{i}", [128, D], F32) for i in range(GB)]
            hn2T = sbt(sM2, "hn2T", [128, 8, GB * 128], BF16)
            u2T = sbt(sM2, "u2T", [128, 32, GB * 128], BF16)
            sqv = prep.junk[:, :].bitcast(F32)
            for g0 in range(0, NOWN, GB):
                nb = min(GB, NOWN - g0)
                ntok = nb * 128
                for i in range(nb):
                    o = g0 + i
                    T.dma('sp', x1[i][:], x1s[o], R=[f"x1s{o}"], W=[f"x1_{i}"])
                    prep.run(None, hn2T[:, :, i * 128:(i + 1) * 128], ["hn2T"], x_sb=x1[i], x_names=[f"x1_{i}"])
                for fc in range(32):
                    pu, pun = PS.half()
                    T.op('pe', lambda e: mm_acc(e, pu[:, 0:ntok], [(wu[:, dc, fc * 128:(fc + 1) * 128], hn2T[:, dc, 0:ntok])
                                                                    for dc in range(8)]), R=["wu", "hn2T"], W=[pun])
                    T.op('act', lambda e: e.activation(out=sqv[:, 0:ntok], in_=pu[:, 0:ntok], func=AF.Square),
                         R=[pun], W=["pn_junk"])
                    T.op('dve', lambda e: e.scalar_tensor_tensor(out=u2T[:, fc, 0:ntok], in0=pu[:, 0:ntok], scalar=0.0,
                                                                 in1=sqv[:, 0:ntok], op0=ALU.is_gt, op1=ALU.mult),
                         R=[pun, "pn_junk"], W=["u2T"])
                for i in range(nb):
                    o = g0 + i
                    pd, pdn = PS.full()

                    def f(e):
                        for hf in range(2):
                            ins = mm_acc(e, pd[:, hf * 512:(hf + 1) * 512],
                                         [(u2T[:, fc, i * 128:(i + 1) * 128], wd[:, fc, hf * 512:(hf + 1) * 512])
                                          for fc in range(32)])
                        return ins
                    T.op('pe', f, R=["wd", "u2T"], W=pdn)
                    stat, sname = prep.rstd(pd[:, :], pdn)
                    T.op('dve', lambda e: e.scalar_tensor_tensor(out=prep.xst[0][:], in0=pd[:, :], scalar=stat[:, 2:3],
                                                                 in1=gpb[:, :], op0=ALU.mult, op1=ALU.mult),
                         R=pdn + [sname, "gpb"], W=["pn_xst0"])
                    T.op('pool', lambda e: e.tensor_tensor(out=x1[i][:], in0=prep.xst[0][:], in1=x1[i][:],
                                                           op=ALU.add), R=["pn_xst0", f"x1_{i}"], W=[f"x1_{i}"])
                    T.dma('pool', y_o[o], x1[i][:], R=[f"x1_{i}"])
            T.barrier()

    T.finish()
    es.close()
    return nc


def _host_inputs(inp, NSB, NPG, NPOOL):
    f32 = np.float32
    xp = np.asarray(inp["x_prompt"], f32)
    xsm = np.asarray(inp["x_sample"], f32)
    meta = np.asarray(inp["meta_tokens"], f32)
    B, SEQ, _ = xp.shape
    TR = SEQ + meta.shape[0]
    NB = 4 * NSB
    assert TR <= (NB - 3) * 128
    ck = np.ascontiguousarray(np.asarray(inp["cache_k"], f32)[0].reshape(NPOOL * 128, 512))
    cv = np.ascontiguousarray(np.asarray(inp["cache_v"], f32)[0].reshape(NPOOL * 128, 512))
    ptab = np.asarray(inp["page_table"], np.int32)
    params = np.zeros((128, NPAR), f32)

    def chan(v):
        return np.asarray(v, f32).reshape(4, 128).T
    cw = np.asarray(inp["conv_w"], f32)[0]
    for j in range(4):
        params[:, j:16:4] = chan(cw[j])
    params[:, 16:20] = chan(inp["conv_b"][0])
    params[:, 20:24] = chan(inp["b_gate_a"][0])
    params[:, 24:28] = chan(inp["b_gate_x"][0])
    params[:, 28:32] = chan(inp["lru_lambda"][0])
    params[:, 32:40] = np.asarray(inp["g_mix_pre"], f32)[0].reshape(8, 128).T
    params[:, 40:48] = np.asarray(inp["g_mlp_pre"], f32)[0].reshape(8, 128).T
    gpost = np.stack([np.asarray(inp["g_mix_post"], f32)[0], np.asarray(inp["g_mlp_post"], f32)[0]], 0)
    k_s = np.arange(128) // 8
    k_t = np.arange(128) % 8
    col = np.arange(1024)
    c_s = (col % 128) // 8
    c_q = col % 8
    smask = ((k_s[:, None] == c_s[None, :]) & (k_t[:, None] < c_q[None, :])).astype(f32)
    shared = dict(ck=ck, cv=cv, params=params, gpost=np.ascontiguousarray(gpost),
                  sbias=np.asarray(inp["sb_bias"], f32).reshape(1, 8), smask=smask,
                  w_in=np.ascontiguousarray(np.asarray(inp["w_in"], f32)[0]),
                  w_out=np.ascontiguousarray(np.asarray(inp["w_out"], f32)[0]),
                  w_up=np.ascontiguousarray(np.asarray(inp["w_up"], f32)[0]),
                  w_down=np.ascontiguousarray(np.asarray(inp["w_down"], f32)[0]),
                  wga=np.ascontiguousarray(np.asarray(inp["w_gate_a"], f32)[0]),
                  wgx=np.ascontiguousarray(np.asarray(inp["w_gate_x"], f32)[0]))
    maps = []
    for c in range(8):
        b, j = c // 4, c % 4
        xloc = np.zeros((NB * 128, D), f32)
        o = (3 - j) * 128
        xloc[o:o + meta.shape[0]] = meta
        xloc[o + meta.shape[0]:o + TR] = xp[b]
        vflag = np.ones((128, 4), f32)
        for L in range(3):
            if L < 3 - j:
                vflag[:, L] = 0.0
        m = dict(shared)
        m.update(xloc=xloc, xs=np.ascontiguousarray(xsm[16 * c:16 * c + 16].reshape(128, D)),
                 pt=np.ascontiguousarray(ptab[16 * c:16 * c + 16].reshape(1, 16 * NPG)),
                 sth=np.ascontiguousarray(np.asarray(inp["state_h"], f32)[0, 16 * c:16 * c + 16]),
                 stc=np.ascontiguousarray(np.asarray(inp["state_conv"], f32)[0, 16 * c:16 * c + 16].reshape(48, 512)),
                 vflag=vflag)
        maps.append(m)
    return maps


def _host_outputs(res, inp, NSB):
    f32 = np.float32
    B, SEQ, _ = inp["x_prompt"].shape
    NM = inp["meta_tokens"].shape[0]
    TR = SEQ + NM
    yp = np.zeros((B, TR, D), f32)
    kp = np.zeros((B, TR, 512), f32)
    vp = np.zeros((B, TR, 512), f32)
    hp = np.zeros((1, B, 512), f32)
    cp = np.zeros((1, B, 3, 512), f32)
    ys = np.zeros((128, 8, D), f32)
    ks = np.zeros((1, 128, 8, 8, 64), f32)
    vs = np.zeros((1, 128, 8, 8, 64), f32)
    hs = np.zeros((1, 128, 512), f32)
    cs = np.zeros((1, 128, 3, 512), f32)
    glast = (TR - 1) // 128
    for c in range(8):
        b, j = c // 4, c % 4
        r = res[c]
        for m in range(NSB):
            g = 4 * m + j
            lo = g * 128
            if lo >= TR:
                continue
            n = min(128, TR - lo)
            yp[b, lo:lo + n] = r["y"][m, :n]
            kp[b, lo:lo + n] = r["ko"][m, :n]
            vp[b, lo:lo + n] = r["vo"][m, :n]
        if j == glast % 4:
            hp[0, b] = r["hp"].T.reshape(512)
            cp[0, b] = r["cp"]
        ys[16 * c:16 * c + 16] = r["y"][NSB].reshape(16, 8, D)
        ks[0, 16 * c:16 * c + 16] = r["ko"][NSB].reshape(16, 8, 8, 64)
        vs[0, 16 * c:16 * c + 16] = r["vo"][NSB].reshape(16, 8, 8, 64)
        hs[0, 16 * c:16 * c + 16] = r["hs"]
        cs[0, 16 * c:16 * c + 16] = r["cs"]
    return (yp[:, NM:], ys, kp.reshape(1, B, TR, 8, 64), vp.reshape(1, B, TR, 8, 64), hp, cp, ks, vs, hs, cs)


_CACHE = {}


def kernel(**inputs):
    SEQ = inputs["x_prompt"].shape[1]
    NM = inputs["meta_tokens"].shape[0]
    nblk = -(-(SEQ + NM) // 128)
    NSB = -(-(nblk + 3) // 4)
    NPG = inputs["page_table"].shape[1]
    NPOOL = inputs["cache_k"].shape[1]
    key = (NSB, NPG, NPOOL)
    if key not in _CACHE:
        _CACHE[key] = build(NSB, NPG, NPOOL)
    nc = _CACHE[key]
    maps = _host_inputs(inputs, NSB, NPG, NPOOL)
    res = run_bass_kernel_spmd(nc, maps, core_ids=list(range(8)))
    return _host_outputs(res.results, inputs, NSB)
```

```python
import contextlib
import numpy as np
import concourse.bass as bass
import concourse.mybir as mybir
from concourse.bass_utils import run_bass_kernel_spmd

F32 = mybir.dt.float32
BF16 = mybir.dt.bfloat16
I32 = mybir.dt.int32
AF = mybir.ActivationFunctionType
ALU = mybir.AluOpType

D = 1024
DBG = {}
NPAR = 48


class Trk:
    ND = 8

    def __init__(self, nc, es):
        self.nc = nc
        self.eng = {'pe': nc.tensor, 'act': nc.scalar, 'dve': nc.vector, 'pool': nc.gpsimd, 'sp': nc.sync}
        self.sem = {}
        self.cnt = {}
        for k in ('pe', 'act', 'dve', 'pool'):
            self.sem[k] = es.enter_context(nc.semaphore("s_" + k))
            self.cnt[k] = 0
        self.ndq = {'sp': self.ND, 'pool': DBG.get('ndpool', 4)}
        for q in ('sp', 'pool'):
            for i in range(self.ndq[q]):
                k = ('d', q, i)
                self.sem[k] = es.enter_context(nc.semaphore(f"d_{q}{i}"))
                self.cnt[k] = 0
        self.dnext = {'sp': 0, 'pool': 0}
        self.seen = {e: {} for e in self.eng}
        self.lastw = {}
        self.reads = {}

    def _wait(self, eng, tok):
        key, val = tok
        if eng == 'pe' and key == 'pe':
            return
        if self.seen[eng].get(key, 0) >= val:
            return
        self.eng[eng].wait_ge(self.sem[key], val)
        self.seen[eng][key] = val

    def _deps(self, eng, R, W):
        for r in R:
            t = self.lastw.get(r)
            if t is not None:
                self._wait(eng, t)
        for w in W:
            t = self.lastw.get(w)
            if t is not None:
                self._wait(eng, t)
            for t in self.reads.get(w, ()):
                self._wait(eng, t)

    def _record(self, tok, R, W):
        for w in W:
            self.lastw[w] = tok
            self.reads[w] = []
        for r in R:
            if r in W:
                continue
            lst = self.reads.setdefault(r, [])
            lst.append(tok)
            if len(lst) > 8:
                best = {}
                for k, v in lst:
                    if best.get(k, 0) < v:
                        best[k] = v
                self.reads[r] = list(best.items())

    def op(self, eng, fn, R=(), W=()):
        self._deps(eng, R, W)
        ins = fn(self.eng[eng])
        self.cnt[eng] += 1
        ins.then_inc(self.sem[eng], 1)
        tok = (eng, self.cnt[eng])
        self._record(tok, R, W)
        return tok

    def _dma_slot(self, q):
        i = self.dnext[q]
        self.dnext[q] = (i + 1) % self.ndq[q]
        k = ('d', q, i)
        if self.cnt[k] > 0:
            self._wait(q, (k, 16 * self.cnt[k]))
        return k

    def dma(self, q, out, in_, R=(), W=()):
        self._deps(q, R, W)
        k = self._dma_slot(q)
        ins = self.eng[q].dma_start(out=out, in_=in_)
        self.cnt[k] += 1
        ins.then_inc(self.sem[k], 16)
        tok = (k, 16 * self.cnt[k])
        self._record(tok, R, W)
        return tok

    def gather(self, out, in_, idx_ap, R=(), W=()):
        q = 'pool'
        self._deps(q, R, W)
        k = self._dma_slot(q)
        ins = self.nc.gpsimd.indirect_dma_start(
            out=out, out_offset=None, in_=in_,
            in_offset=bass.IndirectOffsetOnAxis(ap=idx_ap, axis=0))
        self.cnt[k] += 1
        ins.then_inc(self.sem[k], 16)
        tok = (k, 16 * self.cnt[k])
        self._record(tok, R, W)
        return tok

    def _all(self, eng):
        for k, c in self.cnt.items():
            if c == 0:
                continue
            v = c if isinstance(k, str) else 16 * c
            self._wait(eng, (k, v))

    def barrier(self):
        for e in ('pe', 'act', 'dve', 'pool', 'sp'):
            self._all(e)
        self.lastw.clear()
        self.reads.clear()

    def finish(self):
        self._all('sp')


def build(NSB, NPG, NPOOL, phases="LSKAM"):
    NB = 4 * NSB
    NT = NB * 128
    NOWN = NSB + 1
    SIDX = NSB
    nc = bass.Bass("TRN2", target_bir_lowering=False)
    es = contextlib.ExitStack()

    def din(name, shape, dt=F32):
        return nc.dram_tensor(name, shape, dt, kind="ExternalInput").ap()

    def dout(name, shape, dt=F32):
        return nc.dram_tensor(name, shape, dt, kind="ExternalOutput").ap()

    xloc = din("xloc", [NT, D])
    xs = din("xs", [128, D])
    ck = din("ck", [NPOOL * 128, 512])
    cv = din("cv", [NPOOL * 128, 512])
    pt = din("pt", [1, 16 * NPG], I32)
    sth = din("sth", [16, 512])
    stc = din("stc", [48, 512])
    vflag_d = din("vflag", [128, 4])
    params_d = din("params", [128, NPAR])
    gpost_d = din("gpost", [2, D])
    sbias_d = din("sbias", [1, 8])
    smask_d = din("smask", [128, 1024])
    w_in = din("w_in", [D, 2560])
    w_out = din("w_out", [D, D])
    w_up = din("w_up", [D, 4096])
    w_down = din("w_down", [4096, D])
    wga = din("wga", [8, 64, 64])
    wgx = din("wgx", [8, 64, 64])

    y_o = dout("y", [NOWN, 128, D])
    k_o = dout("ko", [NOWN, 128, 512])
    v_o = dout("vo", [NOWN, 128, 512])
    hp_o = dout("hp", [128, 4])
    cp_o = dout("cp", [3, 512])
    hs_o = dout("hs", [16, 512])
    cs_o = dout("cs", [16, 3, 512])

    mixs = nc.dram_tensor("mixs", [NOWN, 128, 8, 128], BF16).ap()
    x1s = nc.dram_tensor("x1s", [NOWN, 128, D], F32).ap()

    T = Trk(nc, es)

    def sbt(stack, name, shape, dt):
        return stack.enter_context(nc.sbuf_tensor("t_" + name, shape, dt))

    psf = [es.enter_context(nc.psum_tensor(f"ps{i}", [128, 1024], F32)) for i in range(4)]

    class PS:
        n = 0

        @staticmethod
        def half():
            i = PS.n % 8
            PS.n += 1
            return psf[i // 2][:, (i % 2) * 512:(i % 2) * 512 + 512], f"ps{i}"

        @staticmethod
        def full():
            if PS.n % 2:
                PS.n += 1
            i = PS.n % 8
            PS.n += 2
            return psf[i // 2], [f"ps{i}", f"ps{i + 1}"]

    identf = sbt(es, "identf", [128, 128], F32)
    identb = sbt(es, "identb", [128, 128], BF16)
    tri = sbt(es, "tri", [128, 128], BF16)
    otri = sbt(es, "otri", [128, 128], BF16)
    otrif = sbt(es, "otrif", [128, 128], F32)
    onesf = sbt(es, "onesf", [128, 128], F32)
    params = sbt(es, "params", [128, NPAR], F32)
    vflag = sbt(es, "vflagt", [128, 4], F32)
    lruc = sbt(es, "lruc", [128, 16], F32)
    ones2 = sbt(es, "ones2", [2, 128], BF16)
    brow_p = sbt(es, "brow_p", [2, 1024], BF16)
    brow_s = sbt(es, "brow_s", [2, 1024], BF16)
    zl = sbt(es, "zl", [1, 128], BF16)
    zrow = sbt(es, "zrow", [1, 512], BF16)
    bd = sbt(es, "bd", [128, 8, 128], BF16)

    T.op('pool', lambda e: e.memset(onesf[:], 1.0), W=["onesf"])
    T.op('pool', lambda e: e.affine_select(out=identf[:], in_=onesf[:], pattern=[[1, 128]], compare_op=ALU.is_equal,
                                           fill=0.0, base=0, channel_multiplier=-1), R=["onesf"], W=["identf"])
    T.op('pool', lambda e: e.tensor_copy(out=identb[:], in_=identf[:]), R=["identf"], W=["identb"])
    T.op('pool', lambda e: e.affine_select(out=tri[:], in_=onesf[:], pattern=[[-1, 128]], compare_op=ALU.is_ge,
                                           fill=0.0, base=0, channel_multiplier=1), R=["onesf"], W=["tri"])
    T.op('pool', lambda e: e.affine_select(out=otrif[:], in_=onesf[:], pattern=[[1, 128]], compare_op=ALU.is_gt,
                                           fill=0.0, base=0, channel_multiplier=-1), R=["onesf"], W=["otrif"])
    T.op('pool', lambda e: e.tensor_copy(out=otri[:], in_=otrif[:]), R=["otrif"], W=["otri"])
    T.op('pool', lambda e: e.memset(ones2[:], 1.0), W=["ones2"])
    T.op('pool', lambda e: e.memset(zl[:], 0.0), W=["zl"])
    T.op('pool', lambda e: e.memset(zrow[:], 0.0), W=["zrow"])
    T.dma('sp', params[:], params_d[:, :], W=["params"])
    T.dma('sp', vflag[:], vflag_d[:, :], W=["vflag"])

    with contextlib.ExitStack() as s0:
        bsrc = sbt(s0, "bsrc", [2, 8], F32)
        bst = sbt(s0, "bst", [2, 1024], F32)
        bst2 = sbt(s0, "bst2", [2, 1024], F32)
        lo_t = sbt(s0, "lo_t", [2, 1024], BF16)
        T.dma('sp', bsrc[0:1, :], sbias_d[:, :], W=["bsrc"])
        T.dma('sp', bsrc[1:2, :], sbias_d[:, :], W=["bsrc"])
        for dst, order in ((brow_p, "p"), (brow_s, "s")):
            dn = dst.name
            for hp in range(2):
                srcv = bsrc[:, :].rearrange("p (t hp) -> p hp t", hp=2)[:, hp, :]
                if order == "p":
                    T.op('dve', lambda e, hp=hp, srcv=srcv: e.tensor_copy(
                        out=bst[:, hp * 512:(hp + 1) * 512].rearrange("p (t q) -> p t q", t=4),
                        in_=srcv[:, :, None].to_broadcast([2, 4, 128])), R=["bsrc"], W=["bst"])
                else:
                    T.op('dve', lambda e, hp=hp, srcv=srcv: e.tensor_copy(
                        out=bst[:, hp * 512:(hp + 1) * 512].rearrange("p (s t q) -> p s t q", s=16, t=4),
                        in_=srcv[:, None, :, None].to_broadcast([2, 16, 4, 8])), R=["bsrc"], W=["bst"])
            T.op('dve', lambda e: e.tensor_copy(out=dst[:], in_=bst[:]), R=["bst"], W=[dn])
            T.op('dve', lambda e: e.tensor_copy(out=bst2[:], in_=dst[:]), R=[dn], W=["bst2"])
            T.op('dve', lambda e: e.tensor_tensor(out=bst2[:], in0=bst[:], in1=bst2[:], op=ALU.subtract),
                 R=["bst", "bst2"], W=["bst2"])
            T.op('dve', lambda e: e.tensor_copy(out=lo_t[:], in_=bst2[:]), R=["bst2"], W=["lo_t"])
            T.dma('sp', dst[1:2, :], lo_t[1:2, :], R=["lo_t"], W=[dn])

        T.op('act', lambda e: e.activation(out=lruc[:, 0:4], in_=params[:, 28:32], func=AF.Exp, scale=-1.0),
             R=["params"], W=["lruc"])
        T.op('act', lambda e: e.activation(out=lruc[:, 0:4], in_=lruc[:, 0:4], func=AF.Ln, bias=1.0, scale=1.0),
             R=["lruc"], W=["lruc"])
        T.op('dve', lambda e: e.tensor_scalar(out=lruc[:, 4:8], in0=lruc[:, 0:4], scalar1=-16.0, scalar2=None, op0=ALU.mult),
             R=["lruc"], W=["lruc"])
        T.op('dve', lambda e: e.tensor_scalar(out=lruc[:, 0:4], in0=lruc[:, 0:4], scalar1=-8.0, scalar2=None, op0=ALU.mult),
             R=["lruc"], W=["lruc"])
        T.op('dve', lambda e: e.tensor_scalar(out=lruc[:, 8:16], in0=params[:, 20:28], scalar1=-1.0, scalar2=None,
                                              op0=ALU.mult), R=["params", "lruc"], W=["lruc"])
        bdst = sbt(s0, "bdst", [128, 8, 128], F32)
        T.op('dve', lambda e: e.memset(bdst[:], 0.0), W=["bdst"])
        for gi, wsrc in enumerate((wga, wgx)):
            for t in range(4):
                for hb in range(2):
                    T.dma('sp', bdst[hb * 64:(hb + 1) * 64, gi * 4 + t, hb * 64:(hb + 1) * 64], wsrc[2 * t + hb],
                          W=["bdst"])
        T.op('dve', lambda e: e.tensor_copy(out=bd[:], in_=bdst[:]), R=["bdst"], W=["bd"])
        T.barrier()

    def load_w(stack, name, src, c0, c1, gcol):
        ncol = c1 - c0
        nk = src.shape[0] // 128
        wt = sbt(stack, name, [128, nk, ncol], BF16)
        with contextlib.ExitStack() as st:
            stg = [sbt(st, f"{name}_stg{i}", [128, 1024], F32) for i in range(3)]
            k = 0
            for dc in range(nk):
                for cc in range(0, ncol, 1024):
                    w = min(1024, ncol - cc)
                    sl = k % 3
                    sn = f"{name}_stg{sl}"
                    T.dma('sp', stg[sl][:, 0:w], src[dc * 128:(dc + 1) * 128, c0 + cc:c0 + cc + w], W=[sn])
                    eng = ('pool', 'dve', 'act')[k % 3] if gcol is None else ('act', 'dve')[k % 2]
                    if gcol is None:
                        if eng == 'act':
                            T.op(eng, lambda e, sl=sl, dc=dc, cc=cc, w=w: e.copy(out=wt[:, dc, cc:cc + w], in_=stg[sl][:, 0:w]),
                                 R=[sn], W=[name])
                        else:
                            T.op(eng, lambda e, sl=sl, dc=dc, cc=cc, w=w: e.tensor_copy(out=wt[:, dc, cc:cc + w],
                                                                                          in_=stg[sl][:, 0:w]), R=[sn], W=[name])
                    else:
                        gc = gcol + (dc % 8)
                        if eng == 'act':
                            T.op(eng, lambda e, sl=sl, dc=dc, cc=cc, w=w, gc=gc: e.activation(
                                out=wt[:, dc, cc:cc + w], in_=stg[sl][:, 0:w], func=AF.Identity, scale=params[:, gc:gc + 1]),
                                R=[sn, "params"], W=[name])
                        else:
                            T.op(eng, lambda e, sl=sl, dc=dc, cc=cc, w=w, gc=gc: e.tensor_scalar(
                                out=wt[:, dc, cc:cc + w], in0=stg[sl][:, 0:w], scalar1=params[:, gc:gc + 1],
                                scalar2=None, op0=ALU.mult), R=[sn, "params"], W=[name])
                    k += 1
            T.barrier()
        return wt

    class Prep:
        def __init__(self, stack, tag, nx=2):
            self.tag = tag
            self.nx = nx
            self.xst = [sbt(stack, f"{tag}_xst{i}", [128, D], F32) for i in range(nx)]
            self.hn = [sbt(stack, f"{tag}_hn{i}", [128, D], BF16) for i in range(nx)]
            self.junk = sbt(stack, f"{tag}_junk", [128, D], BF16)
            self.stat = [sbt(stack, f"{tag}_stat{i}", [128, 4], F32) for i in range(4)]
            self.k = 0
            self.ks = 0

        def rstd(self, src_ap, src_names):
            si = self.ks % 4
            self.ks += 1
            stat = self.stat[si]
            sname = f"{self.tag}_stat{si}"
            jn = f"{self.tag}_junk"
            T.op('act', lambda e: e.activation(out=self.junk[:], in_=src_ap, func=AF.Square, accum_out=stat[:, 0:1]),
                 R=src_names, W=[jn, sname])
            T.op('act', lambda e: e.activation(out=stat[:, 1:2], in_=stat[:, 0:1], func=AF.Ln, scale=1.0 / D, bias=1e-6),
                 R=[sname], W=[sname])
            T.op('act', lambda e: e.activation(out=stat[:, 2:3], in_=stat[:, 1:2], func=AF.Exp, scale=-0.5),
                 R=[sname], W=[sname])
            return stat, sname

        def run(self, xrows, dstT, dst_names, x_sb=None, x_names=None):
            sl = self.k % max(self.nx, 1)
            self.k += 1
            tag = self.tag
            if x_sb is None:
                T.dma('sp', self.xst[sl][:], xrows, W=[f"{tag}_xst{sl}"])
                x_sb = self.xst[sl]
                x_names = [f"{tag}_xst{sl}"]
            stat, sname = self.rstd(x_sb[:], x_names)
            hnn = f"{tag}_hn{sl}"
            T.op('act', lambda e: e.activation(out=self.hn[sl][:], in_=x_sb[:], func=AF.Identity, scale=stat[:, 2:3]),
                 R=x_names + [sname], W=[hnn])
            pb, pn = PS.half()
            pbb = pb.bitcast(BF16)

            def tr(e):
                for c in range(8):
                    ins = e.transpose(pbb[:, c * 128:(c + 1) * 128], self.hn[sl][:, c * 128:(c + 1) * 128], identb[:])
                return ins
            T.op('pe', tr, R=[hnn, "identb"], W=[pn])
            T.op('dve', lambda e: e.tensor_copy(out=dstT, in_=pbb.rearrange("p (c t) -> p c t", c=8)), R=[pn], W=dst_names)

    def mm_acc(e, out, pairs):
        ins = None
        n = len(pairs)
        for i, (l, r) in enumerate(pairs):
            ins = e.matmul(out, lhsT=l, rhs=r, start=(i == 0), stop=(i == n - 1))
        return ins

    if "L" in phases:
      with contextlib.ExitStack() as sL:
        wlg = load_w(sL, "wlg", w_in, 1536, 2560, 32)
        prep = Prep(sL, "pl")
        hnT = [sbt(sL, f"l_hnT{i}", [128, 8, 512], BF16) for i in range(2)]
        xle = sbt(sL, "xle", [128, 4, 515], F32)
        xles = sbt(sL, "xles", [128, 4, 16, 11], F32)
        hst = sbt(sL, "hst", [128, 4], F32)
        hs0 = sbt(sL, "hs0", [128, 4, 16], F32)
        hpo = sbt(sL, "hpo", [128, 4], F32)
        hsl = sbt(sL, "hsl", [128, 4, 16], F32)
        NTMP = 4
        tmp = {}
        for nm in ("acc", "er", "ei", "a", "a2", "bb", "hT"):
            tmp[nm] = [sbt(sL, f"l_{nm}{i}", [128, 512], F32) for i in range(NTMP)]
        xcb = [sbt(sL, f"l_xcb{i}", [128, 512], BF16) for i in range(NTMP)]
        gsm = {nm: [sbt(sL, f"l_g{nm}{i}", [128, 128], F32) for i in range(NTMP)] for nm in ("g", "u", "e")}
        lrub = [sbt(sL, f"lrub{i}", [128, 4, 128], BF16) for i in range(2)]
        xltm = sbt(sL, "xltm", [128, 512], F32)
        stsb = sbt(sL, "stsb", [48, 512], F32)
        sthb = sbt(sL, "sthb", [16, 512], F32)
        hso = sbt(sL, "hso", [16, 512], F32)

        T.op('dve', lambda e: e.memset(xle[:], 0.0), W=[f"xle{t}" for t in range(4)])
        T.op('dve', lambda e: e.memset(hst[:], 0.0), W=["hst"])
        tcount = [0]

        lt_res = {}

        def lru_tile(t, ntok, sample, xview, xname, own_lo, own_rhs, own_rname, out_tile, out_name, first_sb):
            sl = tcount[0] % NTMP
            tcount[0] += 1
            tn = lambda nm: f"lt_{nm}{sl}"
            acc, er, ei, a, a2, bb, hT = (tmp[k][sl] for k in ("acc", "er", "ei", "a", "a2", "bb", "hT"))

            def v(tile_):
                if sample:
                    return tile_[:, 0:ntok].rearrange("p (s q) -> p s q", s=16)
                return tile_[:, 0:ntok]
            cw = lambda j: params[:, t * 4 + j:t * 4 + j + 1]
            T.op('dve', lambda e: e.tensor_scalar(out=v(acc), in0=xview(0), scalar1=cw(0), scalar2=params[:, 16 + t:17 + t],
                                                  op0=ALU.mult, op1=ALU.add), R=[xname, "params"], W=[tn("acc")])
            for j in (1, 2, 3):
                T.op('dve', lambda e, j=j: e.scalar_tensor_tensor(out=v(acc), in0=xview(j), scalar=cw(j), in1=v(acc),
                                                                  op0=ALU.mult, op1=ALU.add),
                     R=[xname, "params", tn("acc")], W=[tn("acc")])
            T.op('pool', lambda e: e.tensor_copy(out=xcb[sl][:, 0:ntok], in_=acc[:, 0:ntok]), R=[tn("acc")], W=[tn("xcb")])
            yield
            pa, pan = PS.half()
            px, pxn = PS.half()
            T.op('pe', lambda e: e.matmul(pa[:, 0:ntok], lhsT=bd[:, t, :], rhs=xcb[sl][:, 0:ntok], start=True, stop=True),
                 R=["bd", tn("xcb")], W=[pan])
            T.op('pe', lambda e: e.matmul(px[:, 0:ntok], lhsT=bd[:, 4 + t, :], rhs=xcb[sl][:, 0:ntok], start=True, stop=True),
                 R=["bd", tn("xcb")], W=[pxn])
            yield
            T.op('act', lambda e: e.activation(out=er[:, 0:ntok], in_=pa[:, 0:ntok], func=AF.Exp, scale=-1.0,
                                               bias=lruc[:, 8 + t:9 + t]), R=[pan, "lruc"], W=[tn("er")])
            T.op('act', lambda e: e.activation(out=ei[:, 0:ntok], in_=px[:, 0:ntok], func=AF.Exp, scale=-1.0,
                                               bias=lruc[:, 12 + t:13 + t]), R=[pxn, "lruc"], W=[tn("ei")])
            yield
            for nm, tl in (("er", er), ("ei", ei)):
                T.op('act', lambda e, tl=tl: e.activation(out=tl[:, 0:ntok], in_=tl[:, 0:ntok], func=AF.Ln, bias=1.0, scale=1.0),
                     R=[tn(nm)], W=[tn(nm)])
                T.op('act', lambda e, tl=tl: e.activation(out=tl[:, 0:ntok], in_=tl[:, 0:ntok], func=AF.Exp, scale=-1.0),
                     R=[tn(nm)], W=[tn(nm)])
            T.op('act', lambda e: e.activation(out=a[:, 0:ntok], in_=er[:, 0:ntok], func=AF.Exp, scale=lruc[:, t:t + 1]),
                 R=[tn("er"), "lruc"], W=[tn("a")])
            T.op('act', lambda e: e.activation(out=a2[:, 0:ntok], in_=er[:, 0:ntok], func=AF.Exp, scale=lruc[:, 4 + t:5 + t]),
                 R=[tn("er"), "lruc"], W=[tn("a2")])
            yield
            T.op('act', lambda e: e.activation(out=a2[:, 0:ntok], in_=a2[:, 0:ntok], func=AF.Ln, scale=-1.0, bias=1.0),
                 R=[tn("a2")], W=[tn("a2")])
            T.op('act', lambda e: e.activation(out=a2[:, 0:ntok], in_=a2[:, 0:ntok], func=AF.Exp, scale=0.5),
                 R=[tn("a2")], W=[tn("a2")])
            yield
            T.op('pool', lambda e: e.tensor_tensor(out=bb[:, 0:ntok], in0=ei[:, 0:ntok], in1=acc[:, 0:ntok], op=ALU.mult),
                 R=[tn("ei"), tn("acc")], W=[tn("bb")])
            T.op('dve', lambda e: e.tensor_tensor(out=bb[:, 0:ntok], in0=bb[:, 0:ntok], in1=a2[:, 0:ntok], op=ALU.mult),
                 R=[tn("bb"), tn("a2")], W=[tn("bb")])
            yield
            if first_sb:
                for blk in range(3):
                    T.op('dve', lambda e, blk=blk: e.tensor_scalar(out=bb[:, blk * 128:(blk + 1) * 128],
                                                                    in0=bb[:, blk * 128:(blk + 1) * 128],
                                                                    scalar1=vflag[:, blk:blk + 1], scalar2=None, op0=ALU.mult),
                         R=[tn("bb"), "vflag"], W=[tn("bb")])
            if not sample:
                T.op('dve', lambda e: e.tensor_tensor_scan(out=hT[:, 0:ntok], data0=a[:, 0:ntok], data1=bb[:, 0:ntok],
                                                           initial=hst[:, t:t + 1], op0=ALU.mult, op1=ALU.add),
                     R=[tn("a"), tn("bb"), "hst"], W=[tn("hT")])
                T.op('dve', lambda e: e.tensor_copy(out=hst[:, t:t + 1], in_=hT[:, ntok - 1:ntok]), R=[tn("hT")], W=["hst"])
            else:
                for s in range(16):
                    T.op('dve', lambda e, s=s: e.tensor_tensor_scan(out=hT[:, s * 8:(s + 1) * 8], data0=a[:, s * 8:(s + 1) * 8],
                                                                   data1=bb[:, s * 8:(s + 1) * 8], initial=hs0[:, t, s:s + 1],
                                                                   op0=ALU.mult, op1=ALU.add),
                         R=[tn("a"), tn("bb"), "hs0"], W=[tn("hT")])
                T.op('dve', lambda e: e.tensor_copy(out=hsl[:, t, :],
                                                    in_=hT[:, 0:128].rearrange("p (s q) -> p s q", s=16)[:, :, 7]),
                     R=[tn("hT")], W=["hsl"])
            g, u, ee = (gsm[k][sl] for k in ("g", "u", "e"))
            pg, pgn = PS.half()
            T.op('pe', lambda e: mm_acc(e, pg[:, 0:128], [(wlg[:, dc, 512 + t * 128:512 + (t + 1) * 128], own_rhs(dc))
                                                            for dc in range(8)]), R=["wlg", own_rname], W=[pgn])
            T.op('dve', lambda e: e.tensor_copy(out=g[:], in_=pg[:, 0:128]), R=[pgn], W=[tn("g")])
            yield
            T.op('pool', lambda e: e.tensor_tensor(out=u[:], in0=g[:], in1=g[:], op=ALU.mult), R=[tn("g")], W=[tn("u")])
            T.op('dve', lambda e: e.tensor_scalar(out=u[:], in0=u[:], scalar1=0.044715, scalar2=1.0, op0=ALU.mult, op1=ALU.add),
                 R=[tn("u")], W=[tn("u")])
            T.op('pool', lambda e: e.tensor_tensor(out=u[:], in0=u[:], in1=g[:], op=ALU.mult), R=[tn("u"), tn("g")], W=[tn("u")])
            T.op('act', lambda e: e.activation(out=ee[:], in_=u[:], func=AF.Exp, scale=-1.5957691216057308),
                 R=[tn("u")], W=[tn("e")])
            yield
            T.op('act', lambda e: e.activation(out=ee[:], in_=ee[:], func=AF.Ln, bias=1.0, scale=1.0), R=[tn("e")], W=[tn("e")])
            T.op('act', lambda e: e.activation(out=ee[:], in_=ee[:], func=AF.Exp, scale=-1.0), R=[tn("e")], W=[tn("e")])
            T.op('dve', lambda e: e.tensor_tensor(out=g[:], in0=g[:], in1=ee[:], op=ALU.mult), R=[tn("g"), tn("e")], W=[tn("g")])
            T.op('dve', lambda e: e.tensor_tensor(out=out_tile[:, t, :], in0=g[:], in1=hT[:, own_lo:own_lo + 128], op=ALU.mult),
                 R=[tn("g"), tn("hT")], W=[out_name])
            lt_res[t] = (hT, tn("hT"))
            yield

        for sb in range(NSB):
            hb = hnT[sb % 2]
            hname = f"l_hnT{sb % 2}"
            for blk in range(4):
                L = sb * 4 + blk
                prep.run(xloc[L * 128:(L + 1) * 128, :], hb[:, :, blk * 128:(blk + 1) * 128], [hname])
            lb = lrub[sb % 2]
            lname = f"lrub{sb % 2}"
            gens = []
            for t in range(4):
                pxl, pxn = PS.half()
                T.op('pe', lambda e: mm_acc(e, pxl[:, :], [(wlg[:, dc, t * 128:(t + 1) * 128], hb[:, dc, :]) for dc in range(8)]),
                     R=["wlg", hname], W=[pxn])
                T.op('act', lambda e: e.copy(out=xle[:, t, 3:515], in_=pxl[:, :]), R=[pxn], W=[f"xle{t}"])
                gens.append(lru_tile(t, 512, False, (lambda t: (lambda j: xle[:, t, j:j + 512]))(t), f"xle{t}", 384,
                                     lambda dc: hb[:, dc, 384:512], hname, lb, lname, sb == 0))
            live = list(gens)
            while live:
                for g_ in list(live):
                    try:
                        next(g_)
                    except StopIteration:
                        live.remove(g_)
            for t in range(4):
                hT, hTn = lt_res[t]
                T.op('pool', lambda e: e.tensor_copy(out=xle[:, t, 0:3], in_=xle[:, t, 512:515]), R=[f"xle{t}"], W=[f"xle{t}"])
                if sb == NSB - 1:
                    T.op('dve', lambda e: e.tensor_copy(out=hpo[:, t:t + 1], in_=hT[:, 384 + 15:384 + 16]), R=[hTn], W=["hpo"])
            T.dma('pool', mixs[sb, :, 4:8, :], lb[:], R=[lname], W=[f"mixs{sb}"])
            if sb == NSB - 1:
                T.dma('pool', hp_o[:, :], hpo[:], R=["hpo"])
                pc, pcn = PS.half()
                T.op('pe', lambda e: mm_acc(e, pc[:, :], [(hb[:, dc, 384:512], wlg[:, dc, 0:512]) for dc in range(8)]),
                     R=["wlg", hname], W=[pcn])
                T.op('dve', lambda e: e.tensor_copy(out=xltm[:], in_=pc[:, :]), R=[pcn], W=["xltm"])
                T.dma('pool', cp_o[:, :], xltm[13:16, :], R=["xltm"])

        hbs = hnT[NSB % 2]
        hsname = f"l_hnT{NSB % 2}"
        prep.run(xs[:, :], hbs[:, :, 0:128], [hsname])
        T.dma('sp', stsb[:], stc[:, :], W=["stsb"])
        T.dma('sp', sthb[:], sth[:, :], W=["sthb"])
        pst, pstn = PS.half()

        def trs(e):
            for t in range(4):
                ins = e.transpose(pst[:, t * 48:(t + 1) * 48], stsb[0:48, t * 128:(t + 1) * 128], identf[0:48, 0:48])
            return ins
        T.op('pe', trs, R=["stsb", "identf"], W=[pstn])
        T.op('dve', lambda e: e.tensor_copy(out=xles[:, :, :, 0:3],
                                            in_=pst[:, 0:192].rearrange("p (t s j) -> p t s j", t=4, s=16)), R=[pstn], W=["xles"])
        psh, pshn = PS.half()

        def trh(e):
            for t in range(4):
                ins = e.transpose(psh[:, t * 16:(t + 1) * 16], sthb[0:16, t * 128:(t + 1) * 128], identf[0:16, 0:16])
            return ins
        T.op('pe', trh, R=["sthb", "identf"], W=[pshn])
        T.op('dve', lambda e: e.tensor_copy(out=hs0[:], in_=psh[:, 0:64].rearrange("p (t s) -> p t s", t=4)), R=[pshn], W=["hs0"])
        lb = lrub[NSB % 2]
        lname = f"lrub{NSB % 2}"
        for t in range(4):
            pxl, pxn = PS.half()
            T.op('pe', lambda e: mm_acc(e, pxl[:, 0:128], [(wlg[:, dc, t * 128:(t + 1) * 128], hbs[:, dc, 0:128])
                                                             for dc in range(8)]), R=["wlg", hsname], W=[pxn])
            T.op('act', lambda e: e.copy(out=xles[:, t, :, 3:11], in_=pxl[:, 0:128].rearrange("p (s q) -> p s q", s=16)),
                 R=[pxn], W=["xles"])
            for _ in lru_tile(t, 128, True, lambda j: xles[:, t, :, j:j + 8], "xles", 0,
                              lambda dc: hbs[:, dc, 0:128], hsname, lb, lname, False):
                pass
        T.dma('pool', mixs[SIDX, :, 4:8, :], lb[:], R=[lname], W=[f"mixs{SIDX}"])
        pho, phon = PS.half()

        def trho(e):
            for t in range(4):
                ins = e.transpose(pho[0:16, t * 128:(t + 1) * 128], hsl[:, t, :], identf[:, :])
            return ins
        T.op('pe', trho, R=["hsl", "identf"], W=[phon])
        T.op('dve', lambda e: e.tensor_copy(out=hso[:], in_=pho[0:16, :]), R=[phon], W=["hso"])
        T.dma('pool', hs_o[:, :], hso[:], R=["hso"])
        pc, pcn = PS.half()
        T.op('pe', lambda e: mm_acc(e, pc[:, :], [(hbs[:, dc, 0:128], wlg[:, dc, 0:512]) for dc in range(8)]),
             R=["wlg", hsname, "xltm"], W=[pcn])
        T.op('dve', lambda e: e.tensor_copy(out=xltm[:], in_=pc[:, :]), R=[pcn], W=["xltm"])
        for s in range(16):
            T.dma('pool', cs_o[s, :, :], xltm[s * 8 + 5:s * 8 + 8, :], R=["xltm"])
        T.barrier()

    ZN = [["ps0", "ps1"], ["ps2", "ps3"]]
    XN = ["ps4", "ps5"]
    ON = "ps6"
    Xps = psf[2]
    Ops = psf[3][:, 0:512]
    sp7 = psf[3][:, 512:1024]

    def attn_pipeline(n, zgen, maskgen, wvgen, Et, St, Gt, Wt, tag, after_b2=None):
        T.op('pe', lambda e: e.matmul(Ops, lhsT=zl[0:1, :], rhs=zrow[0:1, :], start=True, stop=True), R=["zl", "zrow"], W=[ON])

        def Z(r):
            zgen(r, psf[r % 2], ZN[r % 2])

        def ES(r):
            zt = psf[r % 2]
            zn = ZN[r % 2]
            E = Et[r % 3]
            S = St[r % 3]
            T.op('act', lambda e: e.activation(out=E[:], in_=zt[:, :], func=AF.Exp), R=zn, W=[f"{tag}E{r % 3}"])
            maskgen(r, E, f"{tag}E{r % 3}")
            T.op('act', lambda e: e.activation(out=S[:], in_=E[:], func=AF.Ln, bias=1.0, scale=1.0),
                 R=[f"{tag}E{r % 3}"], W=[f"{tag}S{r % 3}"])

        def TRI(r):
            S = St[r % 3]

            def f(e):
                for hp in range(2):
                    ins = e.matmul(Xps[:, hp * 512:(hp + 1) * 512], lhsT=tri[:, :], rhs=S[:, hp * 512:(hp + 1) * 512],
                                   start=(r == 0), stop=True, skip_group_check=(r > 0))
                return ins
            T.op('pe', f, R=["tri", f"{tag}S{r % 3}"], W=XN)

        def OT(r):
            S = St[r % 3]

            def f(e):
                for hp in range(2):
                    ins = e.matmul(Xps[:, hp * 512:(hp + 1) * 512], lhsT=otri[:, :], rhs=S[:, hp * 512:(hp + 1) * 512],
                                   start=False, stop=True, skip_group_check=True)
                return ins
            T.op('pe', f, R=["otri", f"{tag}S{r % 3}"], W=XN)

        Z(0)
        ES(0)
        if n > 1:
            Z(1)
            ES(1)
        if n > 2:
            Z(2)
        TRI(0)
        for r in range(n):
            E = Et[r % 3]
            G = Gt[r % 2]
            W = Wt[r % 2]
            T.op('act', lambda e: e.activation(out=G[:], in_=Xps[:, :], func=AF.Exp, scale=-1.0), R=XN, W=[f"{tag}G{r % 2}"])
            if r + 1 < n:
                OT(r)
                TRI(r + 1)
            if r + 2 < n:
                ES(r + 2)
            if r + 3 < n:
                Z(r + 3)
            T.op('dve', lambda e: e.tensor_tensor(out=W[:], in0=E[:], in1=G[:], op=ALU.mult),
                 R=[f"{tag}E{r % 3}", f"{tag}G{r % 2}"], W=[f"{tag}W{r % 2}"])
            wvgen(r, W, f"{tag}W{r % 2}")
            if after_b2 is not None:
                after_b2(r)

    if "S" in phases:
      with contextlib.ExitStack() as sS:
        wqkv = load_w(sS, "wqkv", w_in, 0, 1536, 32)
        prep = Prep(sS, "pq")
        hnTs = sbt(sS, "hnTs", [128, 8, 128], BF16)
        qTs = sbt(sS, "qTs", [128, 4, 128], F32)
        kTn = sbt(sS, "kTn", [128, 4, 128], F32)
        ktm = sbt(sS, "ktm", [128, 512], F32)
        vtm = sbt(sS, "vtm", [128, 512], F32)
        smask = sbt(sS, "smaskt", [128, 1024], F32)
        NI = 16 * NPG
        ptb = sbt(sS, "ptb", [128, NI], I32)
        ptf = sbt(sS, "ptf", [128, NI], F32)
        iot = sbt(sS, "iot", [128, 1], I32)
        iotf = sbt(sS, "iotf", [128, 1], F32)
        idxall = sbt(sS, "idxall", [128, NI], I32)
        kpg = sbt(sS, "kpg", [128, 16, 512], F32)
        vpg = sbt(sS, "vpg", [128, 16, 512], F32)
        ktmp = [sbt(sS, f"ktmp{i}", [128, 4, 128], BF16) for i in range(2)]
        qTb = sbt(sS, "qTb", [128, 4, 128], BF16)
        vtmb = sbt(sS, "vtmb", [128, 512], BF16)
        vpgb = sbt(sS, "vpgb", [128, 16, 512], BF16)
        Et = [sbt(sS, f"sE{i}", [128, 1024], F32) for i in range(3)]
        St = [sbt(sS, f"sS{i}", [128, 1024], BF16) for i in range(3)]
        Gt = [sbt(sS, f"sG{i}", [128, 1024], F32) for i in range(2)]
        Wt = [sbt(sS, f"sW{i}", [128, 1024], BF16) for i in range(2)]
        atto = sbt(sS, "satto", [128, 4, 128], BF16)

        T.dma('sp', smask[:], smask_d[:, :], W=["smask"])
        T.dma('sp', ptb[:], pt.partition_broadcast(128), W=["ptb"])
        T.op('pool', lambda e: e.iota(iot[:], pattern=[[0, 1]], base=0, channel_multiplier=1), W=["iot"])
        T.op('dve', lambda e: e.tensor_copy(out=iotf[:], in_=iot[:]), R=["iot"], W=["iotf"])
        T.op('dve', lambda e: e.tensor_copy(out=ptf[:], in_=ptb[:]), R=["ptb"], W=["ptf"])
        T.op('dve', lambda e: e.tensor_scalar(out=ptf[:], in0=ptf[:], scalar1=128.0, scalar2=iotf[:, 0:1], op0=ALU.mult,
                                              op1=ALU.add), R=["ptf", "iotf"], W=["ptf"])
        T.op('dve', lambda e: e.tensor_copy(out=idxall[:], in_=ptf[:]), R=["ptf"], W=["idxall"])

        prep.run(xs[:, :], hnTs[:, :, :], ["hnTs"])
        pq, pqn = PS.half()
        pk, pkn = PS.half()

        def projT(ps_, c0):
            def f(e):
                for t in range(4):
                    ins = mm_acc(e, ps_[:, t * 128:(t + 1) * 128],
                                 [(wqkv[:, dc, c0 + t * 128:c0 + (t + 1) * 128], hnTs[:, dc, :]) for dc in range(8)])
                return ins
            return f
        T.op('pe', projT(pq, 0), R=["wqkv", "hnTs"], W=[pqn])
        T.op('act', lambda e: e.activation(out=qTs[:].rearrange("p t n -> p (t n)"), in_=pq[:, :], func=AF.Copy, scale=0.125), R=[pqn], W=["qTs"])
        T.op('pe', projT(pk, 512), R=["wqkv", "hnTs"], W=[pkn])
        T.op('dve', lambda e: e.tensor_copy(out=kTn[:].rearrange("p t n -> p (t n)"), in_=pk[:, :]), R=[pkn], W=["kTn"])
        pk2, pk2n = PS.half()
        T.op('pe', lambda e: mm_acc(e, pk2[:, :], [(hnTs[:, dc, :], wqkv[:, dc, 512:1024]) for dc in range(8)]),
             R=["wqkv", "hnTs"], W=[pk2n])
        T.op('dve', lambda e: e.tensor_copy(out=ktm[:], in_=pk2[:, :]), R=[pk2n], W=["ktm"])
        T.dma('sp', k_o[SIDX], ktm[:], R=["ktm"])
        pv2, pv2n = PS.half()
        T.op('pe', lambda e: mm_acc(e, pv2[:, :], [(hnTs[:, dc, :], wqkv[:, dc, 1024:1536]) for dc in range(8)]),
             R=["wqkv", "hnTs"], W=[pv2n])
        T.op('dve', lambda e: e.tensor_copy(out=vtm[:], in_=pv2[:, :]), R=[pv2n], W=["vtm"])
        T.dma('sp', v_o[SIDX], vtm[:], R=["vtm"])
        T.op('dve', lambda e: e.tensor_copy(out=vtmb[:], in_=vtm[:]), R=["vtm"], W=["vtmb"])
        T.op('dve', lambda e: e.tensor_copy(out=qTb[:], in_=qTs[:]), R=["qTs"], W=["qTb"])
        T.barrier()

        def gat_k(r):
            i = NPG - r
            for s in range(16):
                T.gather(kpg[:, s, :], ck[:, :], idxall[:, s * NPG + i:s * NPG + i + 1], R=["idxall"], W=[f"kpg{s}"])

        def gat_v(r):
            i = NPG - r
            for s in range(16):
                T.gather(vpg[:, s, :], cv[:, :], idxall[:, s * NPG + i:s * NPG + i + 1], R=["idxall"], W=[f"vpg{s}"])
            for s in range(16):
                T.op('dve', lambda e, s=s: e.tensor_copy(out=vpgb[:, s, :], in_=vpg[:, s, :]), R=[f"vpg{s}"], W=[f"vpgb{s}"])

        def zgen(r, zt, zn):
            if r >= 1:
                gat_k(r)

            def fb(e):
                for hp in range(2):
                    ins = e.matmul(zt[:, hp * 512:(hp + 1) * 512], lhsT=ones2[:, :], rhs=brow_p[:, hp * 512:(hp + 1) * 512],
                                   start=True, stop=True)
                return ins
            T.op('pe', fb, R=["ones2", brow_p.name], W=zn)
            if r == 0:
                def f(e):
                    for h in range(8):
                        t, hp = h // 2, h % 2
                        out = zt[:, hp * 512 + t * 128:hp * 512 + (t + 1) * 128]
                        ins = e.matmul(out, lhsT=kTn[hp * 64:(hp + 1) * 64, t, :],
                                       rhs=qTs[hp * 64:(hp + 1) * 64, t, :],
                                       start=False, stop=True, skip_group_check=True)
                    return ins
                T.op('pe', f, R=["kTn", "qTs"], W=zn)
            else:
                for s in range(16):
                    kt = ktmp[s % 2]
                    ktn = f"ktmp{s % 2}"

                    def ftr(e):
                        for t in range(4):
                            ins = e.transpose(sp7[:, t * 128:(t + 1) * 128], kpg[:, s, t * 128:(t + 1) * 128], identf[:, :])
                        return ins
                    T.op('pe', ftr, R=[f"kpg{s}", "identf"], W=["ps7"])
                    T.op('dve', lambda e: e.tensor_copy(out=kt[:].rearrange("p t n -> p (t n)"), in_=sp7[:, :]),
                         R=["ps7"], W=[ktn])

                    def fq(e):
                        for h in range(8):
                            t, hp = h // 2, h % 2
                            c = hp * 512 + t * 128 + s * 8
                            ins = e.matmul(zt[:, c:c + 8], lhsT=kt[hp * 64:(hp + 1) * 64, t, :],
                                           rhs=qTb[hp * 64:(hp + 1) * 64, t, s * 8:(s + 1) * 8],
                                           start=False, stop=True, skip_group_check=True)
                        return ins
                    T.op('pe', fq, R=[ktn, "qTb"], W=zn)

        def maskgen(r, E, en):
            if r == 0:
                T.op('dve', lambda e: e.tensor_tensor(out=E[:], in0=E[:], in1=smask[:], op=ALU.mult), R=[en, "smask"], W=[en])

        def wvgen(r, W, wn):
            if r == 0:
                def f(e):
                    for h in range(8):
                        t, hp = h // 2, h % 2
                        out = Ops[hp * 64:(hp + 1) * 64, t * 128:(t + 1) * 128]
                        rhs = W[:, hp * 512 + t * 128:hp * 512 + (t + 1) * 128]
                        ins = e.matmul(out, lhsT=vtmb[:, h * 64:(h + 1) * 64], rhs=rhs, start=False, stop=True,
                                       skip_group_check=True)
                    return ins
                T.op('pe', f, R=["vtmb", wn], W=[ON])
            else:
                for s in range(16):
                    def f(e):
                        for h in range(8):
                            t, hp = h // 2, h % 2
                            c = hp * 512 + t * 128 + s * 8
                            ins = e.matmul(Ops[hp * 64:(hp + 1) * 64, t * 128 + s * 8:t * 128 + (s + 1) * 8],
                                           lhsT=vpgb[:, s, h * 64:(h + 1) * 64], rhs=W[:, c:c + 8], start=False, stop=True,
                                           skip_group_check=True)
                        return ins
                    T.op('pe', f, R=[f"vpgb{s}", wn], W=[ON])

        def after_b2(r):
            if r + 1 <= NPG:
                gat_v(r + 1)

        SS = DBG.get('sstop', 9)
        if SS >= 1:
            attn_pipeline(NPG + 1 if SS >= 2 else 1, zgen, maskgen, wvgen, Et, St, Gt, Wt, "s", after_b2 if SS >= 2 else None)
        T.op('dve', lambda e: e.tensor_copy(out=atto[:].rearrange("p t n -> p (t n)"), in_=Ops), R=[ON], W=["satto"])
        T.dma('pool', mixs[SIDX, :, 0:4, :], atto[:], R=["satto"], W=[f"mixa{SIDX}"])
        T.barrier()

    if "K" in phases:
      with contextlib.ExitStack() as sK:
        kT = sbt(sK, "kT", [128, 4, NT], BF16)
        Vr = sbt(sK, "Vr", [128, NB, 512], BF16)
        qT = sbt(sK, "qT", [128, NSB, 4, 128], BF16)
        with contextlib.ExitStack() as sK2:
            wqkv = load_w(sK2, "wqkv2", w_in, 0, 1536, 32)
            prep = Prep(sK2, "pk", nx=1)
            hnT = [sbt(sK2, f"k_hnT{i}", [128, 8, 512], BF16) for i in range(1)]
            kst = [sbt(sK2, f"kst{i}", [128, 512], F32) for i in range(1)]
            vst = [sbt(sK2, f"vst{i}", [128, 512], F32) for i in range(1)]
            KS = DBG.get('kstop', 9)
            for sb in range(NSB):
                if KS < 1:
                    break
                hb = hnT[0]
                hname = "k_hnT0"
                for blk in range(4):
                    L = sb * 4 + blk
                    prep.run(xloc[L * 128:(L + 1) * 128, :], hb[:, :, blk * 128:(blk + 1) * 128], [hname])
                for t in range(4):
                    if KS < 2:
                        break
                    pk, pkn = PS.half()
                    T.op('pe', lambda e: mm_acc(e, pk[:, :], [(wqkv[:, dc, 512 + t * 128:512 + (t + 1) * 128], hb[:, dc, :])
                                                               for dc in range(8)]), R=["wqkv2", hname], W=[pkn])
                    eng = 'act' if t % 2 else 'dve'
                    if eng == 'act':
                        T.op('act', lambda e: e.copy(out=kT[:, t, sb * 512:(sb + 1) * 512], in_=pk[:, :]), R=[pkn], W=[f"kT{sb}"])
                    else:
                        T.op('dve', lambda e: e.tensor_copy(out=kT[:, t, sb * 512:(sb + 1) * 512], in_=pk[:, :]),
                             R=[pkn], W=[f"kT{sb}"])
                for blk in range(4):
                    if KS < 3:
                        break
                    L = sb * 4 + blk
                    pv, pvn = PS.half()
                    T.op('pe', lambda e: mm_acc(e, pv[:, :], [(hb[:, dc, blk * 128:(blk + 1) * 128], wqkv[:, dc, 1024:1536])
                                                               for dc in range(8)]), R=["wqkv2", hname], W=[pvn])
                    if DBG.get('vevac', 1):
                        if blk == 1 and DBG.get('vact', 1):
                            T.op('act', lambda e: e.copy(out=Vr[:, L, :], in_=pv[:, :]), R=[pvn], W=[f"Vr{L}"])
                        else:
                            T.op('dve', lambda e: e.tensor_copy(out=Vr[:, L, :], in_=pv[:, :]), R=[pvn], W=[f"Vr{L}"])
                    if blk == 3 and DBG.get('vout', 1):
                        vs_ = vst[0]
                        T.op('dve', lambda e: e.tensor_copy(out=vs_[:], in_=pv[:, :]), R=[pvn], W=["vst0"])
                        T.dma('pool', v_o[sb], vs_[:], R=["vst0"])
                if KS < 4:
                    continue
                pk2, pk2n = PS.half()
                T.op('pe', lambda e: mm_acc(e, pk2[:, :], [(hb[:, dc, 384:512], wqkv[:, dc, 512:1024]) for dc in range(8)]),
                     R=["wqkv2", hname], W=[pk2n])
                ks_ = kst[0]
                T.op('act', lambda e: e.copy(out=ks_[:], in_=pk2[:, :]), R=[pk2n], W=["kst0"])
                T.dma('pool', k_o[sb], ks_[:], R=["kst0"])
                if KS < 5:
                    continue
                pq, pqn = PS.half()

                def fq(e):
                    for t in range(4):
                        ins = mm_acc(e, pq[:, t * 128:(t + 1) * 128],
                                     [(wqkv[:, dc, t * 128:(t + 1) * 128], hb[:, dc, 384:512]) for dc in range(8)])
                    return ins
                T.op('pe', fq, R=["wqkv2", hname], W=[pqn])
                T.op('act', lambda e: e.activation(out=qT[:, sb, :, :].rearrange("p t n -> p (t n)"), in_=pq[:, :], func=AF.Copy, scale=0.125), R=[pqn], W=["qT"])
            T.barrier()

        if "A" in phases:
          with contextlib.ExitStack() as sA:
            Et = [sbt(sA, f"aE{i}", [128, 1024], BF16) for i in range(3)]
            St = [sbt(sA, f"aS{i}", [128, 1024], BF16) for i in range(3)]
            Gt = [sbt(sA, f"aG{i}", [128, 1024], BF16) for i in range(2)]
            Wt = [sbt(sA, f"aW{i}", [128, 1024], BF16) for i in range(2)]
            atto = [sbt(sA, f"aatto{i}", [128, 4, 128], BF16) for i in range(2)]
            for m in range(NSB):
                L = 4 * m + 3

                def zgen(r, zt, zn):
                    Lk = L - r

                    def f(e):
                        for hp in range(2):
                            e.matmul(zt[:, hp * 512:(hp + 1) * 512], lhsT=ones2[:, :], rhs=brow_p[:, hp * 512:(hp + 1) * 512],
                                     start=True, stop=True)
                        for h in range(8):
                            t, hp = h // 2, h % 2
                            ins = e.matmul(zt[:, hp * 512 + t * 128:hp * 512 + (t + 1) * 128],
                                           lhsT=kT[hp * 64:(hp + 1) * 64, t, Lk * 128:(Lk + 1) * 128],
                                           rhs=qT[hp * 64:(hp + 1) * 64, m, t, :], start=False, stop=True, skip_group_check=True)
                        return ins
                    T.op('pe', f, R=["ones2", brow_p.name, "qT", f"kT{Lk // 4}"], W=zn)

                def maskgen(r, E, en):
                    Lk = L - r
                    if r == 0:
                        T.op('dve', lambda e: e.tensor_tensor(out=E[:].rearrange("p (h q) -> p h q", h=8),
                                                               in0=E[:].rearrange("p (h q) -> p h q", h=8),
                                                               in1=otri[:, None, :].to_broadcast([128, 8, 128]), op=ALU.mult),
                             R=[en, "otri"], W=[en])
                    if Lk < 3:
                        T.op('dve', lambda e: e.tensor_scalar(out=E[:], in0=E[:], scalar1=vflag[:, Lk:Lk + 1], scalar2=None,
                                                               op0=ALU.mult), R=[en, "vflag"], W=[en])

                def wvgen(r, W, wn):
                    Lk = L - r

                    def f(e):
                        for h in range(8):
                            t, hp = h // 2, h % 2
                            ins = e.matmul(Ops[hp * 64:(hp + 1) * 64, t * 128:(t + 1) * 128],
                                           lhsT=Vr[:, Lk, h * 64:(h + 1) * 64],
                                           rhs=W[:, hp * 512 + t * 128:hp * 512 + (t + 1) * 128], start=False, stop=True,
                                           skip_group_check=True)
                        return ins
                    T.op('pe', f, R=[f"Vr{Lk}", wn], W=[ON])

                attn_pipeline(L + 1, zgen, maskgen, wvgen, Et, St, Gt, Wt, "a")
                ao = atto[m % 2]
                T.op('dve', lambda e: e.tensor_copy(out=ao[:].rearrange("p t n -> p (t n)"), in_=Ops), R=[ON], W=[f"aatto{m % 2}"])
                T.dma('pool', mixs[m, :, 0:4, :], ao[:], R=[f"aatto{m % 2}"], W=[f"mixa{m}"])
            T.barrier()

    if "M" in phases:
      with contextlib.ExitStack() as sM:
        with contextlib.ExitStack() as sM1:
            gpb = sbt(sM1, "gpb", [128, D], F32)
            T.dma('sp', gpb[:, :], gpost_d[0:1, :].partition_broadcast(128), W=["gpb"])
            wo = load_w(sM1, "wo", w_out, 0, D, None)
            prep = Prep(sM1, "pm")
            mixT = [sbt(sM1, f"mixT{i}", [128, 8, 128], BF16) for i in range(2)]
            xr = [sbt(sM1, f"xr{i}", [128, D], F32) for i in range(2)]
            x1b = [sbt(sM1, f"x1b{i}", [128, D], F32) for i in range(2)]
            for o in range(NOWN):
                sl = o % 2
                T.dma('sp', mixT[sl][:], mixs[o], W=[f"mixT{sl}"])
                rows = xs[:, :] if o == SIDX else xloc[(4 * o + 3) * 128:(4 * o + 4) * 128, :]
                T.dma('sp', xr[sl][:], rows, W=[f"xr{sl}"])
                pm, pmn = PS.full()

                def f(e):
                    for hf in range(2):
                        ins = mm_acc(e, pm[:, hf * 512:(hf + 1) * 512],
                                     [(mixT[sl][:, fc, :], wo[:, fc, hf * 512:(hf + 1) * 512]) for fc in range(8)])
                    return ins
                T.op('pe', f, R=["wo", f"mixT{sl}"], W=pmn)
                stat, sname = prep.rstd(pm[:, :], pmn)
                T.op('dve', lambda e: e.scalar_tensor_tensor(out=x1b[sl][:], in0=pm[:, :], scalar=stat[:, 2:3], in1=gpb[:, :],
                                                             op0=ALU.mult, op1=ALU.mult), R=pmn + [sname, "gpb"], W=[f"x1b{sl}"])
                T.op('pool', lambda e: e.tensor_tensor(out=x1b[sl][:], in0=x1b[sl][:], in1=xr[sl][:], op=ALU.add),
                     R=[f"x1b{sl}", f"xr{sl}"], W=[f"x1b{sl}"])
                T.dma('pool', x1s[o], x1b[sl][:], R=[f"x1b{sl}"], W=[f"x1s{o}"])
            T.barrier()

        with contextlib.ExitStack() as sM2:
            wu = load_w(sM2, "wu", w_up, 0, 4096, 40)
            wd = load_w(sM2, "wd", w_down, 0, D, None)
            gpb = sbt(sM2, "gpb2", [128, D], F32)
            T.dma('sp', gpb[:, :], gpost_d[1:2, :].partition_broadcast(128), W=["gpb"])
            prep = Prep(sM2, "pn", nx=1)
            GB = 4
            x1 = [sbt(sM2, f"x1_{i}", [128, D], F32) for i in range(GB)]
            hn2T = sbt(sM2, "hn2T", [128, 8, GB * 128], BF16)
            u2T = sbt(sM2, "u2T", [128, 32, GB * 128], BF16)
            sqv = prep.junk[:, :].bitcast(F32)
            for g0 in range(0, NOWN, GB):
                nb = min(GB, NOWN - g0)
                ntok = nb * 128
                for i in range(nb):
                    o = g0 + i
                    T.dma('sp', x1[i][:], x1s[o], R=[f"x1s{o}"], W=[f"x1_{i}"])
                    prep.run(None, hn2T[:, :, i * 128:(i + 1) * 128], ["hn2T"], x_sb=x1[i], x_names=[f"x1_{i}"])
                for fc in range(32):
                    pu, pun = PS.half()
                    T.op('pe', lambda e: mm_acc(e, pu[:, 0:ntok], [(wu[:, dc, fc * 128:(fc + 1) * 128], hn2T[:, dc, 0:ntok])
                                                                    for dc in range(8)]), R=["wu", "hn2T"], W=[pun])
                    T.op('act', lambda e: e.activation(out=sqv[:, 0:ntok], in_=pu[:, 0:ntok], func=AF.Square),
                         R=[pun], W=["pn_junk"])
                    T.op('dve', lambda e: e.scalar_tensor_tensor(out=u2T[:, fc, 0:ntok], in0=pu[:, 0:ntok], scalar=0.0,
                                                                 in1=sqv[:, 0:ntok], op0=ALU.is_gt, op1=ALU.mult),
                         R=[pun, "pn_junk"], W=["u2T"])
                for i in range(nb):
                    o = g0 + i
                    pd, pdn = PS.full()

                    def f(e):
                        for hf in range(2):
                            ins = mm_acc(e, pd[:, hf * 512:(hf + 1) * 512],
                                         [(u2T[:, fc, i * 128:(i + 1) * 128], wd[:, fc, hf * 512:(hf + 1) * 512])
                                          for fc in range(32)])
                        return ins
                    T.op('pe', f, R=["wd", "u2T"], W=pdn)
                    stat, sname = prep.rstd(pd[:, :], pdn)
                    T.op('dve', lambda e: e.scalar_tensor_tensor(out=prep.xst[0][:], in0=pd[:, :], scalar=stat[:, 2:3],
                                                                 in1=gpb[:, :], op0=ALU.mult, op1=ALU.mult),
                         R=pdn + [sname, "gpb"], W=["pn_xst0"])
                    T.op('pool', lambda e: e.tensor_tensor(out=x1[i][:], in0=prep.xst[0][:], in1=x1[i][:],
                                                           op=ALU.add), R=["pn_xst0", f"x1_{i}"], W=[f"x1_{i}"])
                    T.dma('pool', y_o[o], x1[i][:], R=[f"x1_{i}"])
            T.barrier()

    T.finish()
    es.close()
    return nc


def _host_inputs(inp, NSB, NPG, NPOOL):
    f32 = np.float32
    xp = np.asarray(inp["x_prompt"], f32)
    xsm = np.asarray(inp["x_sample"], f32)
    meta = np.asarray(inp["meta_tokens"], f32)
    B, SEQ, _ = xp.shape
    TR = SEQ + meta.shape[0]
    NB = 4 * NSB
    assert TR <= (NB - 3) * 128
    ck = np.ascontiguousarray(np.asarray(inp["cache_k"], f32)[0].reshape(NPOOL * 128, 512))
    cv = np.ascontiguousarray(np.asarray(inp["cache_v"], f32)[0].reshape(NPOOL * 128, 512))
    ptab = np.asarray(inp["page_table"], np.int32)
    params = np.zeros((128, NPAR), f32)

    def chan(v):
        return np.asarray(v, f32).reshape(4, 128).T
    cw = np.asarray(inp["conv_w"], f32)[0]
    for j in range(4):
        params[:, j:16:4] = chan(cw[j])
    params[:, 16:20] = chan(inp["conv_b"][0])
    params[:, 20:24] = chan(inp["b_gate_a"][0])
    params[:, 24:28] = chan(inp["b_gate_x"][0])
    params[:, 28:32] = chan(inp["lru_lambda"][0])
    params[:, 32:40] = np.asarray(inp["g_mix_pre"], f32)[0].reshape(8, 128).T
    params[:, 40:48] = np.asarray(inp["g_mlp_pre"], f32)[0].reshape(8, 128).T
    gpost = np.stack([np.asarray(inp["g_mix_post"], f32)[0], np.asarray(inp["g_mlp_post"], f32)[0]], 0)
    k_s = np.arange(128) // 8
    k_t = np.arange(128) % 8
    col = np.arange(1024)
    c_s = (col % 128) // 8
    c_q = col % 8
    smask = ((k_s[:, None] == c_s[None, :]) & (k_t[:, None] < c_q[None, :])).astype(f32)
    shared = dict(ck=ck, cv=cv, params=params, gpost=np.ascontiguousarray(gpost),
                  sbias=np.asarray(inp["sb_bias"], f32).reshape(1, 8), smask=smask,
                  w_in=np.ascontiguousarray(np.asarray(inp["w_in"], f32)[0]),
                  w_out=np.ascontiguousarray(np.asarray(inp["w_out"], f32)[0]),
                  w_up=np.ascontiguousarray(np.asarray(inp["w_up"], f32)[0]),
                  w_down=np.ascontiguousarray(np.asarray(inp["w_down"], f32)[0]),
                  wga=np.ascontiguousarray(np.asarray(inp["w_gate_a"], f32)[0]),
                  wgx=np.ascontiguousarray(np.asarray(inp["w_gate_x"], f32)[0]))
    maps = []
    for c in range(8):
        b, j = c // 4, c % 4
        xloc = np.zeros((NB * 128, D), f32)
        o = (3 - j) * 128
        xloc[o:o + meta.shape[0]] = meta
        xloc[o + meta.shape[0]:o + TR] = xp[b]
        vflag = np.ones((128, 4), f32)
        for L in range(3):
            if L < 3 - j:
                vflag[:, L] = 0.0
        m = dict(shared)
        m.update(xloc=xloc, xs=np.ascontiguousarray(xsm[16 * c:16 * c + 16].reshape(128, D)),
                 pt=np.ascontiguousarray(ptab[16 * c:16 * c + 16].reshape(1, 16 * NPG)),
                 sth=np.ascontiguousarray(np.asarray(inp["state_h"], f32)[0, 16 * c:16 * c + 16]),
                 stc=np.ascontiguousarray(np.asarray(inp["state_conv"], f32)[0, 16 * c:16 * c + 16].reshape(48, 512)),
                 vflag=vflag)
        maps.append(m)
    return maps


def _host_outputs(res, inp, NSB):
    f32 = np.float32
    B, SEQ, _ = inp["x_prompt"].shape
    NM = inp["meta_tokens"].shape[0]
    TR = SEQ + NM
    yp = np.zeros((B, TR, D), f32)
    kp = np.zeros((B, TR, 512), f32)
    vp = np.zeros((B, TR, 512), f32)
    hp = np.zeros((1, B, 512), f32)
    cp = np.zeros((1, B, 3, 512), f32)
    ys = np.zeros((128, 8, D), f32)
    ks = np.zeros((1, 128, 8, 8, 64), f32)
    vs = np.zeros((1, 128, 8, 8, 64), f32)
    hs = np.zeros((1, 128, 512), f32)
    cs = np.zeros((1, 128, 3, 512), f32)
    glast = (TR - 1) // 128
    for c in range(8):
        b, j = c // 4, c % 4
        r = res[c]
        for m in range(NSB):
            g = 4 * m + j
            lo = g * 128
            if lo >= TR:
                continue
            n = min(128, TR - lo)
            yp[b, lo:lo + n] = r["y"][m, :n]
            kp[b, lo:lo + n] = r["ko"][m, :n]
            vp[b, lo:lo + n] = r["vo"][m, :n]
        if j == glast % 4:
            hp[0, b] = r["hp"].T.reshape(512)
            cp[0, b] = r["cp"]
        ys[16 * c:16 * c + 16] = r["y"][NSB].reshape(16, 8, D)
        ks[0, 16 * c:16 * c + 16] = r["ko"][NSB].reshape(16, 8, 8, 64)
        vs[0, 16 * c:16 * c + 16] = r["vo"][NSB].reshape(16, 8, 8, 64)
        hs[0, 16 * c:16 * c + 16] = r["hs"]
        cs[0, 16 * c:16 * c + 16] = r["cs"]
    return (yp[:, NM:], ys, kp.reshape(1, B, TR, 8, 64), vp.reshape(1, B, TR, 8, 64), hp, cp, ks, vs, hs, cs)


_CACHE = {}


def kernel(**inputs):
    SEQ = inputs["x_prompt"].shape[1]
    NM = inputs["meta_tokens"].shape[0]
    nblk = -(-(SEQ + NM) // 128)
    NSB = -(-(nblk + 3) // 4)
    NPG = inputs["page_table"].shape[1]
    NPOOL = inputs["cache_k"].shape[1]
    key = (NSB, NPG, NPOOL)
    if key not in _CACHE:
        _CACHE[key] = build(NSB, NPG, NPOOL)
    nc = _CACHE[key]
    maps = _host_inputs(inputs, NSB, NPG, NPOOL)
    res = run_bass_kernel_spmd(nc, maps, core_ids=list(range(8)))
    return _host_outputs(res.results, inputs, NSB)
```
